# Optimizing a Trainium2 kernel written in Bass

```python
import math
import jax
import jax.numpy as jnp
from jax import lax
import numpy as np

D_MODEL = 1024
BATCH = 2
SEQ = 8192
DEPTH = 2

GRID_W = 64
CTX_LEN = 256
EPS = 1e-6
ROPE_BASE = 10000.0

SSM_HEADS = 16
SSM_HEAD_DIM = 64
SSM_INNER = SSM_HEADS * SSM_HEAD_DIM
SSM_GROUPS = 2
SSM_STATE = 128
SSM_CONV = 5
SSM_CHUNK = 128
SSM_CONV_DIM = SSM_INNER + 2 * SSM_GROUPS * SSM_STATE

SWA_Q_HEADS = 8
SWA_KV_HEADS = 2
SWA_HEAD_DIM = 128
SWA_WINDOW = 128
SWA_BLOCK = 128

MLA_HEADS = 8
MLA_Q_RANK = 384
MLA_KV_RANK = 256
MLA_NOPE = 128
MLA_ROPE = 64
MLA_QK = MLA_NOPE + MLA_ROPE
MLA_V = 128
MLA_Q_BLOCK = 128

N_BRANCH = 3
FFN_HIDDEN = -(-8 * D_MODEL // (3 * 256)) * 256

IN_WIDTHS = (
    SSM_CONV_DIM,
    2 * SSM_HEADS,
    SWA_KV_HEADS * SWA_HEAD_DIM,
    SWA_KV_HEADS * SWA_HEAD_DIM,
    MLA_KV_RANK,
    MLA_ROPE,
    SSM_INNER,
    SWA_Q_HEADS * SWA_HEAD_DIM,
    MLA_Q_RANK,
    N_BRANCH * D_MODEL,
)
N_KV_SPLITS = 6
KV_COLS = sum(IN_WIDTHS[:N_KV_SPLITS])
IN_COLS = sum(IN_WIDTHS)

kernel_name = 'hybrid_ssd_swa_mla_dit_block'

F32 = jnp.float32


def rms_norm(x, g):
    xf = x.astype(F32)
    y = xf * lax.rsqrt(jnp.mean(xf * xf, axis=-1, keepdims=True) + EPS)
    return (y * g.astype(F32)).astype(x.dtype)


def modulate(x, g, shift, scale):
    return rms_norm(x, g) * (1.0 + scale[:, None, :]) + shift[:, None, :]


def split_cols(u, widths):
    offs = np.cumsum(widths)[:-1].tolist()
    return jnp.split(u, offs, axis=-1)


def axial_rope_tables(rows, rot_dim):
    n_freq = rot_dim // 4
    inv = jnp.power(ROPE_BASE, -jnp.arange(n_freq, dtype=F32) / n_freq)
    r, col = jnp.meshgrid(jnp.arange(rows, dtype=F32), jnp.arange(GRID_W, dtype=F32), indexing='ij')
    ang = jnp.stack([r.reshape(-1)[:, None] * inv, col.reshape(-1)[:, None] * inv], axis=1)
    return jnp.cos(ang), jnp.sin(ang)


def apply_axial_rope(x, cos, sin):
    shp = x.shape
    xr = x.astype(F32).reshape(shp[:-1] + (2, 2, shp[-1] // 4))
    x1, x2 = xr[..., 0, :], xr[..., 1, :]
    c = cos[None, :, None]
    s = sin[None, :, None]
    out = jnp.stack([x1 * c - x2 * s, x2 * c + x1 * s], axis=-2)
    return out.reshape(shp).astype(x.dtype)


def centred_dwconv(u, w, b):
    k, ch = w.shape
    out = lax.conv_general_dilated(u, w[:, None, :].astype(u.dtype), window_strides=(1,),
                                   padding=[(k // 2, k // 2)], dimension_numbers=('NWC', 'WIO', 'NWC'),
                                   feature_group_count=ch)
    return out + b


def ssd_scan(xh, dt, a, bm, cm, h0, with_y):
    bsz, L, H, P = xh.shape
    G, N = bm.shape[-2:]
    hpg = H // G
    Q = SSM_CHUNK
    nc = L // Q
    x = xh.astype(F32).reshape(bsz, nc, Q, G, hpg, P)
    dtc = dt.astype(F32).reshape(bsz, nc, Q, G, hpg)
    bc = bm.astype(F32).reshape(bsz, nc, Q, G, N)
    cc = cm.astype(F32).reshape(bsz, nc, Q, G, N)
    acs = jnp.cumsum(dtc * a.astype(F32).reshape(G, hpg), axis=2)
    xdt = x * dtc[..., None]
    decay_end = jnp.exp(acs[:, :, -1:] - acs)
    states = jnp.einsum('bcjgn,bcjghp->bcghpn', bc, xdt * decay_end[..., None])
    chunk_decay = jnp.exp(acs[:, :, -1])

    def step(h, inp):
        s, d = inp
        return h * d[..., None, None] + s, h

    h_t, h_in = lax.scan(step, h0.astype(F32).reshape(bsz, G, hpg, P, N),
                         (jnp.moveaxis(states, 1, 0), jnp.moveaxis(chunk_decay, 1, 0)))
    h_t = h_t.reshape(bsz, H, P, N)
    if not with_y:
        return None, h_t
    h_in = jnp.moveaxis(h_in, 0, 1)
    acs_t = jnp.moveaxis(acs, 2, -1)
    lower = jnp.tril(jnp.ones((Q, Q), dtype=bool))
    seg = jnp.exp(jnp.where(lower, acs_t[..., :, None] - acs_t[..., None, :], -jnp.inf))
    cb = jnp.einsum('bcign,bcjgn->bcgij', cc, bc)
    y_diag = jnp.einsum('bcghij,bcjghp->bcighp', cb[:, :, :, None] * seg, xdt)
    y_off = jnp.einsum('bcign,bcghpn->bcighp', cc, h_in) * jnp.exp(acs)[..., None]
    y = (y_diag + y_off).reshape(bsz, L, H, P)
    return y.astype(xh.dtype), h_t


def ssd_branch(xbc_l, dt_l, z_l, xbc_c, dt_c, z_c, conv_w, conv_b, dt_bias, a_log, d_skip, norm_g):
    a = -jnp.exp(a_log.astype(F32))

    def prep(xbc, dt):
        bsz, n = xbc.shape[:2]
        u = jax.nn.silu(centred_dwconv(xbc, conv_w, conv_b))
        xs, bm, cm = jnp.split(u, [SSM_INNER, SSM_INNER + SSM_GROUPS * SSM_STATE], axis=-1)
        dts = jax.nn.softplus(dt.reshape(bsz, n, 2, SSM_HEADS).astype(F32) + dt_bias.astype(F32))
        return (xs.reshape(bsz, n, SSM_HEADS, SSM_HEAD_DIM), bm.reshape(bsz, n, SSM_GROUPS, SSM_STATE),
                cm.reshape(bsz, n, SSM_GROUPS, SSM_STATE), dts)

    def rev(t):
        return jnp.flip(t, axis=1)

    def bidir(xs, bm, cm, dts, h_f, h_b, with_y):
        y_f, hf = ssd_scan(xs, dts[:, :, 0], a[0], bm, cm, h_f, with_y)
        y_b, hb = ssd_scan(rev(xs), rev(dts[:, :, 1]), a[1], rev(bm), rev(cm), h_b, with_y)
        y = y_f + rev(y_b) + d_skip[:, None] * xs if with_y else None
        return y, hf, hb

    def gated_out(y, z):
        return rms_norm(y.reshape(z.shape) * jax.nn.silu(z), norm_g)

    with_ctx = z_c is not None
    xs_c, b_c, c_c, dts_c = prep(xbc_c, dt_c)
    h0 = jnp.zeros((xs_c.shape[0], SSM_HEADS, SSM_HEAD_DIM, SSM_STATE), F32)
    y_c, hc_f, hc_b = bidir(xs_c, b_c, c_c, dts_c, h0, h0, with_ctx)
    xs_l, b_l, c_l, dts_l = prep(xbc_l, dt_l)
    y_l, _, _ = bidir(xs_l, b_l, c_l, dts_l, hc_f, hc_b, True)
    out_l = gated_out(y_l, z_l)
    out_c = gated_out(y_c, z_c) if with_ctx else None
    return out_l, out_c


def swa_branch(q_l, k_l, v_l, q_c, k_c, v_c, q_g, k_g, sink, rope):
    bsz, L = k_l.shape[:2]
    n_ctx = k_c.shape[1]
    grp = SWA_Q_HEADS // SWA_KV_HEADS
    blk = SWA_BLOCK
    nb = L // blk
    scale = SWA_HEAD_DIM ** -0.5

    def q_heads(q):
        return rms_norm(q.reshape(q.shape[0], q.shape[1], SWA_Q_HEADS, SWA_HEAD_DIM), q_g)

    def kv_heads(k, v):
        shp = (k.shape[0], k.shape[1], SWA_KV_HEADS, SWA_HEAD_DIM)
        return rms_norm(k.reshape(shp), k_g), v.reshape(shp)

    kc, vc = kv_heads(k_c, v_c)
    kl, vl = kv_heads(k_l, v_l)
    kl = apply_axial_rope(kl, *rope)
    ql = apply_axial_rope(q_heads(q_l), *rope)
    sink_f = sink.astype(F32).reshape(SWA_KV_HEADS, grp)

    qb = ql.reshape(bsz, nb, blk, SWA_KV_HEADS, grp, SWA_HEAD_DIM)

    def band(t):
        tp = jnp.pad(t, ((0, 0), (blk, blk), (0, 0), (0, 0))).reshape(bsz, nb + 2, blk, SWA_KV_HEADS, SWA_HEAD_DIM)
        return jnp.concatenate([tp[:, :-2], tp[:, 1:-1], tp[:, 2:]], axis=2)

    kb, vb = band(kl), band(vl)
    s_win = jnp.einsum('bnqhgd,bnkhd->bnhgqk', qb, kb).astype(F32) * scale
    q_idx = jnp.arange(blk)[:, None] + blk
    k_idx = jnp.arange(3 * blk)[None, :]
    k_pos = (jnp.arange(nb)[:, None, None] - 1) * blk + k_idx
    mask = (jnp.abs(k_idx - q_idx) <= SWA_WINDOW)[None] & (k_pos >= 0) & (k_pos < L)
    s_win = jnp.where(mask[None, :, None, None], s_win, -jnp.inf)
    s_ctx = jnp.einsum('bnqhgd,bkhd->bnhgqk', qb, kc).astype(F32) * scale
    s_sink = jnp.broadcast_to(sink_f[None, None, :, :, None, None], s_win.shape[:-1] + (1,))
    p = jax.nn.softmax(jnp.concatenate([s_win, s_ctx, s_sink], axis=-1), axis=-1).astype(vb.dtype)
    o = (jnp.einsum('bnhgqk,bnkhd->bnqhgd', p[..., :3 * blk], vb)
         + jnp.einsum('bnhgqk,bkhd->bnqhgd', p[..., 3 * blk:3 * blk + n_ctx], vc))
    out_l = o.reshape(bsz, L, SWA_Q_HEADS * SWA_HEAD_DIM)
    if q_c is None:
        return out_l, None
    qc = q_heads(q_c).reshape(bsz, n_ctx, SWA_KV_HEADS, grp, SWA_HEAD_DIM)
    s_c = jnp.einsum('bqhgd,bkhd->bhgqk', qc, kc).astype(F32) * scale
    s_cs = jnp.broadcast_to(sink_f[None, :, :, None, None], s_c.shape[:-1] + (1,))
    pc = jax.nn.softmax(jnp.concatenate([s_c, s_cs], axis=-1), axis=-1).astype(vc.dtype)
    out_c = jnp.einsum('bhgqk,bkhd->bqhgd', pc[..., :n_ctx], vc).reshape(bsz, n_ctx, SWA_Q_HEADS * SWA_HEAD_DIM)
    return out_l, out_c


def mla_keys(ckv, kr, kv_lat_g, w_ukv, k_g, rope):
    bsz, n = ckv.shape[:2]
    kv = (rms_norm(ckv, kv_lat_g) @ w_ukv).reshape(bsz, n, MLA_HEADS, MLA_NOPE + MLA_V)
    k_nope = rms_norm(kv[..., :MLA_NOPE], k_g[:MLA_NOPE])
    k_rope = rms_norm(kr[:, :, None, :], k_g[MLA_NOPE:])
    if rope is not None:
        k_rope = apply_axial_rope(k_rope, *rope)
    k = jnp.concatenate([k_nope, jnp.broadcast_to(k_rope, (bsz, n, MLA_HEADS, MLA_ROPE))], axis=-1)
    return k, kv[..., MLA_NOPE:]


def mla_queries(cq, q_lat_g, w_uq, q_g, rope):
    bsz, n = cq.shape[:2]
    q = (rms_norm(cq, q_lat_g) @ w_uq).reshape(bsz, n, MLA_HEADS, MLA_QK)
    q_nope = rms_norm(q[..., :MLA_NOPE], q_g[:MLA_NOPE])
    q_rope = rms_norm(q[..., MLA_NOPE:], q_g[MLA_NOPE:])
    if rope is not None:
        q_rope = apply_axial_rope(q_rope, *rope)
    return jnp.concatenate([q_nope, q_rope], axis=-1)


def mla_branch(cq_l, ckv_l, kr_l, cq_c, ckv_c, kr_c, q_lat_g, kv_lat_g, w_uq, w_ukv, q_g, k_g, rope):
    bsz, L = cq_l.shape[:2]
    n_ctx = ckv_c.shape[1]
    scale = MLA_QK ** -0.5
    k_c, v_c = mla_keys(ckv_c, kr_c, kv_lat_g, w_ukv, k_g, None)
    k_l, v_l = mla_keys(ckv_l, kr_l, kv_lat_g, w_ukv, k_g, rope)
    q_l = mla_queries(cq_l, q_lat_g, w_uq, q_g, rope)
    k_all = jnp.concatenate([k_l, k_c], axis=1)
    v_all = jnp.concatenate([v_l, v_c], axis=1)
    nb = L // MLA_Q_BLOCK
    qb = jnp.moveaxis(q_l.reshape(bsz, nb, MLA_Q_BLOCK, MLA_HEADS, MLA_QK), 1, 0)

    def attend(qblk):
        s = jnp.einsum('bqhd,bkhd->bhqk', qblk, k_all).astype(F32) * scale
        p = jax.nn.softmax(s, axis=-1).astype(v_all.dtype)
        return jnp.einsum('bhqk,bkhd->bqhd', p, v_all)

    out_l = jnp.moveaxis(lax.map(attend, qb), 0, 1).reshape(bsz, L, MLA_HEADS * MLA_V)
    if cq_c is None:
        return out_l, None
    q_c = mla_queries(cq_c, q_lat_g, w_uq, q_g, None)
    s = jnp.einsum('bqhd,bkhd->bhqk', q_c, k_c).astype(F32) * scale
    p = jax.nn.softmax(s, axis=-1).astype(v_c.dtype)
    out_c = jnp.einsum('bhqk,bkhd->bqhd', p, v_c).reshape(bsz, n_ctx, MLA_HEADS * MLA_V)
    return out_l, out_c


def merge_branches(gates, y_ssm, y_swa, y_mla, w_p_ssm, w_p_swa, w_p_mla, w_o):
    g = jax.nn.sigmoid(gates.astype(F32)).astype(gates.dtype)
    g_ssm, g_swa, g_mla = jnp.split(g, N_BRANCH, axis=-1)
    merged = g_ssm * (y_ssm @ w_p_ssm) + g_swa * (y_swa @ w_p_swa) + g_mla * (y_mla @ w_p_mla)
    return merged @ w_o


def swiglu(h, w_in, w_out):
    g, u = jnp.split(h @ w_in, 2, axis=-1)
    return (jax.nn.silu(g) * u) @ w_out


def setup_inputs(seed: int = 0) -> dict:
    key = jax.random.key(seed)
    ks = iter(jax.random.split(key, 40))

    def nrm(shape, s):
        return jax.random.normal(next(ks), shape, F32) * s

    def gain(shape):
        return 1.0 + nrm(shape, 0.02)

    x = nrm((BATCH, SEQ, D_MODEL), 1.0)
    c = nrm((BATCH, D_MODEL), 1.0)
    ctx = nrm((BATCH, CTX_LEN, D_MODEL), 1.0)
    c_ctx = nrm((D_MODEL,), 1.0)
    w_mod = nrm((DEPTH, D_MODEL, 6 * D_MODEL), 0.5 * D_MODEL ** -0.5)
    b_mod = nrm((DEPTH, 6 * D_MODEL), 0.01)
    norm1_g = gain((DEPTH, D_MODEL))
    norm2_g = gain((DEPTH, D_MODEL))
    w_in = nrm((DEPTH, D_MODEL, IN_COLS), D_MODEL ** -0.5)
    ssm_conv_w = nrm((DEPTH, SSM_CONV, SSM_CONV_DIM), SSM_CONV ** -0.5)
    ssm_conv_b = nrm((DEPTH, SSM_CONV_DIM), 0.01)
    dt0 = jnp.exp(jax.random.uniform(next(ks), (DEPTH, 2, SSM_HEADS), F32, math.log(1e-3), math.log(1e-1)))
    ssm_dt_bias = dt0 + jnp.log(-jnp.expm1(-dt0))
    ssm_a_log = jnp.log(jax.random.uniform(next(ks), (DEPTH, 2, SSM_HEADS), F32, 1.0, 16.0))
    ssm_d = gain((DEPTH, SSM_HEADS))
    ssm_norm_g = gain((DEPTH, SSM_INNER))
    swa_q_norm_g = gain((DEPTH, SWA_HEAD_DIM))
    swa_k_norm_g = gain((DEPTH, SWA_HEAD_DIM))
    swa_sink = nrm((DEPTH, SWA_Q_HEADS), 0.5)
    mla_q_lat_g = gain((DEPTH, MLA_Q_RANK))
    mla_kv_lat_g = gain((DEPTH, MLA_KV_RANK))
    w_mla_uq = nrm((DEPTH, MLA_Q_RANK, MLA_HEADS * MLA_QK), MLA_Q_RANK ** -0.5)
    w_mla_ukv = nrm((DEPTH, MLA_KV_RANK, MLA_HEADS * (MLA_NOPE + MLA_V)), MLA_KV_RANK ** -0.5)
    mla_q_norm_g = gain((DEPTH, MLA_QK))
    mla_k_norm_g = gain((DEPTH, MLA_QK))
    w_p_ssm = nrm((DEPTH, SSM_INNER, D_MODEL), SSM_INNER ** -0.5)
    w_p_swa = nrm((DEPTH, SWA_Q_HEADS * SWA_HEAD_DIM, D_MODEL), (SWA_Q_HEADS * SWA_HEAD_DIM) ** -0.5)
    w_p_mla = nrm((DEPTH, MLA_HEADS * MLA_V, D_MODEL), (MLA_HEADS * MLA_V) ** -0.5)
    w_out = nrm((DEPTH, D_MODEL, D_MODEL), D_MODEL ** -0.5)
    w_ffn_in = nrm((DEPTH, D_MODEL, 2 * FFN_HIDDEN), D_MODEL ** -0.5)
    w_ffn_out = nrm((DEPTH, FFN_HIDDEN, D_MODEL), FFN_HIDDEN ** -0.5)
    return {'x': x, 'c': c, 'ctx': ctx, 'c_ctx': c_ctx, 'w_mod': w_mod, 'b_mod': b_mod,
            'norm1_g': norm1_g, 'norm2_g': norm2_g, 'w_in': w_in,
            'ssm_conv_w': ssm_conv_w, 'ssm_conv_b': ssm_conv_b, 'ssm_dt_bias': ssm_dt_bias,
            'ssm_a_log': ssm_a_log, 'ssm_d': ssm_d, 'ssm_norm_g': ssm_norm_g,
            'swa_q_norm_g': swa_q_norm_g, 'swa_k_norm_g': swa_k_norm_g, 'swa_sink': swa_sink,
            'mla_q_lat_g': mla_q_lat_g, 'mla_kv_lat_g': mla_kv_lat_g, 'w_mla_uq': w_mla_uq,
            'w_mla_ukv': w_mla_ukv, 'mla_q_norm_g': mla_q_norm_g, 'mla_k_norm_g': mla_k_norm_g,
            'w_p_ssm': w_p_ssm, 'w_p_swa': w_p_swa, 'w_p_mla': w_p_mla, 'w_out': w_out,
            'w_ffn_in': w_ffn_in, 'w_ffn_out': w_ffn_out}


def reference(x, c, ctx, c_ctx, w_mod, b_mod, norm1_g, norm2_g, w_in,
              ssm_conv_w, ssm_conv_b, ssm_dt_bias, ssm_a_log, ssm_d, ssm_norm_g,
              swa_q_norm_g, swa_k_norm_g, swa_sink,
              mla_q_lat_g, mla_kv_lat_g, w_mla_uq, w_mla_ukv, mla_q_norm_g, mla_k_norm_g,
              w_p_ssm, w_p_swa, w_p_mla, w_out, w_ffn_in, w_ffn_out):
    n_lat = x.shape[1]
    rows = n_lat // GRID_W
    rope_swa = axial_rope_tables(rows, SWA_HEAD_DIM)
    rope_mla = axial_rope_tables(rows, MLA_ROPE)
    silu_c = jax.nn.silu(c)
    silu_cc = jax.nn.silu(c_ctx)[None]
    x_l, x_c = x, ctx
    for i in range(DEPTH):
        last = i == DEPTH - 1
        sh1, sc1, gt1, sh2, sc2, gt2 = jnp.split(silu_c @ w_mod[i] + b_mod[i], 6, axis=-1)
        csh1, csc1, cgt1, csh2, csc2, cgt2 = jnp.split(silu_cc @ w_mod[i] + b_mod[i], 6, axis=-1)

        h_l = modulate(x_l, norm1_g[i], sh1, sc1)
        h_c = modulate(x_c, norm1_g[i], csh1, csc1)
        (xbc_l, dt_l, k_l, v_l, ckv_l, kr_l, z_l, q_l, cq_l, gates_l) = split_cols(h_l @ w_in[i], IN_WIDTHS)
        if last:
            (xbc_c, dt_c, k_c, v_c, ckv_c, kr_c) = split_cols(h_c @ w_in[i][:, :KV_COLS], IN_WIDTHS[:N_KV_SPLITS])
            z_c = q_c = cq_c = gates_c = None
        else:
            (xbc_c, dt_c, k_c, v_c, ckv_c, kr_c, z_c, q_c, cq_c, gates_c) = split_cols(h_c @ w_in[i], IN_WIDTHS)

        y_ssm_l, y_ssm_c = ssd_branch(xbc_l, dt_l, z_l, xbc_c, dt_c, z_c, ssm_conv_w[i], ssm_conv_b[i],
                                      ssm_dt_bias[i], ssm_a_log[i], ssm_d[i], ssm_norm_g[i])
        y_swa_l, y_swa_c = swa_branch(q_l, k_l, v_l, q_c, k_c, v_c, swa_q_norm_g[i], swa_k_norm_g[i],
                                      swa_sink[i], rope_swa)
        y_mla_l, y_mla_c = mla_branch(cq_l, ckv_l, kr_l, cq_c, ckv_c, kr_c, mla_q_lat_g[i], mla_kv_lat_g[i],
                                      w_mla_uq[i], w_mla_ukv[i], mla_q_norm_g[i], mla_k_norm_g[i], rope_mla)

        x_l = x_l + gt1[:, None] * merge_branches(gates_l, y_ssm_l, y_swa_l, y_mla_l,
                                                  w_p_ssm[i], w_p_swa[i], w_p_mla[i], w_out[i])
        x_l = x_l + gt2[:, None] * swiglu(modulate(x_l, norm2_g[i], sh2, sc2), w_ffn_in[i], w_ffn_out[i])

        if not last:
            x_c = x_c + cgt1[:, None] * merge_branches(gates_c, y_ssm_c, y_swa_c, y_mla_c,
                                                      w_p_ssm[i], w_p_swa[i], w_p_mla[i], w_out[i])
            x_c = x_c + cgt2[:, None] * swiglu(modulate(x_c, norm2_g[i], csh2, csc2), w_ffn_in[i], w_ffn_out[i])
    return x_l
```

```python
import numpy as np
import ml_dtypes
from contextlib import ExitStack
import concourse.bass as bass
import concourse.mybir as mybir
from concourse.bass_utils import run_bass_kernel_spmd

F32 = mybir.dt.float32
BF16 = mybir.dt.bfloat16
AF = mybir.ActivationFunctionType
ALU = mybir.AluOpType
AX = mybir.AxisListType

ENGS = ("tensor", "vector", "scalar", "gpsimd", "sync")
PE_DRAIN = True


class T:
    def __init__(self, name, ap, is_dram=False):
        self.name = name
        self.ap = ap
        self.is_dram = is_dram
        self.w_e = {}
        self.w_d = {}
        self.r_e = {}
        self.r_d = {}
        self.sem = None
        self.semv = 0
        self.disjoint = is_dram

    def __getitem__(self, k):
        return self.ap[k]


class Prog:
    def __init__(self):
        self.nc = bass.Bass("TRN2", target_bir_lowering=False)
        self.es = ExitStack()
        self.ops = {e: [] for e in ENGS}
        self.seen_e = {e: {} for e in ENGS}
        self.seen_d = {e: {} for e in ENGS}
        self.esem = {}
        self.n_dma_sems = 0
        self._n = 0

    def _name(self, name):
        self._n += 1
        return f"{name}_{self._n}"

    def sbuf(self, name, shape, dtype=F32):
        t = self.es.enter_context(self.nc.sbuf_tensor(self._name(name), list(shape), dtype))
        return T(name, t)

    def psum(self, name, shape, dtype=F32):
        t = self.es.enter_context(self.nc.psum_tensor(self._name(name), list(shape), dtype))
        return T(name, t)

    def view(self, name, ap):
        return T(name, ap)

    def dram(self, name, shape, dtype=F32, kind="Internal"):
        t = self.nc.dram_tensor(name, list(shape), dtype, kind=kind)
        return T(name, t.ap(), is_dram=True)

    def _collect(self, eng, reads, writes, is_dma=False):
        we = {}
        wd = {}

        def need_e(d, k):
            if d == eng and eng == "tensor" and not is_dma:
                return
            if self.seen_e[eng].get(d, -1) >= k:
                return
            if we.get(d, -1) < k:
                we[d] = k

        def need_d(st, v):
            if self.seen_d[eng].get(id(st), 0) >= v:
                return
            cur = wd.get(id(st))
            if cur is None or cur[1] < v:
                wd[id(st)] = (st, v)

        for t in reads:
            for d, k in t.w_e.items():
                need_e(d, k)
            for st, v in t.w_d.values():
                need_d(st, v)
        for t in writes:
            if not t.disjoint:
                for d, k in t.w_e.items():
                    need_e(d, k)
                for st, v in t.w_d.values():
                    need_d(st, v)
            for d, k in t.r_e.items():
                if d == eng and not is_dma:
                    continue
                need_e(d, k)
            for st, v in t.r_d.values():
                need_d(st, v)
        for d, k in we.items():
            self.seen_e[eng][d] = k
        for st, v in wd.values():
            self.seen_d[eng][id(st)] = v
        return list(we.items()), list(wd.values())

    def op(self, eng, fn, reads=(), writes=()):
        reads = [t for t in reads if t is not None]
        writes = [t for t in writes if t is not None]
        we, wd = self._collect(eng, reads, writes)
        idx = len(self.ops[eng])
        self.ops[eng].append(dict(fn=fn, we=we, wd=wd, dma=None))
        for t in reads:
            t.r_e[eng] = idx
        for t in writes:
            t.w_e = {eng: idx}
            t.w_d = {}
            t.r_e = {}
            t.r_d = {}
        return idx

    def dma(self, dst, dst_ap, src, src_ap, q="sync", semT=None, **kw):
        self.dma_group([(dst_ap, src_ap)], dst, src, q=q, semT=semT, **kw)

    def dma_group(self, pairs, dst, src, q="sync", semT=None, **kw):
        if semT is None:
            semT = src if dst.is_dram else dst
        if semT.sem is None:
            semT.sem = self.es.enter_context(self.nc.semaphore(self._name("d" + semT.name)))
            self.n_dma_sems += 1
        we, wd = self._collect(q, [src], [dst], is_dma=True)
        if semT.semv > 0 and self.seen_d[q].get(id(semT), 0) < semT.semv:
            wd = [x for x in wd if x[0] is not semT] + [(semT, semT.semv)]
            self.seen_d[q][id(semT)] = semT.semv
        n = len(pairs)
        semT.semv += 16 * n
        ev = (semT, semT.semv)

        def fn(e, pairs=pairs, sem=semT, kw=kw):
            for (o, i) in pairs:
                e.dma_start(out=o, in_=i, **kw).then_inc(sem.sem, 16)
            return None

        self.ops[q].append(dict(fn=fn, we=we, wd=wd, dma=True))
        src.r_d[id(semT)] = ev
        if dst.disjoint:
            dst.w_d[id(semT)] = ev
        else:
            dst.w_e = {}
            dst.w_d = {id(semT): ev}
        dst.r_e = {}
        dst.r_d = {}

    def final_wait(self, tiles, eng="sync"):
        we, wd = self._collect(eng, tiles, [])
        self.ops[eng].append(dict(fn=lambda e: None, we=we, wd=wd, dma=True))

    def finalize(self):
        nc = self.nc
        ms = {e: set() for e in ENGS}
        for e in ENGS:
            for o in self.ops[e]:
                for d, k in o["we"]:
                    ms[d].add(k)
        rank = {}
        for e in ENGS:
            srt = sorted(ms[e])
            rank[e] = {k: i + 1 for i, k in enumerate(srt)}
        for e in ENGS:
            if ms[e]:
                self.esem[e] = self.es.enter_context(nc.semaphore(self._name("e" + e)))
        with nc.Block() as block:
            for e in ENGS:
                ops = self.ops[e]
                if not ops:
                    continue

                def body(eng, e=e, ops=ops):
                    for idx, o in enumerate(ops):
                        for d, k in o["we"]:
                            eng.wait_ge(self.esem[d], rank[d][k])
                        for st, v in o["wd"]:
                            eng.wait_ge(st.sem, v)
                        inst = o["fn"](eng)
                        if not o["dma"] and idx in ms[e]:
                            assert inst is not None, "milestone op must return instruction"
                            if e == "tensor" and PE_DRAIN:
                                eng.drain().then_inc(self.esem[e], 1)
                            else:
                                inst.then_inc(self.esem[e], 1)

                getattr(block, e)(body)
        self.es.close()
        return nc

import math, os
DBG = os.environ.get('KDBG', '')

D = 1024
NT = 18
NR = NT * 128
HC = 2312
EPS = 1e-6
IN_COLS = 7904


def tile_col(i):
    return 2 + 128 * i if i < 2 else 262 + 128 * (i - 2)


class Ring:
    def __init__(self, P, name, n, shape, dtype=F32, psum=False):
        mk = P.psum if psum else P.sbuf
        self.b = [mk(f"{name}{i}", shape, dtype) for i in range(n)]
        self.i = 0

    def next(self):
        t = self.b[self.i % len(self.b)]
        self.i += 1
        return t


def gnorm(P, src, srcT, G, W, gain, dst, dstT, sq, ss, lnscale=0.0, eng="vector"):
    P.op("scalar", lambda e: e.activation(out=sq[:, 0:G * W].rearrange("p (g w) -> p g w", g=G), in_=src, func=AF.Square),
         [srcT], [sq])
    P.op("vector", lambda e: e.tensor_reduce(out=ss[:, 0:G], in_=sq[:, 0:G * W].rearrange("p (g w) -> p g w", g=G),
                                             axis=AX.X, op=ALU.add), [sq], [ss])
    P.op("scalar", lambda e: e.activation(out=ss[:, 0:G], in_=ss[:, 0:G], func=AF.Ln, scale=1.0 / W, bias=P.c_eps[:, 0:1]), [ss, P.c_epsT], [ss])
    P.op("scalar", lambda e: e.activation(out=ss[:, 0:G], in_=ss[:, 0:G], func=AF.Exp, scale=-0.5, bias=P.cbias(lnscale)), [ss, P.c_epsT], [ss])
    for g in range(G):
        P.op(eng, lambda e, g=g: e.scalar_tensor_tensor(out=dst[:, g, :], in0=src[:, g, :], scalar=ss[:, g:g + 1], in1=gain,
                                                         op0=ALU.mult, op1=ALU.mult), [srcT, ss, P.gainsT], [dstT])


def rope(P, xn, xnT, G, W, cos, sin, tabT, dst, dstT, tmp, tmpT):
    nf = W // 4
    if G == 1:
        sp = lambda ap: ap.rearrange("p g (a b f) -> p (g a) b f", a=2, b=2)
        c = cos.rearrange("p (a f) -> p a f", a=2)
        s = sin.rearrange("p (a f) -> p a f", a=2)
        x1, x2 = sp(xn)[:, :, 0, :], sp(xn)[:, :, 1, :]
        d1, d2 = sp(dst)[:, :, 0, :], sp(dst)[:, :, 1, :]
        t = tmp[:, 0:4 * 2 * nf].rearrange("p (k a f) -> p k a f", k=4, a=2)
    else:
        sp = lambda ap: ap.rearrange("p g (a b f) -> p g a b f", a=2, b=2)
        c = cos.rearrange("p (a f) -> p a f", a=2).unsqueeze(1).to_broadcast([128, G, 2, nf])
        s = sin.rearrange("p (a f) -> p a f", a=2).unsqueeze(1).to_broadcast([128, G, 2, nf])
        x1, x2 = sp(xn)[:, :, :, 0, :], sp(xn)[:, :, :, 1, :]
        d1, d2 = sp(dst)[:, :, :, 0, :], sp(dst)[:, :, :, 1, :]
        t = tmp[:, 0:4 * G * 2 * nf].rearrange("p (k g a f) -> p k g a f", k=4, g=G, a=2)
    E2 = "vector"
    P.op("vector", lambda e: e.tensor_tensor(out=t[:, 0], in0=x1, in1=c, op=ALU.mult), [xnT, tabT], [tmpT])
    P.op("vector", lambda e: e.tensor_tensor(out=t[:, 1], in0=x2, in1=s, op=ALU.mult), [xnT, tabT], [tmpT])
    P.op("vector", lambda e: e.tensor_tensor(out=d1, in0=t[:, 0], in1=t[:, 1], op=ALU.subtract), [tmpT], [dstT])
    P.op(E2, lambda e: e.tensor_tensor(out=t[:, 2], in0=x2, in1=c, op=ALU.mult), [xnT, tabT], [tmpT])
    P.op(E2, lambda e: e.tensor_tensor(out=t[:, 3], in0=x1, in1=s, op=ALU.mult), [xnT, tabT], [tmpT])
    P.op(E2, lambda e: e.tensor_tensor(out=d2, in0=t[:, 2], in1=t[:, 3], op=ALU.add), [tmpT], [dstT])


def build_LA(lvl=99, blocks=None):
    P = Prog()
    def fin():
        P.final_wait(outs)
        return P.finalize()
    din = lambda n, s, dt=F32: P.dram(n, s, dt, kind="ExternalInput")
    dout = lambda n, s, dt=F32: P.dram(n, s, dt, kind="ExternalOutput")
    xin = din("xin", [HC, D])
    cT = din("cT", [128, 8, 2])
    w_mod = din("w_mod", [D, 6 * D])
    bmodT = din("bmodT", [128, 48])
    g1T = din("g1T", [128, 8])
    hmask = din("hmask", [128, 4])
    w_in = din("w_in", [D, IN_COLS])
    cwb = din("cwb", [128, 5, 1536])
    cbrow = din("cbrow", [1, 1536])
    cbT = din("cbT", [128, 4])
    dtb = din("dtb", [128, 32])
    ropetab = din("ropetab", [NR, 192])
    gains = din("gains", [128, 1408])
    w_uq = din("w_uq", [384, 1536])
    w_ukv = din("w_ukv", [256, 2048])
    identf_d = din("identf", [128, 128])

    o_mod = dout("o_mod", [128, 48, 2])
    o_xs = dout("o_xs", [NR, 1024])
    o_btm = dout("o_btm", [NR, 256], BF16)
    o_bT = dout("o_bT", [2, 128, NR], BF16)
    o_cT = dout("o_cT", [2, 128, NR], BF16)
    o_dt = dout("o_dt", [NR, 32])
    o_zs = dout("o_zs", [NR, 1024])
    o_gates = dout("o_gates", [NR, 3072])
    o_skT = dout("o_skT", [2, 128, NR], BF16)
    o_sv = dout("o_sv", [NR, 256], BF16)
    o_sqT = dout("o_sqT", [8, 128, NR], BF16)
    o_mkT = dout("o_mkT", [8, 128, NR], BF16)
    o_mkrT = dout("o_mkrT", [128, NR], BF16)
    o_mv = dout("o_mv", [NR, 1024], BF16)
    o_mqT = dout("o_mqT", [8, 128, NR], BF16)
    o_mqrT = dout("o_mqrT", [4, 128, NR], BF16)
    outs = [o_mod, o_xs, o_btm, o_bT, o_cT, o_dt, o_zs, o_gates, o_skT, o_sv, o_sqT, o_mkT, o_mkrT, o_mv, o_mqT, o_mqrT]

    identf = P.sbuf("identf", [128, 128], F32)
    P.dma(identf, identf[:], identf_d, identf_d[:, :])
    identb = P.sbuf("identb", [128, 128], BF16)
    P.op("vector", lambda e: e.tensor_copy(out=identb[:], in_=identf[:]), [identf], [identb])
    ones_row = P.sbuf("ones_row", [1, 128], BF16)
    P.op("vector", lambda e: e.memset(ones_row[:], 1.0), [], [ones_row])
    cconst = P.sbuf("cconst", [128, 8], F32)
    P.c_epsT = cconst
    P.c_eps = cconst
    P.op("vector", lambda e: e.memset(cconst[:, 0:1], EPS), [], [cconst])
    P.op("vector", lambda e: e.memset(cconst[:, 1:2], 0.0), [], [cconst])
    P.op("vector", lambda e: e.memset(cconst[:, 2:3], math.log(128 ** -0.5)), [], [cconst])
    P.op("vector", lambda e: e.memset(cconst[:, 3:4], math.log(192 ** -0.5)), [], [cconst])
    P.op("vector", lambda e: e.memset(cconst[:, 4:5], 1.0), [], [cconst])
    LN_SWA, LN_MLA = math.log(128 ** -0.5), math.log(192 ** -0.5)

    def cbias(v):
        if v == 0.0:
            return cconst[:, 1:2]
        if v == LN_SWA:
            return cconst[:, 2:3]
        if v == LN_MLA:
            return cconst[:, 3:4]
        raise ValueError
    P.cbias = cbias

    gains_s = P.sbuf("gains_s", [128, 1408], F32)
    P.gainsT = gains_s
    P.dma(gains_s, gains_s[:], gains, gains[:, :])
    G_SQ, G_SK, G_QL, G_KVL, G_MQ, G_MK = 0, 128, 256, 640, 896, 1088
    dtb_s = P.sbuf("dtb_s", [128, 32], F32)
    P.dma(dtb_s, dtb_s[:], dtb, dtb[:, :])
    cbT_s = P.sbuf("cbT_s", [128, 4], F32)
    P.dma(cbT_s, cbT_s[:], cbT, cbT[:, :])
    cbrow_f = P.sbuf("cbrow_f", [1, 1536], F32)
    P.dma(cbrow_f, cbrow_f[:], cbrow, cbrow[:, :])
    cbrow_b = P.sbuf("cbrow_b", [1, 1536], BF16)
    P.op("vector", lambda e: e.tensor_copy(out=cbrow_b[:], in_=cbrow_f[:]), [cbrow_f], [cbrow_b])
    wuq_s = P.sbuf("wuq_s", [128, 3, 1536], BF16)
    P.dma(wuq_s, wuq_s[:], w_uq, w_uq.ap.rearrange("(kt p) n -> p kt n", p=128), q="gpsimd")
    wukv_s = P.sbuf("wukv_s", [128, 2, 2048], BF16)
    P.dma(wukv_s, wukv_s[:], w_ukv, w_ukv.ap.rearrange("(kt p) n -> p kt n", p=128), q="gpsimd")

    cT_s = P.sbuf("cT_s", [128, 8, 2], F32)
    P.dma(cT_s, cT_s[:], cT, cT[:, :, :])
    sc_s = P.sbuf("sc_s", [128, 8, 2], F32)
    P.op("scalar", lambda e: e.activation(out=sc_s[:], in_=cT_s[:], func=AF.Silu), [cT_s], [sc_s])
    bmod_s = P.sbuf("bmod_s", [128, 48], F32)
    P.dma(bmod_s, bmod_s[:], bmodT, bmodT[:, :])
    g1_s = P.sbuf("g1_s", [128, 8], F32)
    P.dma(g1_s, g1_s[:], g1T, g1T[:, :])
    modT = P.sbuf("modT", [128, 48, 2], F32)
    wm_ring = Ring(P, "wm", 2, [128, 8, 256], F32)
    pmod_full = P.psum("pmod", [128, 512], F32)
    pmod = pmod_full
    for cb in range(24):
        wm = wm_ring.next()
        P.dma(wm, wm[:], w_mod, w_mod.ap.rearrange("(kt p) n -> p kt n", p=128)[:, :, cb * 256:(cb + 1) * 256])
        for ct in range(2):
            for kt in range(8):
                P.op("tensor", lambda e, wm=wm, ct=ct, kt=kt: e.matmul(out=pmod[:, ct * 2:ct * 2 + 2], lhsT=wm[:, kt, ct * 128:(ct + 1) * 128],
                                                                       rhs=sc_s[:, kt, :], start=(kt == 0), stop=(kt == 7)),
                     [wm, sc_s], [pmod])
        P.op("vector", lambda e, cb=cb: e.tensor_tensor(out=modT[:, cb * 2:(cb + 1) * 2, :], in0=pmod[:, 0:4].rearrange("p (c t) -> p c t", c=2),
                                                        in1=bmod_s[:, cb * 2:(cb + 1) * 2].unsqueeze(2).to_broadcast([128, 2, 2]),
                                                        op=ALU.add), [pmod, bmod_s], [modT])
    P.dma(o_mod, o_mod[:, :, :], modT, modT[:])
    if lvl < 1:
        return fin()
    A1 = P.sbuf("A1", [128, 8, 2], F32)
    P.op("vector", lambda e: e.tensor_scalar(out=A1[:], in0=modT[:, 8:16, :], scalar1=1.0, scalar2=None, op0=ALU.add), [modT], [A1])
    P.op("vector", lambda e: e.tensor_tensor(out=A1[:], in0=A1[:], in1=g1_s[:].unsqueeze(2).to_broadcast([128, 8, 2]), op=ALU.mult),
         [A1, g1_s], [A1])

    hT = P.sbuf("hT", [128, 8, HC], BF16)
    hTv = [P.view(f"hTv{i}", None) for i in range(NT + 1)]
    xring = Ring(P, "xt", 2, [128, D], F32)
    sq = P.sbuf("sq", [128, 1024], F32)
    ss = P.sbuf("ss", [128, 8], F32)
    xn = P.sbuf("xn", [128, D], F32)
    ptr = Ring(P, "ptr", 2, [128, 4, 128], F32, psum=True)

    def make_hT(i, nrow, rows, cdst, col):
        xt = xring.next()
        pairs = []
        p0 = 0
        for (r, n) in rows:
            pairs.append((xt[p0:p0 + n, :], xin[r:r + n, :]))
            p0 += n
        P.dma_group(pairs, xt, xin)
        P.op("scalar", lambda e: e.activation(out=sq[0:nrow, :], in_=xt[0:nrow, :], func=AF.Square, accum_out=ss[0:nrow, 0:1]), [xt], [sq, ss])
        P.op("scalar", lambda e: e.activation(out=ss[0:nrow, 0:1], in_=ss[0:nrow, 0:1], func=AF.Ln, scale=1.0 / D, bias=cconst[0:nrow, 0:1]), [ss, cconst], [ss])
        P.op("scalar", lambda e: e.activation(out=ss[0:nrow, 0:1], in_=ss[0:nrow, 0:1], func=AF.Exp, scale=-0.5, bias=cconst[0:nrow, 1:2]), [ss, cconst], [ss])
        P.op("vector", lambda e: e.tensor_scalar(out=xn[0:nrow, :], in0=xt[0:nrow, :], scalar1=ss[0:nrow, 0:1], scalar2=None, op0=ALU.mult), [xt, ss], [xn])
        for half in range(2):
            pt = ptr.next()
            for j in range(4):
                kt = half * 4 + j
                P.op("tensor", lambda e, kt=kt, j=j, pt=pt: e.transpose(out=pt[:, j, 0:nrow], in_=xn[0:nrow, kt * 128:(kt + 1) * 128],
                                                                        identity=identf[0:nrow, 0:nrow]), [xn, identf], [pt])
            for j in range(4):
                kt = half * 4 + j
                for (cc, pp, n) in cdst:
                    P.op("scalar", lambda e, kt=kt, j=j, pt=pt, cc=cc, pp=pp, n=n: e.activation(
                        out=hT[:, kt, cc:cc + n], in_=pt[:, j, pp:pp + n], func=AF.Identity,
                        scale=A1[:, kt, col:col + 1], bias=modT[:, kt, col:col + 1]), [pt, A1, modT], [hTv[i]])

    for i in range(NT):
        make_hT(i, 128, [(tile_col(i), 128)], [(tile_col(i), 0, 128)], 1 if i < 2 else 0)
    make_hT(NT, 4, [(260, 2), (2310, 2)], [(260, 0, 2), (2310, 2, 2)], 0)
    hm_s = P.sbuf("hm_s", [128, 4], F32)
    P.dma(hm_s, hm_s[:], hmask, hmask[:, :])
    for kt in range(8):
        for k, cc in enumerate([260, 2310]):
            P.op("vector", lambda e, kt=kt, k=k, cc=cc: e.tensor_tensor(out=hT[:, kt, cc:cc + 2], in0=hT[:, kt, cc:cc + 2],
                                                                        in1=hm_s[:, 2 * k:2 * k + 2], op=ALU.mult), [hTv[NT], hm_s], [hTv[NT]])
    for cc in (0, 258):
        P.op("vector", lambda e, cc=cc: e.memset(hT[:, :, cc:cc + 2], 0.0), [], [hTv[NT]])
    hT_all = hTv

    if lvl < 2:
        return fin()
    wring = Ring(P, "wb", 2, [128, 8, 512], BF16)
    wkring = Ring(P, "wk", 1, [128, 5, 8, 512], BF16)
    cwring = Ring(P, "cw", 1, [128, 5, 512], F32)
    pacc = Ring(P, "pacc", 3, [128, 512], F32, psum=True)
    ptb = Ring(P, "ptb", 1, [128, 8, 128], BF16, psum=True)
    pup = Ring(P, "pup", 1, [128, 512], F32, psum=True)
    st_f = Ring(P, "stf", 3, [128, 512], F32)
    st_b = Ring(P, "stb", 3, [128, 512], BF16)
    st_T = Ring(P, "stT", 3, [128, 8, 128], BF16)
    nrm = Ring(P, "nrm", 2, [128, 512], F32)
    lat_T = Ring(P, "latT", 2, [128, 3, 128], BF16)
    nrb = Ring(P, "nrb", 2, [128, 512], BF16)
    rtmp = P.sbuf("rtmp", [128, 1024], F32)
    tab_ring = Ring(P, "tab", 2, [128, 192], F32)
    w_in_v = w_in.ap.rearrange("(kt p) n -> p kt n", p=128)

    def load_w(col0, n):
        wb = wring.next()
        P.dma(wb, wb[:, :, 0:n], w_in, w_in_v[:, :, col0:col0 + n], q="gpsimd")
        return wb

    def load_wk(col0, n):
        wb = load_w(col0, n)
        cw = cwring.next()
        P.dma(cw, cw[:, :, 0:n], cwb, cwb[:, :, col0:col0 + n])
        wk = wkring.next()
        for k in range(5):
            P.op("gpsimd", lambda e, k=k, wk=wk, wb=wb, cw=cw: e.tensor_tensor(
                out=wk[:, k, :, 0:n], in0=wb[:, :, 0:n], in1=cw[:, k, 0:n].unsqueeze(1).to_broadcast([128, 8, n]), op=ALU.mult),
                [wb, cw], [wk])
        return wk

    def tm_matmul(i, ps, n, wb=None, wk=None, bias_col0=None):
        c0 = tile_col(i)
        rd = [hT_all[i]]
        if wk is not None:
            rd = [hT_all[t] for t in range(max(i - 1, 0), min(i + 2, NT))] + [hT_all[NT]]
        if wk is None:
            for kt in range(8):
                P.op("tensor", lambda e, kt=kt: e.matmul(out=ps[:, 0:n], lhsT=hT[:, kt, c0:c0 + 128], rhs=wb[:, kt, 0:n],
                                                         start=(kt == 0), stop=(kt == 7)), rd + [wb], [ps])
        else:
            first = True
            for k in range(5):
                for kt in range(8):
                    P.op("tensor", lambda e, k=k, kt=kt, first=first: e.matmul(out=ps[:, 0:n], lhsT=hT[:, kt, c0 + k - 2:c0 + k - 2 + 128],
                                                                              rhs=wk[:, k, kt, 0:n], start=first, stop=False), rd + [wk], [ps])
                    first = False
            P.op("tensor", lambda e: e.matmul(out=ps[:, 0:n], lhsT=ones_row[0:1, :], rhs=cbrow_b[0:1, bias_col0:bias_col0 + n],
                                              start=False, stop=True), [ones_row, cbrow_b], [ps])

    def store(dst, dst_ap, st, st_ap):
        P.dma(dst, dst_ap, st, st_ap)

    def act_store(ps_ap, psT, n, func, dst, r0, c0, bf=False):
        st = (st_b if bf else st_f).next()
        P.op("scalar", lambda e: e.activation(out=st[:, 0:n], in_=ps_ap, func=func), [psT], [st])
        store(dst, dst[r0:r0 + 128, c0:c0 + n], st, st[:, 0:n])

    def transpose_store(src, srcT, G, dst, g0, r0, rows=128):
        pt = ptb.next()
        for g in range(G):
            P.op("tensor", lambda e, g=g: e.transpose(out=pt[:, g, :], in_=src[:, g * 128:(g + 1) * 128], identity=identb[:]), [srcT, identb], [pt])
        st = st_T.next()
        P.op("vector", lambda e: e.tensor_copy(out=st[:, 0:G, :], in_=pt[:, 0:G, :]), [pt], [st])
        if len(dst.ap.shape) == 3:
            store(dst, dst.ap[g0:g0 + G, :, r0:r0 + 128].rearrange("g d t -> d g t"), st, st[:, 0:G, :])
        else:
            store(dst, dst.ap[:, r0:r0 + 128], st, st[:, 0, :])

    def load_tab(i):
        tb = tab_ring.next()
        P.dma(tb, tb[:], ropetab, ropetab[i * 128:(i + 1) * 128, :])
        return tb

    def _blk2():
        for blk in range(2):
            wk = load_wk(blk * 512, 512)
            for i in range(NT):
                ps = pacc.next()
                tm_matmul(i, ps, 512, wk=wk, bias_col0=blk * 512)
                act_store(ps[:, 0:512], ps, 512, AF.Silu, o_xs, i * 128, blk * 512)
    def _blk3():
        wk = load_wk(1024, 512)
        for i in range(NT):
            ps = pacc.next()
            tm_matmul(i, ps, 256, wk=wk, bias_col0=1024)
            act_store(ps[:, 0:256], ps, 256, AF.Silu, o_btm, i * 128, 0, bf=True)
        groups = [(0, 2)] + [(2 + 4 * j, 4) for j in range(4)]
        for ct in range(4):
            dst = o_bT if ct < 2 else o_cT
            for (t0, nt) in groups:
                c0 = tile_col(t0)
                n = nt * 128
                ps = pacc.next()
                rd = [hT_all[t] for t in range(max(t0 - 1, 0), min(t0 + nt + 1, NT))] + [hT_all[NT], wk]
                first = True
                for k in range(5):
                    for kt in range(8):
                        P.op("tensor", lambda e, k=k, kt=kt, first=first, ps=ps, c0=c0, n=n, ct=ct: e.matmul(
                            out=ps[:, 0:n], lhsT=wk[:, k, kt, ct * 128:(ct + 1) * 128], rhs=hT[:, kt, c0 + k - 2:c0 + k - 2 + n],
                            start=first, stop=(k == 4 and kt == 7)), rd, [ps])
                        first = False
                st = st_b.next()
                P.op("scalar", lambda e, ps=ps, n=n, st=st, ct=ct: e.activation(out=st[:, 0:n], in_=ps[:, 0:n], func=AF.Silu, bias=cbT_s[:, ct:ct + 1]),
                     [ps, cbT_s], [st])
                store(dst, dst.ap[ct % 2, :, t0 * 128:t0 * 128 + n], st, st[:, 0:n])
    def _blk4():
        wb = load_w(1536, 32)
        for i in range(NT):
            ps = pacc.next()
            tm_matmul(i, ps, 32, wb=wb)
            st = st_f.next()
            P.op("vector", lambda e, ps=ps, st=st: e.tensor_tensor(out=st[:, 0:32], in0=ps[:, 0:32], in1=dtb_s[:], op=ALU.add), [ps, dtb_s], [st])
            P.op("scalar", lambda e, st=st: e.activation(out=st[:, 0:32], in_=st[:, 0:32], func=AF.Exp), [st], [st])
            P.op("scalar", lambda e, st=st: e.activation(out=st[:, 0:32], in_=st[:, 0:32], func=AF.Ln, bias=cconst[:, 4:5]), [st, cconst], [st])
            store(o_dt, o_dt[i * 128:(i + 1) * 128, :], st, st[:, 0:32])
    def _blk5():
        wb = load_w(1568, 512)
        for i in range(NT):
            ps = pacc.next()
            tm_matmul(i, ps, 512, wb=wb)
            tb = load_tab(i)
            xnf = nrm.next()
            gnorm(P, ps[:, 0:256].rearrange("p (g w) -> p g w", g=2), ps, 2, 128, gains_s[:, G_SK:G_SK + 128],
                  xnf[:, 0:256].rearrange("p (g w) -> p g w", g=2), xnf, sq, ss)
            xb = nrb.next()
            rope(P, xnf[:, 0:256].rearrange("p (g w) -> p g w", g=2), xnf, 2, 128, tb[:, 0:64], tb[:, 64:128], tb,
                 xb[:, 0:256].rearrange("p (g w) -> p g w", g=2), xb, rtmp, rtmp)
            if 'noT' not in DBG:
                transpose_store(xb, xb, 2, o_skT, 0, i * 128)
            if 'noV' not in DBG:
                act_store(ps[:, 256:512], ps, 256, AF.Identity, o_sv, i * 128, 0, bf=True)
    def _blk6():
        wb = load_w(2080, 256)
        for i in range(NT):
            ps = pacc.next()
            tm_matmul(i, ps, 256, wb=wb)
            cb16 = nrb.next()
            gnorm(P, ps[:, 0:256].rearrange("p (g w) -> p g w", g=1), ps, 1, 256, gains_s[:, G_KVL:G_KVL + 256],
                  cb16[:, 0:256].rearrange("p (g w) -> p g w", g=1), cb16, sq, ss)
            pt = ptb.next()
            for g in range(2):
                P.op("tensor", lambda e, g=g, pt=pt, cb16=cb16: e.transpose(out=pt[:, g, :], in_=cb16[:, g * 128:(g + 1) * 128], identity=identb[:]),
                     [cb16, identb], [pt])
            ckT = lat_T.next()
            P.op("vector", lambda e, pt=pt, ckT=ckT: e.tensor_copy(out=ckT[:, 0:2, :], in_=pt[:, 0:2, :]), [pt], [ckT])
            for hb in (range(4) if 'no6a' not in DBG else []):
                pu = pup.next()
                for kt in range(2):
                    P.op("tensor", lambda e, kt=kt, pu=pu, hb=hb, ckT=ckT: e.matmul(out=pu[:], lhsT=ckT[:, kt, :], rhs=wukv_s[:, kt, hb * 512:(hb + 1) * 512],
                                                                                   start=(kt == 0), stop=(kt == 1)), [ckT, wukv_s], [pu])
                kb = nrb.next()
                gnorm(P, pu[:].rearrange("p (g w) -> p g w", g=2)[:, :, 0:128], pu, 2, 128, gains_s[:, G_MK:G_MK + 128],
                      kb[:, 0:256].rearrange("p (g w) -> p g w", g=2), kb, sq, ss)
                transpose_store(kb, kb, 2, o_mkT, hb * 2, i * 128)
                st = st_b.next()
                P.op("scalar", lambda e, st=st, pu=pu: e.activation(out=st[:, 0:256].rearrange("p (g w) -> p g w", g=2),
                                                                     in_=pu[:].rearrange("p (g w) -> p g w", g=2)[:, :, 128:256], func=AF.Identity), [pu], [st])
                store(o_mv, o_mv[i * 128:(i + 1) * 128, hb * 256:(hb + 1) * 256], st, st[:, 0:256])
        wb2 = load_w(2336, 64)
        for i in range(NT):
            ps = pacc.next()
            tm_matmul(i, ps, 64, wb=wb2)
            tb = load_tab(i)
            krf = nrm.next()
            gnorm(P, ps[:, 0:64].rearrange("p (g w) -> p g w", g=1), ps, 1, 64, gains_s[:, G_MK + 128:G_MK + 192],
                  krf[:, 0:64].rearrange("p (g w) -> p g w", g=1), krf, sq, ss)
            krb = nrb.next()
            rope(P, krf[:, 0:64].rearrange("p (g w) -> p g w", g=1), krf, 1, 64, tb[:, 128:160], tb[:, 160:192], tb,
                 krb[:, 0:64].rearrange("p (g w) -> p g w", g=1), krb, rtmp, rtmp)
            P.op("vector", lambda e, krb=krb: e.tensor_copy(out=krb[:, 64:128], in_=krb[:, 0:64]), [krb], [krb])
            transpose_store(krb, krb, 1, o_mkrT, 0, i * 128)
    def _blk7():
        for blk in range(2):
            wb = load_w(2400 + blk * 512, 512)
            for i in range(NT):
                ps = pacc.next()
                tm_matmul(i, ps, 512, wb=wb)
                act_store(ps[:, 0:512], ps, 512, AF.Silu, o_zs, i * 128, blk * 512)
    def _blk8():
        for blk in range(2):
            wb = load_w(3424 + blk * 512, 512)
            for i in range(NT):
                ps = pacc.next()
                tm_matmul(i, ps, 512, wb=wb)
                tb = load_tab(i)
                xnf = nrm.next()
                gnorm(P, ps[:].rearrange("p (g w) -> p g w", g=4), ps, 4, 128, gains_s[:, G_SQ:G_SQ + 128],
                      xnf[:].rearrange("p (g w) -> p g w", g=4), xnf, sq, ss, lnscale=LN_SWA)
                xb = nrb.next()
                rope(P, xnf[:].rearrange("p (g w) -> p g w", g=4), xnf, 4, 128, tb[:, 0:64], tb[:, 64:128], tb,
                     xb[:].rearrange("p (g w) -> p g w", g=4), xb, rtmp, rtmp)
                transpose_store(xb, xb, 4, o_sqT, blk * 4, i * 128)
    def _blk9():
        wb = load_w(4448, 384)
        for i in range(NT):
            ps = pacc.next()
            tm_matmul(i, ps, 384, wb=wb)
            tb = load_tab(i)
            cb16 = nrb.next()
            gnorm(P, ps[:, 0:384].rearrange("p (g w) -> p g w", g=1), ps, 1, 384, gains_s[:, G_QL:G_QL + 384],
                  cb16[:, 0:384].rearrange("p (g w) -> p g w", g=1), cb16, sq, ss)
            pt = ptb.next()
            for g in range(3):
                P.op("tensor", lambda e, g=g, pt=pt, cb16=cb16: e.transpose(out=pt[:, g, :], in_=cb16[:, g * 128:(g + 1) * 128], identity=identb[:]),
                     [cb16, identb], [pt])
            cqT = lat_T.next()
            P.op("vector", lambda e, pt=pt, cqT=cqT: e.tensor_copy(out=cqT[:, 0:3, :], in_=pt[:, 0:3, :]), [pt], [cqT])
            for hb in range(4):
                pu = pup.next()
                for kt in range(3):
                    P.op("tensor", lambda e, kt=kt, pu=pu, hb=hb, cqT=cqT: e.matmul(out=pu[:, 0:384], lhsT=cqT[:, kt, :], rhs=wuq_s[:, kt, hb * 384:(hb + 1) * 384],
                                                                                   start=(kt == 0), stop=(kt == 2)), [cqT, wuq_s], [pu])
                puv = pu[:, 0:384].rearrange("p (g w) -> p g w", g=2)
                qnb = nrb.next()
                gnorm(P, puv[:, :, 0:128], pu, 2, 128, gains_s[:, G_MQ:G_MQ + 128],
                      qnb[:, 0:256].rearrange("p (g w) -> p g w", g=2), qnb, sq, ss, lnscale=LN_MLA)
                transpose_store(qnb, qnb, 2, o_mqT, hb * 2, i * 128)
                qrf = nrm.next()
                gnorm(P, puv[:, :, 128:192], pu, 2, 64, gains_s[:, G_MQ + 128:G_MQ + 192],
                      qrf[:, 0:128].rearrange("p (g w) -> p g w", g=2), qrf, sq, ss, lnscale=LN_MLA)
                qrb = nrb.next()
                rope(P, qrf[:, 0:128].rearrange("p (g w) -> p g w", g=2), qrf, 2, 64, tb[:, 128:160], tb[:, 160:192], tb,
                     qrb[:, 0:128].rearrange("p (g w) -> p g w", g=2), qrb, rtmp, rtmp)
                transpose_store(qrb, qrb, 1, o_mqrT, hb, i * 128)
    def _blk10():
        for blk in range(6):
            wb = load_w(4832 + blk * 512, 512)
            for i in range(NT):
                ps = pacc.next()
                tm_matmul(i, ps, 512, wb=wb)
                act_store(ps[:, 0:512], ps, 512, AF.Sigmoid, o_gates, i * 128, blk * 512)

    for _k, _f in [(2, _blk2), (3, _blk3), (4, _blk4), (5, _blk5), (6, _blk6), (7, _blk7), (8, _blk8), (9, _blk9), (10, _blk10)]:
        if (blocks is None and lvl >= _k) or (blocks is not None and _k in blocks):
            _f()
    P.final_wait(outs)
    return P.finalize()

import math

NEG = -30000.0


def build_LS():
    P = Prog()
    din = lambda n, s, dt=F32: P.dram(n, s, dt, kind="ExternalInput")
    dout = lambda n, s, dt=F32: P.dram(n, s, dt, kind="ExternalOutput")
    xs = din("xs", [NR, 1024])
    btm = din("btm", [NR, 256], BF16)
    bT = din("bT", [2, 128, NR], BF16)
    cT = din("cT", [2, 128, NR], BF16)
    dt = din("dt", [NR, 32])
    alog = din("alog", [128, 32])
    dskip = din("dskip", [128, 16])
    cst = din("cst", [128, 5, 128])
    nmask = din("nmask", [128, 2, 128])

    o_y = dout("o_y", [NR, 1024])
    o_cum = dout("o_cum", [NR, 32])
    o_F = dout("o_F", [4, 128, 1024])
    o_sumA = dout("o_sumA", [128, 4, 16])
    outs = [o_y, o_cum, o_F, o_sumA]

    cst_s = P.sbuf("cst_s", [128, 5, 128], F32)
    P.dma(cst_s, cst_s[:], cst, cst[:, :, :])
    U, L, ONES, IDF = cst_s[:, 0, :], cst_s[:, 1, :], cst_s[:, 2, :], cst_s[:, 3, :]
    nm_s = P.sbuf("nm_s", [128, 2, 128], F32)
    P.dma(nm_s, nm_s[:], nmask, nmask[:, :, :])
    a_s = P.sbuf("a_s", [128, 32], F32)
    P.dma(a_s, a_s[:], alog, alog[:, :])
    P.op("scalar", lambda e: e.activation(out=a_s[:], in_=a_s[:], func=AF.Exp), [a_s], [a_s])
    P.op("vector", lambda e: e.tensor_scalar(out=a_s[:], in0=a_s[:], scalar1=-1.0, scalar2=None, op0=ALU.mult), [a_s], [a_s])
    dsk_s = P.sbuf("dsk_s", [128, 16], F32)
    P.dma(dsk_s, dsk_s[:], dskip, dskip[:, :])
    onesb = P.sbuf("onesb", [128, 128], BF16)
    P.op("vector", lambda e: e.memset(onesb[:], 1.0), [], [onesb])

    yall = P.sbuf("yall", [128, NT, 1024], F32)
    yv = [P.view(f"yv{i}", None) for i in range(NT)]
    cum_all = P.sbuf("cum_all", [128, NT, 32], F32)
    cumv = [P.view(f"cumv{i}", None) for i in range(NT)]

    xring = Ring(P, "xs", 2, [128, 1024], F32)
    bring = Ring(P, "btm", 2, [128, 256], BF16)
    btring = Ring(P, "bT", 2, [128, 2, 128], BF16)
    ctring = Ring(P, "cT", 2, [128, 2, 128], BF16)
    dtring = Ring(P, "dt", 2, [128, 32], F32)

    dA = P.sbuf("dA", [128, 16], F32)
    acs = P.sbuf("acs", [128, 16], F32)
    tot = P.sbuf("tot", [128, 16], F32)
    run = P.sbuf("run", [128, 16], F32)
    dend = P.sbuf("dend", [128, 16], F32)
    eacs = P.sbuf("eacs", [128, 16], F32)
    cd = P.sbuf("cd", [128, 16], F32)
    R1 = P.sbuf("R1", [128, 16, 128], F32)
    R2 = P.sbuf("R2", [128, 16, 128], F32)
    xdt = P.sbuf("xdt", [128, 1024], BF16)
    xdtf = P.sbuf("xdtf", [128, 1024], F32)
    xdtd = P.sbuf("xdtd", [128, 1024], BF16)
    cbs = P.sbuf("cbs", [128, 2, 128], F32)
    Eb = Ring(P, "Eb", 2, [128, 512], F32)
    Mb = Ring(P, "Mb", 2, [128, 512], BF16)
    Hin = P.sbuf("Hin", [128, 1024], F32)
    Hbf = P.sbuf("Hbf", [128, 1024], BF16)
    ytmp = P.sbuf("ytmp", [128, 1024], F32)
    htmp = P.sbuf("htmp", [128, 1024], F32)

    psmall = P.psum("psmall", [128, 512], F32)
    parg = Ring(P, "parg", 2, [128, 512], F32, psum=True)
    py = P.psum("py", [128, 1024], F32)
    poff = P.psum("poff", [128, 1024], F32)

    def chunk(ci, d, first):
        r0 = ci * 128
        xt = xring.next(); P.dma(xt, xt[:], xs, xs[r0:r0 + 128, :])
        bt = bring.next(); P.dma(bt, bt[:], btm, btm[r0:r0 + 128, :])
        bTt = btring.next(); P.dma(bTt, bTt[:], bT, bT.ap[:, :, r0:r0 + 128].rearrange("g n t -> n g t"))
        cTt = ctring.next(); P.dma(cTt, cTt[:], cT, cT.ap[:, :, r0:r0 + 128].rearrange("g n t -> n g t"))
        dtt = dtring.next(); P.dma(dtt, dtt[:], dt, dt[r0:r0 + 128, :])
        CM = U if d == 0 else L
        dts = dtt[:, d * 16:(d + 1) * 16]
        if first:
            P.op("vector", lambda e: e.memset(run[:], 0.0), [], [run])
            P.op("vector", lambda e: e.memset(Hin[:], 0.0), [], [Hin])
            P.op("vector", lambda e: e.memset(Hbf[:], 0.0), [], [Hbf])
        P.op("vector", lambda e: e.tensor_tensor(out=dA[:], in0=dts, in1=a_s[:, d * 16:(d + 1) * 16], op=ALU.mult), [dtt, a_s], [dA])
        P.op("tensor", lambda e: e.matmul(out=psmall[:, 0:16], lhsT=CM, rhs=dA[:], start=True, stop=True), [cst_s, dA], [psmall])
        P.op("tensor", lambda e: e.matmul(out=psmall[:, 16:32], lhsT=ONES, rhs=dA[:], start=True, stop=True), [cst_s, dA], [psmall])
        P.op("vector", lambda e: e.tensor_copy(out=acs[:], in_=psmall[:, 0:16]), [psmall], [acs])
        P.op("vector", lambda e: e.tensor_copy(out=tot[:], in_=psmall[:, 16:32]), [psmall], [tot])
        P.op("vector", lambda e: e.tensor_tensor(out=cum_all[:, ci, d * 16:(d + 1) * 16], in0=acs[:], in1=run[:], op=ALU.add), [acs, run], [cumv[ci]])
        P.op("vector", lambda e: e.tensor_tensor(out=run[:], in0=run[:], in1=tot[:], op=ALU.add), [run, tot], [run])
        P.op("vector", lambda e: e.tensor_tensor(out=dend[:], in0=tot[:], in1=acs[:], op=ALU.subtract), [tot, acs], [dend])
        P.op("scalar", lambda e: e.activation(out=dend[:], in_=dend[:], func=AF.Exp), [dend], [dend])
        P.op("scalar", lambda e: e.activation(out=eacs[:], in_=acs[:], func=AF.Exp), [acs], [eacs])
        P.op("scalar", lambda e: e.activation(out=cd[:], in_=tot[:], func=AF.Exp), [tot], [cd])
        x3 = xt[:].rearrange("p (h w) -> p h w", h=16)
        P.op("vector", lambda e: e.tensor_tensor(out=xdtf[:].rearrange("p (h w) -> p h w", h=16), in0=x3,
                                                 in1=dts.unsqueeze(2).to_broadcast([128, 16, 64]), op=ALU.mult), [xt, dtt], [xdtf])
        P.op("gpsimd", lambda e: e.tensor_copy(out=xdt[:], in_=xdtf[:]), [xdtf], [xdt])
        P.op("vector", lambda e: e.tensor_tensor(out=xdtd[:].rearrange("p (h w) -> p h w", h=16), in0=xdtf[:].rearrange("p (h w) -> p h w", h=16),
                                                 in1=dend[:].unsqueeze(2).to_broadcast([128, 16, 64]), op=ALU.mult), [xdtf, dend], [xdtd])
        for g in range(2):
            P.op("tensor", lambda e, g=g: e.matmul(out=psmall[:, 128 + g * 128:256 + g * 128], lhsT=bTt[:, g, :], rhs=cTt[:, g, :], start=True, stop=True),
                 [bTt, cTt], [psmall])
        P.op("scalar", lambda e: e.activation(out=cbs[:], in_=psmall[:, 128:384].rearrange("p (g w) -> p g w", g=2), func=AF.Identity), [psmall], [cbs])
        P.op("gpsimd", lambda e: e.tensor_tensor(out=R1[:], in0=dA[:].unsqueeze(2).to_broadcast([128, 16, 128]),
                                                 in1=CM.unsqueeze(1).to_broadcast([128, 16, 128]), op=ALU.mult), [dA, cst_s], [R1])
        P.op("vector", lambda e: e.tensor_tensor(out=R2[:], in0=nm_s[:, d, :].unsqueeze(1).to_broadcast([128, 16, 128]),
                                                 in1=acs[:].unsqueeze(2).to_broadcast([128, 16, 128]), op=ALU.subtract), [nm_s, acs], [R2])
        if not first:
            for g in range(2):
                P.op("tensor", lambda e, g=g: e.matmul(out=poff[:, g * 512:(g + 1) * 512], lhsT=cTt[:, g, :], rhs=Hbf[:, g * 512:(g + 1) * 512],
                                                       start=True, stop=True), [cTt, Hbf], [poff])
            P.op("vector", lambda e: e.tensor_tensor(out=ytmp[:].rearrange("p (h w) -> p h w", h=16), in0=poff[:].rearrange("p (h w) -> p h w", h=16),
                                                     in1=eacs[:].unsqueeze(2).to_broadcast([128, 16, 64]), op=ALU.mult), [poff, eacs], [ytmp])
        for bk in range(4):
            pa = parg.next()
            P.op("tensor", lambda e, bk=bk, pa=pa: e.matmul(out=pa[:], lhsT=ONES, rhs=R1[:, bk * 4:(bk + 1) * 4, :], start=True, stop=False), [cst_s, R1], [pa])
            P.op("tensor", lambda e, bk=bk, pa=pa: e.matmul(out=pa[:], lhsT=IDF, rhs=R2[:, bk * 4:(bk + 1) * 4, :], start=False, stop=True), [cst_s, R2], [pa])
            Et = Eb.next()
            P.op("scalar", lambda e, pa=pa, Et=Et: e.activation(out=Et[:], in_=pa[:], func=AF.Exp), [pa], [Et])
            Mt = Mb.next()
            g = bk // 2
            P.op("gpsimd", lambda e, Et=Et, Mt=Mt, g=g: e.tensor_tensor(out=Mt[:].rearrange("p (h w) -> p h w", h=4), in0=Et[:].rearrange("p (h w) -> p h w", h=4),
                                                                         in1=cbs[:, g, :].unsqueeze(1).to_broadcast([128, 4, 128]), op=ALU.mult), [Et, cbs], [Mt])
            for hh in range(4):
                h = bk * 4 + hh
                P.op("tensor", lambda e, hh=hh, h=h, Mt=Mt: e.matmul(out=py[:, h * 64:(h + 1) * 64], lhsT=Mt[:, hh * 128:(hh + 1) * 128], rhs=xdt[:, h * 64:(h + 1) * 64],
                                                                    start=True, stop=True), [Mt, xdt], [py])
        if d == 0:
            if first:
                P.op("vector", lambda e: e.tensor_copy(out=yall[:, ci, :], in_=py[:]), [py], [yv[ci]])
            else:
                P.op("vector", lambda e: e.tensor_tensor(out=yall[:, ci, :], in0=py[:], in1=ytmp[:], op=ALU.add), [py, ytmp], [yv[ci]])
        else:
            if not first:
                P.op("gpsimd", lambda e: e.tensor_tensor(out=yall[:, ci, :], in0=yall[:, ci, :], in1=ytmp[:], op=ALU.add), [yv[ci], ytmp], [yv[ci]])
            P.op("vector", lambda e: e.tensor_tensor(out=yall[:, ci, :], in0=py[:], in1=yall[:, ci, :], op=ALU.add), [py, yv[ci]], [yv[ci]])
            P.op("vector", lambda e: e.tensor_tensor(out=ytmp[:].rearrange("p (h w) -> p h w", h=16), in0=x3,
                                                     in1=dsk_s[:].unsqueeze(2).to_broadcast([128, 16, 64]), op=ALU.mult), [xt, dsk_s], [ytmp])
            P.op("vector", lambda e: e.tensor_tensor(out=yall[:, ci, :], in0=yall[:, ci, :], in1=ytmp[:], op=ALU.add), [yv[ci], ytmp], [yv[ci]])
            P.dma(o_y, o_y[r0:r0 + 128, :], yv[ci], yall[:, ci, :], semT=yv[ci])
            P.dma(o_cum, o_cum[r0:r0 + 128, :], cumv[ci], cum_all[:, ci, :], semT=cumv[ci])
        for g in range(2):
            P.op("tensor", lambda e, g=g: e.matmul(out=poff[:, g * 512:(g + 1) * 512], lhsT=bt[:, g * 128:(g + 1) * 128], rhs=xdtd[:, g * 512:(g + 1) * 512],
                                                   start=True, stop=True), [bt, xdtd], [poff])
        if first:
            P.op("vector", lambda e: e.tensor_copy(out=Hin[:], in_=poff[:]), [poff], [Hin])
        else:
            P.op("vector", lambda e: e.tensor_tensor(out=htmp[:].rearrange("p (h w) -> p h w", h=16), in0=Hin[:].rearrange("p (h w) -> p h w", h=16),
                                                     in1=cd[:].unsqueeze(2).to_broadcast([128, 16, 64]), op=ALU.mult), [Hin, cd], [htmp])
            P.op("vector", lambda e: e.tensor_tensor(out=Hin[:], in0=poff[:], in1=htmp[:], op=ALU.add), [poff, htmp], [Hin])
        P.op("scalar", lambda e: e.activation(out=Hbf[:], in_=Hin[:], func=AF.Identity), [Hin], [Hbf])

    def seq(tiles, d, slot):
        order = tiles if d == 0 else tiles[::-1]
        for k, ci in enumerate(order):
            chunk(ci, d, k == 0)
        P.dma(o_F, o_F.ap[slot, :, :], Hin, Hin[:])
        P.dma(o_sumA, o_sumA.ap[:, slot, :], run, run[:])

    seq([0, 1], 0, 0)
    seq(list(range(2, NT)), 0, 2)
    seq([0, 1], 1, 1)
    seq(list(range(2, NT)), 1, 3)
    P.final_wait(outs)
    return P.finalize()

import math

NKB = 66
NSB = 20


def build_LT(do_swa=True, do_mla=True):
    P = Prog()
    din = lambda n, s, dt=F32: P.dram(n, s, dt, kind="ExternalInput")
    dout = lambda n, s, dt=F32: P.dram(n, s, dt, kind="ExternalOutput")
    sqT = din("sqT", [8, 128, NR], BF16)
    skT = din("skT", [2, 128, NSB * 128], BF16)
    sv = din("sv", [NSB * 128, 256], BF16)
    smask = din("smask", [128, 4, 512], BF16)
    sink = din("sink", [128, 8])
    mqT = din("mqT", [8, 128, NR], BF16)
    mqrT = din("mqrT", [4, 128, NR], BF16)
    mkT = din("mkT", [8, 128, NKB * 128], BF16)
    mkrT = din("mkrT", [128, NKB * 128], BF16)
    mv = din("mv", [8, NKB * 128, 128], BF16)
    identf_d = din("identf", [128, 128])
    o_yswaT = dout("o_yswaT", [8, 128, NR], BF16)
    o_ymlaT = dout("o_ymlaT", [8, 128, NR], BF16)
    outs = [o_yswaT, o_ymlaT]

    identf = P.sbuf("identf", [128, 128], F32)
    P.dma(identf, identf[:], identf_d, identf_d[:, :])
    identb = P.sbuf("identb", [128, 128], BF16)
    P.op("vector", lambda e: e.tensor_copy(out=identb[:], in_=identf[:]), [identf], [identb])
    onesb = P.sbuf("onesb", [128, 128], BF16)
    P.op("vector", lambda e: e.memset(onesb[:], 1.0), [], [onesb])

    ps_s = Ring(P, "ps_s", 3, [128, 512], F32, psum=True)
    ps_o = Ring(P, "ps_o", 2, [128, 512], F32, psum=True)
    ps_d = Ring(P, "ps_d", 2, [128, 512], F32, psum=True)
    pT = Ring(P, "pT", 3, [128, 512], BF16)
    rden = Ring(P, "rden", 2, [128, 512], F32)
    yst = Ring(P, "yst", 2, [128, 512], BF16)

    def attend(nq, steps, sink_ap, sinkT, dst, dst_ap):
        po = ps_o.next()
        pd = ps_d.next()
        n = len(steps)
        for si, (mms, v_ap, vT) in enumerate(steps):
            ps = ps_s.next()
            for mi, (l_ap, r_ap, rd) in enumerate(mms):
                P.op("tensor", lambda e, l_ap=l_ap, r_ap=r_ap, ps=ps, mi=mi, last=(mi == len(mms) - 1): e.matmul(
                    out=ps[:, 0:nq], lhsT=l_ap, rhs=r_ap, start=(mi == 0), stop=last), rd, [ps])
            pt = pT.next()
            P.op("scalar", lambda e, ps=ps, pt=pt: e.activation(out=pt[:, 0:nq], in_=ps[:, 0:nq], func=AF.Exp), [ps], [pt])
            P.op("tensor", lambda e, pt=pt, v_ap=v_ap, si=si: e.matmul(out=po[:, 0:nq], lhsT=v_ap, rhs=pt[:, 0:nq], start=(si == 0), stop=(si == n - 1)),
                 [pt, vT], [po])
            P.op("tensor", lambda e, pt=pt, si=si: e.matmul(out=pd[:, 0:nq], lhsT=onesb[:], rhs=pt[:, 0:nq], start=(si == 0), stop=(si == n - 1)),
                 [pt, onesb], [pd])
        rd_ = rden.next()
        if sink_ap is not None:
            P.op("vector", lambda e: e.tensor_tensor(out=rd_[:, 0:nq].rearrange("p (h q) -> p h q", h=4), in0=pd[:, 0:nq].rearrange("p (h q) -> p h q", h=4),
                                                     in1=sink_ap, op=ALU.add), [pd, sinkT], [rd_])
            P.op("vector", lambda e: e.reciprocal(out=rd_[:, 0:nq], in_=rd_[:, 0:nq]), [rd_], [rd_])
        else:
            P.op("vector", lambda e: e.reciprocal(out=rd_[:, 0:nq], in_=pd[:, 0:nq]), [pd], [rd_])
        ys = yst.next()
        P.op("vector", lambda e: e.tensor_tensor(out=ys[:, 0:nq], in0=po[:, 0:nq], in1=rd_[:, 0:nq], op=ALU.mult), [po, rd_], [ys])
        P.dma(dst, dst_ap, ys, ys[:, 0:nq] if len(dst_ap.shape) == 2 else ys[:, 0:nq].rearrange("p (h q) -> p h q", h=4))

    if do_swa:
        sq_s = P.sbuf("sq_s", [128, 8, NR], BF16)
        for h in range(8):
            pass
        P.dma_group([(sq_s[:, h, :], sqT.ap[h, :, :]) for h in range(8)], sq_s, sqT)
        sk_s = P.sbuf("sk_s", [128, 2, NSB * 128], BF16)
        P.dma_group([(sk_s[:, g, :], skT.ap[g, :, :]) for g in range(2)], sk_s, skT)
        sv_s = P.sbuf("sv_s", [128, NSB, 256], BF16)
        P.dma(sv_s, sv_s[:], sv, sv.ap.rearrange("(b p) d -> p b d", p=128))
        sm_s = P.sbuf("sm_s", [128, 4, 512], BF16)
        P.dma(sm_s, sm_s[:], smask, smask[:, :, :])
        esink = P.sbuf("esink", [128, 8], F32)
        P.dma(esink, esink[:], sink, sink[:, :])
        P.op("scalar", lambda e: e.activation(out=esink[:], in_=esink[:], func=AF.Exp), [esink], [esink])
        for i in range(NT):
            for g in range(2):
                q_ap = sq_s[:, g * 4:(g + 1) * 4, i * 128:(i + 1) * 128]
                if i < 2:
                    blks = [(0, None), (1, None)]
                else:
                    t = i - 2
                    blks = [(2 + t, 2 if t == 0 else 0), (3 + t, None), (4 + t, 3 if t == 15 else 1), (0, None), (1, None)]
                steps = []
                for (kb, mk) in blks:
                    mms = [(sk_s[:, g, kb * 128:(kb + 1) * 128], q_ap, [sk_s, sq_s])]
                    if mk is not None:
                        mms.append((identb[:], sm_s[:, mk, :], [identb, sm_s]))
                    steps.append((mms, sv_s[:, kb, g * 128:(g + 1) * 128], sv_s))
                attend(512, steps, esink[:, g * 4:(g + 1) * 4].unsqueeze(2).to_broadcast([128, 4, 128]), esink,
                       o_yswaT, o_yswaT.ap[g * 4:(g + 1) * 4, :, i * 128:(i + 1) * 128].rearrange("h d t -> d h t"))

    if do_mla:
        mq_s = P.sbuf("mq_s", [128, 8, NR], BF16)
        P.dma_group([(mq_s[:, h, :], mqT.ap[h, :, :]) for h in range(8)], mq_s, mqT)
        mqr_s = P.sbuf("mqr_s", [128, 4, NR], BF16)
        P.dma_group([(mqr_s[:, h, :], mqrT.ap[h, :, :]) for h in range(4)], mqr_s, mqrT)
        kr_s = P.sbuf("kr_s", [128, NKB * 128], BF16)
        P.dma(kr_s, kr_s[:], mkrT, mkrT[:, :])
        kring = Ring(P, "mk", 1, [128, NKB * 128], BF16)
        vring = Ring(P, "mv", 1, [128, NKB, 128], BF16)
        for h in range(8):
            k_s = kring.next()
            P.dma(k_s, k_s[:], mkT, mkT.ap[h, :, :])
            v_s = vring.next()
            vsrc = mv.ap[h, :, :].rearrange("(b p) d -> p b d", p=128)
            P.dma_group([(v_s[:, j * 22:(j + 1) * 22, :], vsrc[:, j * 22:(j + 1) * 22, :]) for j in range(3)], v_s, mv)
            hp = h % 2
            groups = [(0, 256, [64, 65])] + [(256 + 512 * j, 512, list(range(NKB))) for j in range(4)]
            for (q0, nq, kbs) in groups:
                steps = []
                for kb in kbs:
                    mms = [(k_s[:, kb * 128:(kb + 1) * 128], mq_s[:, h, q0:q0 + nq], [k_s, mq_s]),
                           (kr_s[hp * 64:(hp + 1) * 64, kb * 128:(kb + 1) * 128], mqr_s[hp * 64:(hp + 1) * 64, h // 2, q0:q0 + nq], [kr_s, mqr_s])]
                    steps.append((mms, v_s[:, kb, :], v_s))
                attend(nq, steps, None, None, o_ymlaT, o_ymlaT.ap[h, :, q0:q0 + nq])

    P.final_wait(outs)
    return P.finalize()

import math


def row_bcast(P, col_ap, colT, identf, ones8, dst, pst, scr):
    gtT, bd = scr
    P.op("tensor", lambda e: e.transpose(out=pst[0:8, 0:128], in_=col_ap, identity=identf[:]), [colT, identf], [pst])
    P.op("vector", lambda e: e.tensor_copy(out=gtT[0:8, :], in_=pst[0:8, 0:128]), [pst], [gtT])
    P.op("vector", lambda e: e.tensor_tensor(out=bd[0:8, :, :], in0=gtT[0:8, :].unsqueeze(1).to_broadcast([8, 8, 128]),
                                             in1=identf[0:8, 0:8].unsqueeze(2).to_broadcast([8, 8, 128]), op=ALU.mult), [gtT, identf], [bd])
    for hf in range(2):
        P.op("tensor", lambda e, hf=hf: e.matmul(out=pst[:, 0:512], lhsT=ones8[0:8, :], rhs=bd[0:8, hf * 4:(hf + 1) * 4, :], start=True, stop=True),
             [ones8, bd], [pst])
        P.op("vector", lambda e, hf=hf: e.tensor_copy(out=dst[:, hf * 512:(hf + 1) * 512], in_=pst[:, 0:512]), [pst], [dst])


def build_LC():
    P = Prog()
    din = lambda n, s, dt=F32: P.dram(n, s, dt, kind="ExternalInput")
    dout = lambda n, s, dt=F32: P.dram(n, s, dt, kind="ExternalOutput")
    yssm = din("yssm", [NR, 1024])
    cum = din("cum", [NR, 32])
    cT = din("cT", [2, 128, NR], BF16)
    zs = din("zs", [NR, 1024])
    gates = din("gates", [NR, 3072])
    yswaT = din("yswaT", [8, 128, NR], BF16)
    ymlaT = din("ymlaT", [8, 128, NR], BF16)
    xrows = din("xrows", [NR, 1024])
    mod = din("mod", [128, 48, 2])
    Fslot = din("Fslot", [2, 4, 128, 1024])
    sAslot = din("sAslot", [128, 2, 4, 16])
    w_p = [din(f"w_p{b}", [1024, 1024]) for b in range(3)]
    w_o = din("w_o", [1024, 1024])
    sng = din("sng", [128, 1024])
    identf_d = din("identf", [128, 128])
    o_x1 = dout("o_x1", [NR, 1024])

    identf = P.sbuf("identf", [128, 128], F32)
    P.dma(identf, identf[:], identf_d, identf_d[:, :])
    identb = P.sbuf("identb", [128, 128], BF16)
    P.op("vector", lambda e: e.tensor_copy(out=identb[:], in_=identf[:]), [identf], [identb])
    ones8 = P.sbuf("ones8", [8, 128], F32)
    P.op("vector", lambda e: e.memset(ones8[:], 1.0), [], [ones8])
    cconst = P.sbuf("cconst", [128, 8], F32)
    P.c_epsT = cconst
    P.c_eps = cconst
    P.op("vector", lambda e: e.memset(cconst[:, 0:1], EPS), [], [cconst])
    P.op("vector", lambda e: e.memset(cconst[:, 1:2], 0.0), [], [cconst])
    P.cbias = lambda v: cconst[:, 1:2]
    sng_s = P.sbuf("sng_s", [128, 1024], F32)
    P.dma(sng_s, sng_s[:], sng, sng[:, :])
    P.gainsT = sng_s
    mod_s = P.sbuf("mod_s", [128, 48, 2], F32)
    P.dma(mod_s, mod_s[:], mod, mod[:, :, :])
    cT_s = P.sbuf("cT_s", [128, 2, NR], BF16)
    P.dma_group([(cT_s[:, g, :], cT.ap[g, :, :]) for g in range(2)], cT_s, cT)

    pbig = Ring(P, "pbig", 2, [128, 1024], F32, psum=True)
    pacc = Ring(P, "pacc", 2, [128, 512], F32, psum=True)
    ptb = Ring(P, "ptb", 1, [128, 8, 128], BF16, psum=True)
    pst = P.psum("pst", [128, 512], F32)

    gtT = P.sbuf("gtT", [8, 128], F32)
    bd = P.sbuf("bd", [8, 8, 128], F32)
    gt1b = [P.sbuf(f"gt1b{c}", [128, 1024], F32) for c in range(2)]
    for c in range(2):
        row_bcast(P, mod_s[:, 16:24, c], mod_s, identf, ones8, gt1b[c], pst, (gtT, bd))

    Hib = [P.sbuf(f"Hib{d}", [128, 1024], BF16) for d in range(2)]
    Hacc = P.sbuf("Hacc", [128, 1024], F32)
    yring = Ring(P, "yt", 2, [128, 1024], F32)
    Fring = yring
    sA_s = P.sbuf("sA_s", [128, 2, 4, 16], F32)
    P.dma(sA_s, sA_s[:], sAslot, sAslot[:, :, :, :])
    P.op("scalar", lambda e: e.activation(out=sA_s[:], in_=sA_s[:], func=AF.Exp), [sA_s], [sA_s])
    for d in range(2):
        for s in range(4):
            Ft = Fring.next()
            P.dma(Ft, Ft[:], Fslot, Fslot.ap[d, s, :, :])
            if s == 0:
                P.op("vector", lambda e, Ft=Ft: e.tensor_copy(out=Hacc[:], in_=Ft[:]), [Ft], [Hacc])
            else:
                P.op("vector", lambda e, d=d, s=s: e.tensor_tensor(out=Hacc[:].rearrange("p (h w) -> p h w", h=16), in0=Hacc[:].rearrange("p (h w) -> p h w", h=16),
                                                                   in1=sA_s[:, d, s, :].unsqueeze(2).to_broadcast([128, 16, 64]), op=ALU.mult), [Hacc, sA_s], [Hacc])
                P.op("vector", lambda e, Ft=Ft: e.tensor_tensor(out=Hacc[:], in0=Hacc[:], in1=Ft[:], op=ALU.add), [Hacc, Ft], [Hacc])
        P.op("scalar", lambda e, d=d: e.activation(out=Hib[d][:], in_=Hacc[:], func=AF.Identity), [Hacc], [Hib[d]])

    srcT = P.sbuf("srcT", [128, 8, NR], BF16)
    srcv = [P.view(f"srcv{i}", None) for i in range(NT)]
    zring = Ring(P, "zt", 2, [128, 1024], F32)
    cring = Ring(P, "ct", 2, [128, 32], F32)
    sq = P.sbuf("sq", [128, 1024], F32)
    ss = P.sbuf("ss", [128, 8], F32)
    tmpf = P.sbuf("tmpf", [128, 1024], F32)
    ub = Ring(P, "ub", 2, [128, 1024], BF16)

    def to_srcT(i, srcb, srcbT):
        for hf in range(2):
            pt = ptb.next()
            for j in range(4):
                kt = hf * 4 + j
                P.op("tensor", lambda e, kt=kt, j=j, pt=pt: e.transpose(out=pt[:, j, :], in_=srcb[:, kt * 128:(kt + 1) * 128], identity=identb[:]),
                     [srcbT, identb], [pt])
            P.op("vector", lambda e, hf=hf, pt=pt: e.tensor_copy(out=srcT[:, hf * 4:(hf + 1) * 4, i * 128:(i + 1) * 128], in_=pt[:, 0:4, :]), [pt], [srcv[i]])

    def _tile1(i):
        yt = yring.next(); P.dma(yt, yt[:], yssm, yssm[i * 128:(i + 1) * 128, :])
        zt = zring.next(); P.dma(zt, zt[:], zs, zs[i * 128:(i + 1) * 128, :])
        if i >= 2:
            ct = cring.next(); P.dma(ct, ct[:], cum, cum[i * 128:(i + 1) * 128, :])
            P.op("scalar", lambda e, ct=ct: e.activation(out=ct[:], in_=ct[:], func=AF.Exp), [ct], [ct])
            for d in range(2):
                pc = pbig.next()
                for g in range(2):
                    P.op("tensor", lambda e, g=g, d=d, pc=pc: e.matmul(out=pc[:, g * 512:(g + 1) * 512], lhsT=cT_s[:, g, i * 128:(i + 1) * 128],
                                                                       rhs=Hib[d][:, g * 512:(g + 1) * 512], start=True, stop=True), [cT_s, Hib[d]], [pc])
                P.op("vector", lambda e, d=d, pc=pc, ct=ct: e.tensor_tensor(out=tmpf[:].rearrange("p (h w) -> p h w", h=16), in0=pc[:].rearrange("p (h w) -> p h w", h=16),
                                                                            in1=ct[:, d * 16:(d + 1) * 16].unsqueeze(2).to_broadcast([128, 16, 64]), op=ALU.mult),
                     [pc, ct], [tmpf])
                P.op("gpsimd", lambda e, yt=yt: e.tensor_tensor(out=yt[:], in0=yt[:], in1=tmpf[:], op=ALU.add), [yt, tmpf], [yt])
        P.op("gpsimd", lambda e, yt=yt, zt=zt: e.tensor_tensor(out=yt[:], in0=yt[:], in1=zt[:], op=ALU.mult), [yt, zt], [yt])
        u16 = ub.next()
        gnorm(P, yt[:].rearrange("p (g w) -> p g w", g=1), yt, 1, 1024, sng_s[:], u16[:].rearrange("p (g w) -> p g w", g=1), u16, sq, ss)
        to_srcT(i, u16, u16)


    for i in range(NT):
        _tile1(i)
    merged = P.sbuf("merged", [128, NT, 1024], F32)
    mv_ = [P.view(f"mv{i}", None) for i in range(NT)]
    wring = Ring(P, "wp", 1, [128, 8, 1024], BF16)
    gring = Ring(P, "gt", 3, [128, 512], F32)
    tmp2 = Ring(P, "tmp2", 2, [128, 512], F32)

    def load_w(w):
        wb = wring.next()
        P.dma_group([(wb[:, :, hf * 512:(hf + 1) * 512], w.ap.rearrange("(kt p) n -> p kt n", p=128)[:, :, hf * 512:(hf + 1) * 512]) for hf in range(2)],
                    wb, w, q="gpsimd")
        return wb

    for b in range(3):
        wb = load_w(w_p[b])
        if b > 0:
            src_d = yswaT if b == 1 else ymlaT
            for i in range(NT):
                P.dma(srcv[i], srcT[:, :, i * 128:(i + 1) * 128], src_d, src_d.ap[:, :, i * 128:(i + 1) * 128].rearrange("h d t -> d h t"), semT=srcv[i])
        def _tile2(i, b=b, wb=wb):
            for hf in range(2):
                ps = pacc.next()
                for kt in range(8):
                    P.op("tensor", lambda e, kt=kt, ps=ps, hf=hf, wb=wb: e.matmul(out=ps[:], lhsT=srcT[:, kt, i * 128:(i + 1) * 128], rhs=wb[:, kt, hf * 512:(hf + 1) * 512],
                                                                                 start=(kt == 0), stop=(kt == 7)), [srcv[i], wb], [ps])
                gt = gring.next()
                P.dma(gt, gt[:], gates, gates[i * 128:(i + 1) * 128, b * 1024 + hf * 512:b * 1024 + (hf + 1) * 512])
                if b == 0:
                    P.op("vector", lambda e, ps=ps, gt=gt, hf=hf: e.tensor_tensor(out=merged[:, i, hf * 512:(hf + 1) * 512], in0=ps[:], in1=gt[:], op=ALU.mult),
                         [ps, gt], [mv_[i]])
                else:
                    t2 = tmp2.next()
                    P.op("vector", lambda e, ps=ps, gt=gt, t2=t2: e.tensor_tensor(out=t2[:], in0=ps[:], in1=gt[:], op=ALU.mult), [ps, gt], [t2])
                    P.op("gpsimd", lambda e, t2=t2, hf=hf: e.tensor_tensor(out=merged[:, i, hf * 512:(hf + 1) * 512], in0=merged[:, i, hf * 512:(hf + 1) * 512], in1=t2[:], op=ALU.add),
                         [mv_[i], t2], [mv_[i]])
        for i in range(NT):
            _tile2(i)

    wb = load_w(w_o)
    xring = yring
    oring = zring
    def _tile3(i):
        m16 = ub.next()
        P.op("scalar", lambda e, m16=m16: e.activation(out=m16[:], in_=merged[:, i, :], func=AF.Identity), [mv_[i]], [m16])
        to_srcT(i, m16, m16)
        xt = xring.next(); P.dma(xt, xt[:], xrows, xrows[i * 128:(i + 1) * 128, :])
        xo = oring.next()
        col = 1 if i < 2 else 0
        for hf in range(2):
            ps = pacc.next()
            for kt in range(8):
                P.op("tensor", lambda e, kt=kt, ps=ps, hf=hf: e.matmul(out=ps[:], lhsT=srcT[:, kt, i * 128:(i + 1) * 128], rhs=wb[:, kt, hf * 512:(hf + 1) * 512],
                                                                      start=(kt == 0), stop=(kt == 7)), [srcv[i], wb], [ps])
            t2 = tmp2.next()
            P.op("vector", lambda e, ps=ps, t2=t2, hf=hf: e.tensor_tensor(out=t2[:], in0=ps[:], in1=gt1b[col][:, hf * 512:(hf + 1) * 512], op=ALU.mult), [ps, gt1b[col]], [t2])
            P.op("gpsimd", lambda e, t2=t2, hf=hf, xo=xo, xt=xt: e.tensor_tensor(out=xo[:, hf * 512:(hf + 1) * 512], in0=t2[:], in1=xt[:, hf * 512:(hf + 1) * 512], op=ALU.add),
                 [t2, xt], [xo])
        P.dma(o_x1, o_x1[i * 128:(i + 1) * 128, :], xo, xo[:])
    for i in range(NT):
        _tile3(i)
    P.final_wait([o_x1])
    return P.finalize()


def build_LF():
    P = Prog()
    din = lambda n, s, dt=F32: P.dram(n, s, dt, kind="ExternalInput")
    dout = lambda n, s, dt=F32: P.dram(n, s, dt, kind="ExternalOutput")
    x1 = din("x1", [NR, 1024])
    mod = din("mod", [128, 48, 2])
    g2T = din("g2T", [128, 8])
    w_fi = din("w_fi", [1024, 5632])
    w_fo = din("w_fo", [2816, 1024])
    identf_d = din("identf", [128, 128])
    o_x2 = dout("o_x2", [NR, 1024])

    identf = P.sbuf("identf", [128, 128], F32)
    P.dma(identf, identf[:], identf_d, identf_d[:, :])
    ones8 = P.sbuf("ones8", [8, 128], F32)
    P.op("vector", lambda e: e.memset(ones8[:], 1.0), [], [ones8])
    cconst = P.sbuf("cconst", [128, 8], F32)
    P.op("vector", lambda e: e.memset(cconst[:, 0:1], EPS), [], [cconst])
    P.op("vector", lambda e: e.memset(cconst[:, 1:2], 0.0), [], [cconst])
    mod_s = P.sbuf("mod_s", [128, 48, 2], F32)
    P.dma(mod_s, mod_s[:], mod, mod[:, :, :])
    g2_s = P.sbuf("g2_s", [128, 8], F32)
    P.dma(g2_s, g2_s[:], g2T, g2T[:, :])
    A2 = P.sbuf("A2", [128, 8, 2], F32)
    P.op("vector", lambda e: e.tensor_scalar(out=A2[:], in0=mod_s[:, 32:40, :], scalar1=1.0, scalar2=None, op0=ALU.add), [mod_s], [A2])
    P.op("vector", lambda e: e.tensor_tensor(out=A2[:], in0=A2[:], in1=g2_s[:].unsqueeze(2).to_broadcast([128, 8, 2]), op=ALU.mult), [A2, g2_s], [A2])

    ptr = Ring(P, "ptr", 2, [128, 4, 128], F32, psum=True)
    pg = Ring(P, "pg", 2, [128, 512], F32, psum=True)
    pu = Ring(P, "pu", 2, [128, 512], F32, psum=True)
    pout = Ring(P, "pout", 2, [128, 512], F32, psum=True)

    gtT = P.sbuf("gtT", [8, 128], F32)
    bd = P.sbuf("bd", [8, 8, 128], F32)
    gt2b = [P.sbuf(f"gt2b{c}", [128, 1024], F32) for c in range(2)]
    pst = pout.b[0]
    for c in range(2):
        row_bcast(P, mod_s[:, 40:48, c], mod_s, identf, ones8, gt2b[c], pst, (gtT, bd))

    wout = P.sbuf("wout", [128, 22, 1024], BF16)
    wo_v = w_fo.ap.rearrange("(c p) n -> p c n", p=128)
    woutv = [P.view(f"woutv{j}", None) for j in range(11)]
    for j in range(11):
        P.dma(woutv[j], wout[:, 2 * j:2 * j + 2, :], w_fo, wo_v[:, 2 * j:2 * j + 2, :], q="gpsimd", semT=woutv[j])

    HT = 9 * 128
    h2T = P.sbuf("h2T", [128, 8, HT], BF16)
    hidT = P.sbuf("hidT", [128, 22, HT], BF16)
    hidv = [[P.view(f"hid{c}_{g}", None) for g in range(3)] for c in range(22)]
    xring = Ring(P, "x1t", 3, [128, 1024], F32)
    sq = P.sbuf("sq", [128, 1024], F32)
    ss = P.sbuf("ss", [128, 8], F32)
    xn = P.sbuf("xn", [128, 1024], F32)
    wgr = Ring(P, "wg", 2, [128, 8, 256], BF16)
    wur = Ring(P, "wu", 2, [128, 8, 256], BF16)
    sgr = Ring(P, "sg", 2, [128, 512], F32)
    t2r = Ring(P, "t2", 2, [128, 512], F32)
    xor_ = Ring(P, "xo", 2, [128, 1024], F32)
    wfi_v = w_fi.ap.rearrange("(kt p) n -> p kt n", p=128)
    groups = [(0, 512), (512, 512), (1024, 128)]

    for half in range(2):
        h2v = [P.view(f"h2v{half}_{k}", None) for k in range(9)]
        for k in range(9):
            i = half * 9 + k
            col = 1 if i < 2 else 0
            xt = xring.next(); P.dma(xt, xt[:], x1, x1[i * 128:(i + 1) * 128, :])
            P.op("scalar", lambda e, xt=xt: e.activation(out=sq[:], in_=xt[:], func=AF.Square, accum_out=ss[:, 0:1]), [xt], [sq, ss])
            P.op("scalar", lambda e: e.activation(out=ss[:, 0:1], in_=ss[:, 0:1], func=AF.Ln, scale=1.0 / 1024, bias=cconst[:, 0:1]), [ss, cconst], [ss])
            P.op("scalar", lambda e: e.activation(out=ss[:, 0:1], in_=ss[:, 0:1], func=AF.Exp, scale=-0.5, bias=cconst[:, 1:2]), [ss, cconst], [ss])
            P.op("vector", lambda e, xt=xt: e.tensor_scalar(out=xn[:], in0=xt[:], scalar1=ss[:, 0:1], scalar2=None, op0=ALU.mult), [xt, ss], [xn])
            for hf in range(2):
                pt = ptr.next()
                for j in range(4):
                    kt = hf * 4 + j
                    P.op("tensor", lambda e, kt=kt, j=j, pt=pt: e.transpose(out=pt[:, j, :], in_=xn[:, kt * 128:(kt + 1) * 128], identity=identf[:]), [xn, identf], [pt])
                for j in range(4):
                    kt = hf * 4 + j
                    P.op("scalar", lambda e, kt=kt, j=j, pt=pt, k=k, col=col: e.activation(out=h2T[:, kt, k * 128:(k + 1) * 128], in_=pt[:, j, :], func=AF.Identity,
                                                                                           scale=A2[:, kt, col:col + 1], bias=mod_s[:, 24 + kt, col:col + 1]),
                         [pt, A2, mod_s], [h2v[k]])
        for cb in range(11):
            wg = wgr.next(); P.dma(wg, wg[:], w_fi, wfi_v[:, :, cb * 256:(cb + 1) * 256], q="gpsimd")
            wu = wur.next(); P.dma(wu, wu[:], w_fi, wfi_v[:, :, 2816 + cb * 256:2816 + (cb + 1) * 256], q="gpsimd")
            for gi, (t0, n) in enumerate(groups):
                rd = [h2v[k] for k in range(t0 // 128, (t0 + n) // 128)]
                for ct in range(2):
                    c = cb * 2 + ct
                    a = pg.next(); u = pu.next()
                    for kt in range(8):
                        P.op("tensor", lambda e, kt=kt, a=a, ct=ct, wg=wg, t0=t0, n=n: e.matmul(out=a[:, 0:n], lhsT=wg[:, kt, ct * 128:(ct + 1) * 128], rhs=h2T[:, kt, t0:t0 + n],
                                                                                                  start=(kt == 0), stop=(kt == 7)), rd + [wg], [a])
                    for kt in range(8):
                        P.op("tensor", lambda e, kt=kt, u=u, ct=ct, wu=wu, t0=t0, n=n: e.matmul(out=u[:, 0:n], lhsT=wu[:, kt, ct * 128:(ct + 1) * 128], rhs=h2T[:, kt, t0:t0 + n],
                                                                                                  start=(kt == 0), stop=(kt == 7)), rd + [wu], [u])
                    sg = sgr.next()
                    P.op("scalar", lambda e, a=a, sg=sg, n=n: e.activation(out=sg[:, 0:n], in_=a[:, 0:n], func=AF.Silu), [a], [sg])
                    P.op("vector", lambda e, u=u, sg=sg, c=c, t0=t0, n=n: e.tensor_tensor(out=hidT[:, c, t0:t0 + n], in0=u[:, 0:n], in1=sg[:, 0:n], op=ALU.mult),
                         [u, sg], [hidv[c][gi]])
        for k in range(9):
            i = half * 9 + k
            col = 1 if i < 2 else 0
            gi = 0 if k < 4 else (1 if k < 8 else 2)
            xt = xring.next(); P.dma(xt, xt[:], x1, x1[i * 128:(i + 1) * 128, :])
            xo = xor_.next()
            for hf in range(2):
                ps = pout.next()
                for c in range(22):
                    P.op("tensor", lambda e, c=c, ps=ps, hf=hf, k=k: e.matmul(out=ps[:], lhsT=hidT[:, c, k * 128:(k + 1) * 128], rhs=wout[:, c, hf * 512:(hf + 1) * 512],
                                                                             start=(c == 0), stop=(c == 21)), [hidv[c][gi], woutv[c // 2]], [ps])
                t2 = t2r.next()
                P.op("vector", lambda e, ps=ps, t2=t2, hf=hf, col=col: e.tensor_tensor(out=t2[:], in0=ps[:], in1=gt2b[col][:, hf * 512:(hf + 1) * 512], op=ALU.mult),
                     [ps, gt2b[col]], [t2])
                P.op("gpsimd", lambda e, t2=t2, hf=hf, xo=xo, xt=xt: e.tensor_tensor(out=xo[:, hf * 512:(hf + 1) * 512], in0=t2[:], in1=xt[:, hf * 512:(hf + 1) * 512], op=ALU.add),
                     [t2, xt], [xo])
            P.dma(o_x2, o_x2[i * 128:(i + 1) * 128, :], xo, xo[:])
    P.final_wait([o_x2])
    return P.finalize()

import numpy as np

D = 1024
SEQ = 8192
CTX = 256
GRID_W = 64
NT = 18
NR = NT * 128
HC = 2312
TPC = 2048


def rope_np(rows, rot_dim):
    n_freq = rot_dim // 4
    inv = np.power(np.float32(10000.0), -np.arange(n_freq, dtype=np.float32) / np.float32(n_freq)).astype(np.float32)
    r, col = np.meshgrid(np.arange(rows, dtype=np.float32), np.arange(GRID_W, dtype=np.float32), indexing='ij')
    ang = np.stack([r.reshape(-1)[:, None] * inv, col.reshape(-1)[:, None] * inv], axis=1).astype(np.float32)
    L = ang.shape[0]
    return np.cos(ang).reshape(L, -1).astype(np.float32), np.sin(ang).reshape(L, -1).astype(np.float32)


_ROPE = None


def rope_tab_core(q):
    global _ROPE
    if _ROPE is None:
        cs, ss = rope_np(SEQ // GRID_W, 128)
        cm, sm = rope_np(SEQ // GRID_W, 64)
        _ROPE = np.concatenate([cs, ss, cm, sm], axis=1)
    tab = np.zeros((NR, 192), np.float32)
    tab[:256, 0:64] = 1.0
    tab[:256, 128:160] = 1.0
    tab[256:] = _ROPE[q * TPC:(q + 1) * TPC]
    return tab


def colT(v, nt):
    return np.ascontiguousarray(v.reshape(nt, 128).T)


def bcast(v, p=128):
    return np.ascontiguousarray(np.broadcast_to(v.reshape(1, -1), (p, v.size)))


def prep_LA(inp, li, x_l, x_c):
    maps = []
    ident = np.eye(128, dtype=np.float32)
    gains = np.concatenate([inp['swa_q_norm_g'][li], inp['swa_k_norm_g'][li], inp['mla_q_lat_g'][li], inp['mla_kv_lat_g'][li],
                            inp['mla_q_norm_g'][li], inp['mla_k_norm_g'][li], np.zeros(128, np.float32)]).astype(np.float32)
    cw = inp['ssm_conv_w'][li]
    cwb = np.ascontiguousarray(np.broadcast_to(cw[None], (128, 5, 1536)))
    cb = inp['ssm_conv_b'][li]
    shared = dict(w_mod=inp['w_mod'][li], bmodT=colT(inp['b_mod'][li], 48), g1T=colT(inp['norm1_g'][li], 8),
                  w_in=inp['w_in'][li], cwb=cwb, cbrow=cb.reshape(1, 1536), cbT=colT(cb[1024:], 4),
                  dtb=bcast(inp['ssm_dt_bias'][li].reshape(-1)), gains=bcast(gains), w_uq=inp['w_mla_uq'][li], w_ukv=inp['w_mla_ukv'][li],
                  identf=ident)
    for r in range(8):
        b, q = r // 4, r % 4
        xin = np.zeros((HC, D), np.float32)
        xin[2:258] = x_c[b]
        t0 = q * TPC
        xin[262:262 + TPC] = x_l[b, t0:t0 + TPC]
        hm = np.zeros((128, 4), np.float32)
        if q > 0:
            xin[260:262] = x_l[b, t0 - 2:t0]
            hm[:, 0:2] = 1.0
        if q < 3:
            xin[2310:2312] = x_l[b, t0 + TPC:t0 + TPC + 2]
            hm[:, 2:4] = 1.0
        cT = np.stack([inp['c'][b].reshape(8, 128).T, inp['c_ctx'].reshape(8, 128).T], axis=-1).astype(np.float32)
        m = dict(shared)
        m.update(xin=xin, cT=np.ascontiguousarray(cT), hmask=hm, ropetab=rope_tab_core(q))
        maps.append(m)
    return maps


NEG = -30000.0


def ssd_consts():
    t = np.arange(128)
    U = (t[:, None] <= t[None, :]).astype(np.float32)
    L = (t[:, None] >= t[None, :]).astype(np.float32)
    cst = np.stack([U, L, np.ones((128, 128), np.float32), np.eye(128, dtype=np.float32), np.zeros((128, 128), np.float32)], axis=1)
    nm = np.stack([np.where(U > 0, 0.0, NEG), np.where(L > 0, 0.0, NEG)], axis=1).astype(np.float32)
    return np.ascontiguousarray(cst), np.ascontiguousarray(nm)


def prep_LS(inp, li, outA):
    cst, nm = ssd_consts()
    maps = []
    for r in range(8):
        o = outA[r]
        maps.append(dict(xs=o['o_xs'], btm=o['o_btm'], bT=o['o_bT'], cT=o['o_cT'], dt=o['o_dt'],
                         alog=bcast(inp['ssm_a_log'][li].reshape(-1)), dskip=bcast(inp['ssm_d'][li]), cst=cst, nmask=nm))
    return maps


def swa_masks(q):
    import ml_dtypes
    k = np.arange(128)[:, None]; qq = np.arange(128)[None, :]
    mP = np.where(k >= qq, 0.0, NEG).astype(np.float32)
    mN = np.where(k <= qq, 0.0, NEG).astype(np.float32)
    allneg = np.full((128, 128), NEG, np.float32)
    ms = [mP, mN, allneg if q == 0 else mP, allneg if q == 3 else mN]
    m = np.stack([np.tile(x, (1, 4)) for x in ms], axis=1)
    return np.ascontiguousarray(m).astype(ml_dtypes.bfloat16)


def prep_LT(inp, li, outA):
    maps = []
    ident = np.eye(128, dtype=np.float32)
    for r in range(8):
        b, q = r // 4, r % 4
        o = outA[r]
        grp = [outA[b * 4 + j] for j in range(4)]
        skT = np.asarray(o['o_skT'])
        sv = np.asarray(o['o_sv'])
        zk = np.zeros((2, 128, 128), skT.dtype); zv = np.zeros((128, 256), sv.dtype)
        pk = np.asarray(grp[q - 1]['o_skT'])[:, :, NR - 128:] if q > 0 else zk
        nk = np.asarray(grp[q + 1]['o_skT'])[:, :, 256:384] if q < 3 else zk
        pv = np.asarray(grp[q - 1]['o_sv'])[NR - 128:] if q > 0 else zv
        nv = np.asarray(grp[q + 1]['o_sv'])[256:384] if q < 3 else zv
        skT_e = np.concatenate([skT[:, :, :256], pk, skT[:, :, 256:], nk], axis=2)
        sv_e = np.concatenate([sv[:256], pv, sv[256:], nv], axis=0)
        mkT = np.concatenate([np.asarray(g_['o_mkT'])[:, :, 256:] for g_ in grp] + [np.asarray(o['o_mkT'])[:, :, :256]], axis=2)
        mkrT = np.concatenate([np.asarray(g_['o_mkrT'])[:, 256:] for g_ in grp] + [np.asarray(o['o_mkrT'])[:, :256]], axis=1)
        mvv = np.concatenate([np.asarray(g_['o_mv'])[256:] for g_ in grp] + [np.asarray(o['o_mv'])[:256]], axis=0)
        mvh = np.ascontiguousarray(mvv.reshape(-1, 8, 128).transpose(1, 0, 2))
        maps.append(dict(sqT=o['o_sqT'], skT=np.ascontiguousarray(skT_e), sv=np.ascontiguousarray(sv_e), smask=swa_masks(q),
                         sink=bcast(inp['swa_sink'][li]), mqT=o['o_mqT'], mqrT=o['o_mqrT'], mkT=np.ascontiguousarray(mkT),
                         mkrT=np.ascontiguousarray(mkrT), mv=mvh, identf=ident))
    return maps


def fake_outA(ref, r):
    import ml_dtypes
    bf = ml_dtypes.bfloat16
    b, q = r // 4, r % 4
    sl = slice(q * TPC, (q + 1) * TPC)
    cat = lambda c, l: np.concatenate([ref[c][b], ref[l][b, sl]], axis=0)
    conv = cat('c_conv', 'l_conv')
    B = conv[:, 1024:1280]; C = conv[:, 1280:1536]
    T = lambda a: np.ascontiguousarray(a.transpose(1, 2, 0))
    mk = cat('c_mk', 'l_mk'); mq = cat('c_mq', 'l_mq') * np.float32(192 ** -0.5)
    kr = mk[:, 0, 128:]
    return dict(o_xs=conv[:, :1024].copy(), o_btm=B.astype(bf), o_bT=T(B.reshape(-1, 2, 128)).astype(bf), o_cT=T(C.reshape(-1, 2, 128)).astype(bf),
                o_dt=cat('c_dt', 'l_dt'), o_zs=cat('c_zs', 'l_zs'), o_gates=cat('c_gates', 'l_gates'),
                o_skT=T(cat('c_sk', 'l_sk')).astype(bf), o_sv=cat('c_sv', 'l_sv').reshape(-1, 256).astype(bf),
                o_sqT=T(cat('c_sq', 'l_sq') * np.float32(128 ** -0.5)).astype(bf),
                o_mkT=T(mk[:, :, :128]).astype(bf), o_mkrT=np.ascontiguousarray(np.concatenate([kr, kr], 1).T).astype(bf),
                o_mv=cat('c_mv', 'l_mv').reshape(-1, 1024).astype(bf), o_mqT=T(mq[:, :, :128]).astype(bf),
                o_mqrT=np.ascontiguousarray(mq[:, :, 128:].reshape(-1, 4, 128).transpose(1, 2, 0)).astype(bf))


def prep_LC(inp, li, outA, outS, outT, x_l, x_c):
    maps = []
    ident = np.eye(128, dtype=np.float32)
    shared = dict(w_p0=inp['w_p_ssm'][li], w_p1=inp['w_p_swa'][li], w_p2=inp['w_p_mla'][li], w_o=inp['w_out'][li],
                  sng=bcast(inp['ssm_norm_g'][li]), identf=ident)
    for r in range(8):
        b, q = r // 4, r % 4
        F = np.zeros((2, 4, 128, 1024), np.float32)
        sA = np.zeros((128, 2, 4, 16), np.float32)
        F[0, 0] = np.asarray(outS[r]['o_F'])[0]
        F[1, 0] = np.asarray(outS[r]['o_F'])[1]
        for s in range(1, 4):
            j = s - 1
            if j < q:
                F[0, s] = np.asarray(outS[b * 4 + j]['o_F'])[2]
                sA[:, 0, s] = np.asarray(outS[b * 4 + j]['o_sumA'])[:, 2, :]
            j = 4 - s
            if j > q:
                F[1, s] = np.asarray(outS[b * 4 + j]['o_F'])[3]
                sA[:, 1, s] = np.asarray(outS[b * 4 + j]['o_sumA'])[:, 3, :]
        xrows = np.concatenate([x_c[b], x_l[b, q * TPC:(q + 1) * TPC]], axis=0).astype(np.float32)
        m = dict(shared)
        m.update(yssm=outS[r]['o_y'], cum=outS[r]['o_cum'], cT=outA[r]['o_cT'], zs=outA[r]['o_zs'], gates=outA[r]['o_gates'],
                 yswaT=outT[r]['o_yswaT'], ymlaT=outT[r]['o_ymlaT'], xrows=np.ascontiguousarray(xrows), mod=outA[r]['o_mod'],
                 Fslot=F, sAslot=sA)
        maps.append(m)
    return maps


def prep_LF(inp, li, outA, x1s):
    ident = np.eye(128, dtype=np.float32)
    shared = dict(g2T=colT(inp['norm2_g'][li], 8), w_fi=inp['w_ffn_in'][li], w_fo=inp['w_ffn_out'][li], identf=ident)
    maps = []
    for r in range(8):
        m = dict(shared)
        m.update(x1=x1s[r], mod=outA[r]['o_mod'])
        maps.append(m)
    return maps


_PROGS = {}


def _prog(name):
    if name not in _PROGS:
        _PROGS[name] = {"A": build_LA, "S": build_LS, "T": build_LT, "C": build_LC, "F": build_LF}[name]()
    return _PROGS[name]


def _run(name, maps):
    nc = {"A": build_LA, "S": build_LS, "T": build_LT, "C": build_LC, "F": build_LF}[name]()
    res = run_bass_kernel_spmd(nc, maps, core_ids=list(range(8)))
    return [{k: np.asarray(v) for k, v in res.results[r].items()} for r in range(8)]


def kernel(**inputs):
    inp = {k: np.asarray(v) for k, v in inputs.items()}
    x_l = inp['x'].astype(np.float32)
    x_c = inp['ctx'].astype(np.float32)
    for li in range(2):
        outA = _run("A", prep_LA(inp, li, x_l, x_c))
        outS = _run("S", prep_LS(inp, li, outA))
        outT = _run("T", prep_LT(inp, li, outA))
        outC = _run("C", prep_LC(inp, li, outA, outS, outT, x_l, x_c))
        outF = _run("F", prep_LF(inp, li, outA, {r: outC[r]['o_x1'] for r in range(8)}))
        x_l = np.stack([np.concatenate([outF[b * 4 + q]['o_x2'][256:] for q in range(4)], axis=0) for b in range(2)]).astype(np.float32)
        x_c = np.stack([outF[b * 4]['o_x2'][:256] for b in range(2)]).astype(np.float32)
    return x_l
```

```python
import numpy as np
import ml_dtypes
from contextlib import ExitStack
import concourse.bass as bass
import concourse.mybir as mybir
from concourse.bass_utils import run_bass_kernel_spmd

F32 = mybir.dt.float32
BF16 = mybir.dt.bfloat16
AF = mybir.ActivationFunctionType
ALU = mybir.AluOpType
AX = mybir.AxisListType

ENGS = ("tensor", "vector", "scalar", "gpsimd", "sync")
PE_DRAIN = False


class T:
    def __init__(self, name, ap, is_dram=False):
        self.name = name
        self.ap = ap
        self.is_dram = is_dram
        self.w_e = {}
        self.w_d = {}
        self.r_e = {}
        self.r_d = {}
        self.sem = None
        self.semv = 0
        self.disjoint = is_dram

    def __getitem__(self, k):
        return self.ap[k]


class Prog:
    def __init__(self):
        self.nc = bass.Bass("TRN2", target_bir_lowering=False)
        self.es = ExitStack()
        self.ops = {e: [] for e in ENGS}
        self.seen_e = {e: {} for e in ENGS}
        self.seen_d = {e: {} for e in ENGS}
        self.esem = {}
        self.n_dma_sems = 0
        self._n = 0

    def _name(self, name):
        self._n += 1
        return f"{name}_{self._n}"

    def sbuf(self, name, shape, dtype=F32):
        t = self.es.enter_context(self.nc.sbuf_tensor(self._name(name), list(shape), dtype))
        return T(name, t)

    def psum(self, name, shape, dtype=F32):
        t = self.es.enter_context(self.nc.psum_tensor(self._name(name), list(shape), dtype))
        return T(name, t)

    def view(self, name, ap):
        return T(name, ap)

    def dram(self, name, shape, dtype=F32, kind="Internal"):
        t = self.nc.dram_tensor(name, list(shape), dtype, kind=kind)
        return T(name, t.ap(), is_dram=True)

    def _collect(self, eng, reads, writes, is_dma=False):
        we = {}
        wd = {}

        def need_e(d, k):
            if d == eng and eng == "tensor" and not is_dma:
                return
            if self.seen_e[eng].get(d, -1) >= k:
                return
            if we.get(d, -1) < k:
                we[d] = k

        def need_d(st, v):
            if self.seen_d[eng].get(id(st), 0) >= v:
                return
            cur = wd.get(id(st))
            if cur is None or cur[1] < v:
                wd[id(st)] = (st, v)

        for t in reads:
            for d, k in t.w_e.items():
                need_e(d, k)
            for st, v in t.w_d.values():
                need_d(st, v)
        for t in writes:
            if not t.disjoint:
                for d, k in t.w_e.items():
                    need_e(d, k)
                for st, v in t.w_d.values():
                    need_d(st, v)
            for d, k in t.r_e.items():
                if d == eng and not is_dma:
                    continue
                need_e(d, k)
            for st, v in t.r_d.values():
                need_d(st, v)
        for d, k in we.items():
            self.seen_e[eng][d] = k
        for st, v in wd.values():
            self.seen_d[eng][id(st)] = v
        return list(we.items()), list(wd.values())

    def op(self, eng, fn, reads=(), writes=()):
        reads = [t for t in reads if t is not None]
        writes = [t for t in writes if t is not None]
        we, wd = self._collect(eng, reads, writes)
        idx = len(self.ops[eng])
        self.ops[eng].append(dict(fn=fn, we=we, wd=wd, dma=None))
        for t in reads:
            t.r_e[eng] = idx
        for t in writes:
            t.w_e = {eng: idx}
            t.w_d = {}
            t.r_e = {}
            t.r_d = {}
        return idx

    def dma(self, dst, dst_ap, src, src_ap, q="sync", semT=None, **kw):
        self.dma_group([(dst_ap, src_ap)], dst, src, q=q, semT=semT, **kw)

    def dma_group(self, pairs, dst, src, q="sync", semT=None, **kw):
        if semT is None:
            semT = src if dst.is_dram else dst
        if semT.sem is None:
            semT.sem = self.es.enter_context(self.nc.semaphore(self._name("d" + semT.name)))
            self.n_dma_sems += 1
        we, wd = self._collect(q, [src], [dst], is_dma=True)
        if semT.semv > 0 and self.seen_d[q].get(id(semT), 0) < semT.semv:
            wd = [x for x in wd if x[0] is not semT] + [(semT, semT.semv)]
            self.seen_d[q][id(semT)] = semT.semv
        n = len(pairs)
        semT.semv += 16 * n
        ev = (semT, semT.semv)

        def fn(e, pairs=pairs, sem=semT, kw=kw):
            for (o, i) in pairs:
                e.dma_start(out=o, in_=i, **kw).then_inc(sem.sem, 16)
            return None

        self.ops[q].append(dict(fn=fn, we=we, wd=wd, dma=True))
        src.r_d[id(semT)] = ev
        if dst.disjoint:
            dst.w_d[id(semT)] = ev
        else:
            dst.w_e = {}
            dst.w_d = {id(semT): ev}
        dst.r_e = {}
        dst.r_d = {}

    def collective(self, kind, dst, dst_ap, src, src_ap, groups):
        q = "gpsimd"
        semT = dst
        if semT.sem is None:
            semT.sem = self.es.enter_context(self.nc.semaphore(self._name("c" + semT.name)))
            self.n_dma_sems += 1
        we, wd = self._collect(q, [src], [dst], is_dma=True)
        for st, v in list(dst.w_d.values()) + list(dst.r_d.values()):
            if self.seen_d[q].get(id(st), 0) < v:
                wd = [x for x in wd if x[0] is not st] + [(st, v)]
                self.seen_d[q][id(st)] = v
        if semT.semv > 0 and self.seen_d[q].get(id(semT), 0) < semT.semv:
            wd = [x for x in wd if x[0] is not semT] + [(semT, semT.semv)]
            self.seen_d[q][id(semT)] = semT.semv
        semT.semv += 16
        ev = (semT, semT.semv)

        def fn(e):
            e.collective_compute(kind, ALU.bypass, replica_groups=groups, ins=[src_ap], outs=[dst_ap]).then_inc(semT.sem, 16)
            return None

        self.ops[q].append(dict(fn=fn, we=we, wd=wd, dma=True))
        src.r_d[id(semT)] = ev
        dst.w_e = {}
        dst.w_d = {id(semT): ev}
        dst.r_e = {}
        dst.r_d = {}

    def final_wait(self, tiles, eng="sync"):
        we, wd = self._collect(eng, tiles, [])
        self.ops[eng].append(dict(fn=lambda e: None, we=we, wd=wd, dma=True))

    def finalize(self):
        nc = self.nc
        ms = {e: set() for e in ENGS}
        for e in ENGS:
            for o in self.ops[e]:
                for d, k in o["we"]:
                    ms[d].add(k)
        rank = {}
        for e in ENGS:
            srt = sorted(ms[e])
            rank[e] = {k: i + 1 for i, k in enumerate(srt)}
        for e in ENGS:
            if ms[e]:
                self.esem[e] = self.es.enter_context(nc.semaphore(self._name("e" + e)))
        with nc.Block() as block:
            for e in ENGS:
                ops = self.ops[e]
                if not ops:
                    continue

                def body(eng, e=e, ops=ops):
                    for idx, o in enumerate(ops):
                        for d, k in o["we"]:
                            eng.wait_ge(self.esem[d], rank[d][k])
                        for st, v in o["wd"]:
                            eng.wait_ge(st.sem, v)
                        inst = o["fn"](eng)
                        if not o["dma"] and idx in ms[e]:
                            assert inst is not None, "milestone op must return instruction"
                            if e == "tensor" and PE_DRAIN:
                                eng.drain().then_inc(self.esem[e], 1)
                            else:
                                inst.then_inc(self.esem[e], 1)

                getattr(block, e)(body)
        self.es.close()
        return nc

import math, os
DBG = os.environ.get('KDBG', '')

D = 1024
NT = 18
NR = NT * 128
HC = 2312
EPS = 1e-6
IN_COLS = 7904


def tile_col(i):
    return 2 + 128 * i if i < 2 else 262 + 128 * (i - 2)


class Ring:
    def __init__(self, P, name, n, shape, dtype=F32, psum=False):
        mk = P.psum if psum else P.sbuf
        self.b = [mk(f"{name}{i}", shape, dtype) for i in range(n)]
        self.i = 0

    def next(self):
        t = self.b[self.i % len(self.b)]
        self.i += 1
        return t


def gnorm(P, src, srcT, G, W, gain, dst, dstT, sq, ss, lnscale=0.0, eng="vector"):
    P.op("scalar", lambda e: e.activation(out=sq[:, 0:G * W].rearrange("p (g w) -> p g w", g=G), in_=src, func=AF.Square),
         [srcT], [sq])
    P.op("vector", lambda e: e.tensor_reduce(out=ss[:, 0:G], in_=sq[:, 0:G * W].rearrange("p (g w) -> p g w", g=G),
                                             axis=AX.X, op=ALU.add), [sq], [ss])
    P.op("scalar", lambda e: e.activation(out=ss[:, 0:G], in_=ss[:, 0:G], func=AF.Ln, scale=1.0 / W, bias=P.c_eps[:, 0:1]), [ss, P.c_epsT], [ss])
    P.op("scalar", lambda e: e.activation(out=ss[:, 0:G], in_=ss[:, 0:G], func=AF.Exp, scale=-0.5, bias=P.cbias(lnscale)), [ss, P.c_epsT], [ss])
    for g in range(G):
        P.op(eng, lambda e, g=g: e.scalar_tensor_tensor(out=dst[:, g, :], in0=src[:, g, :], scalar=ss[:, g:g + 1], in1=gain,
                                                         op0=ALU.mult, op1=ALU.mult), [srcT, ss, P.gainsT], [dstT])


def rope(P, xn, xnT, G, W, cos, sin, tabT, dst, dstT, tmp, tmpT):
    nf = W // 4
    if G == 1:
        sp = lambda ap: ap.rearrange("p g (a b f) -> p (g a) b f", a=2, b=2)
        c = cos.rearrange("p (a f) -> p a f", a=2)
        s = sin.rearrange("p (a f) -> p a f", a=2)
        x1, x2 = sp(xn)[:, :, 0, :], sp(xn)[:, :, 1, :]
        d1, d2 = sp(dst)[:, :, 0, :], sp(dst)[:, :, 1, :]
        t = tmp[:, 0:4 * 2 * nf].rearrange("p (k a f) -> p k a f", k=4, a=2)
    else:
        sp = lambda ap: ap.rearrange("p g (a b f) -> p g a b f", a=2, b=2)
        c = cos.rearrange("p (a f) -> p a f", a=2).unsqueeze(1).to_broadcast([128, G, 2, nf])
        s = sin.rearrange("p (a f) -> p a f", a=2).unsqueeze(1).to_broadcast([128, G, 2, nf])
        x1, x2 = sp(xn)[:, :, :, 0, :], sp(xn)[:, :, :, 1, :]
        d1, d2 = sp(dst)[:, :, :, 0, :], sp(dst)[:, :, :, 1, :]
        t = tmp[:, 0:4 * G * 2 * nf].rearrange("p (k g a f) -> p k g a f", k=4, g=G, a=2)
    E2 = "vector"
    P.op("vector", lambda e: e.tensor_tensor(out=t[:, 0], in0=x1, in1=c, op=ALU.mult), [xnT, tabT], [tmpT])
    P.op("vector", lambda e: e.tensor_tensor(out=t[:, 1], in0=x2, in1=s, op=ALU.mult), [xnT, tabT], [tmpT])
    P.op("vector", lambda e: e.tensor_tensor(out=d1, in0=t[:, 0], in1=t[:, 1], op=ALU.subtract), [tmpT], [dstT])
    P.op(E2, lambda e: e.tensor_tensor(out=t[:, 2], in0=x2, in1=c, op=ALU.mult), [xnT, tabT], [tmpT])
    P.op(E2, lambda e: e.tensor_tensor(out=t[:, 3], in0=x1, in1=s, op=ALU.mult), [xnT, tabT], [tmpT])
    P.op(E2, lambda e: e.tensor_tensor(out=d2, in0=t[:, 2], in1=t[:, 3], op=ALU.add), [tmpT], [dstT])


def build_LA(lvl=99, blocks=None):
    P = Prog()
    def fin():
        P.final_wait(outs)
        return P.finalize()
    din = lambda n, s, dt=F32: P.dram(n, s, dt, kind="ExternalInput")
    dout = lambda n, s, dt=F32: P.dram(n, s, dt, kind="ExternalOutput")
    xin = din("xin", [HC, D])
    cT = din("cT", [128, 8, 2])
    w_mod = din("w_mod", [D, 6 * D])
    bmodT = din("bmodT", [128, 48])
    g1T = din("g1T", [128, 8])
    hmask = din("hmask", [128, 4])
    w_in = din("w_in", [D, IN_COLS])
    cwb = din("cwb", [128, 5, 1536])
    cbrow = din("cbrow", [1, 1536])
    cbT = din("cbT", [128, 4])
    dtb = din("dtb", [128, 32])
    ropetab = din("ropetab", [NR, 192])
    gains = din("gains", [128, 1408])
    w_uq = din("w_uq", [384, 1536])
    w_ukv = din("w_ukv", [256, 2048])
    identf_d = din("identf", [128, 128])

    o_mod = dout("o_mod", [128, 48, 2])
    o_xs = dout("o_xs", [NR, 1024])
    o_btm = dout("o_btm", [NR, 256], BF16)
    o_bT = dout("o_bT", [2, 128, NR], BF16)
    o_cT = dout("o_cT", [2, 128, NR], BF16)
    o_dt = dout("o_dt", [NR, 32])
    o_zs = dout("o_zs", [NR, 1024])
    o_gates = dout("o_gates", [NR, 3072])
    o_skT = dout("o_skT", [2, 128, NR], BF16)
    o_sv = dout("o_sv", [NR, 256], BF16)
    o_sqT = dout("o_sqT", [8, 128, NR], BF16)
    o_mkT = dout("o_mkT", [8, 128, NR], BF16)
    o_mkrT = dout("o_mkrT", [128, NR], BF16)
    o_mv = dout("o_mv", [NR, 1024], BF16)
    o_mqT = dout("o_mqT", [8, 128, NR], BF16)
    o_mqrT = dout("o_mqrT", [4, 128, NR], BF16)
    outs = [o_mod, o_xs, o_btm, o_bT, o_cT, o_dt, o_zs, o_gates, o_skT, o_sv, o_sqT, o_mkT, o_mkrT, o_mv, o_mqT, o_mqrT]

    identf = P.sbuf("identf", [128, 128], F32)
    P.dma(identf, identf[:], identf_d, identf_d[:, :])
    identb = P.sbuf("identb", [128, 128], BF16)
    P.op("vector", lambda e: e.tensor_copy(out=identb[:], in_=identf[:]), [identf], [identb])
    ones_row = P.sbuf("ones_row", [1, 128], BF16)
    P.op("vector", lambda e: e.memset(ones_row[:], 1.0), [], [ones_row])
    cconst = P.sbuf("cconst", [128, 8], F32)
    P.c_epsT = cconst
    P.c_eps = cconst
    P.op("vector", lambda e: e.memset(cconst[:, 0:1], EPS), [], [cconst])
    P.op("vector", lambda e: e.memset(cconst[:, 1:2], 0.0), [], [cconst])
    P.op("vector", lambda e: e.memset(cconst[:, 2:3], math.log(128 ** -0.5)), [], [cconst])
    P.op("vector", lambda e: e.memset(cconst[:, 3:4], math.log(192 ** -0.5)), [], [cconst])
    P.op("vector", lambda e: e.memset(cconst[:, 4:5], 1.0), [], [cconst])
    LN_SWA, LN_MLA = math.log(128 ** -0.5), math.log(192 ** -0.5)

    def cbias(v):
        if v == 0.0:
            return cconst[:, 1:2]
        if v == LN_SWA:
            return cconst[:, 2:3]
        if v == LN_MLA:
            return cconst[:, 3:4]
        raise ValueError
    P.cbias = cbias

    gains_s = P.sbuf("gains_s", [128, 1408], F32)
    P.gainsT = gains_s
    P.dma(gains_s, gains_s[:], gains, gains[:, :])
    G_SQ, G_SK, G_QL, G_KVL, G_MQ, G_MK = 0, 128, 256, 640, 896, 1088
    dtb_s = P.sbuf("dtb_s", [128, 32], F32)
    P.dma(dtb_s, dtb_s[:], dtb, dtb[:, :])
    cbT_s = P.sbuf("cbT_s", [128, 4], F32)
    P.dma(cbT_s, cbT_s[:], cbT, cbT[:, :])
    cbrow_f = P.sbuf("cbrow_f", [1, 1536], F32)
    P.dma(cbrow_f, cbrow_f[:], cbrow, cbrow[:, :])
    cbrow_b = P.sbuf("cbrow_b", [1, 1536], BF16)
    P.op("vector", lambda e: e.tensor_copy(out=cbrow_b[:], in_=cbrow_f[:]), [cbrow_f], [cbrow_b])
    wuq_s = P.sbuf("wuq_s", [128, 3, 1536], BF16)
    P.dma(wuq_s, wuq_s[:], w_uq, w_uq.ap.rearrange("(kt p) n -> p kt n", p=128), q="gpsimd")
    wukv_s = P.sbuf("wukv_s", [128, 2, 2048], BF16)
    P.dma(wukv_s, wukv_s[:], w_ukv, w_ukv.ap.rearrange("(kt p) n -> p kt n", p=128), q="gpsimd")

    cT_s = P.sbuf("cT_s", [128, 8, 2], F32)
    P.dma(cT_s, cT_s[:], cT, cT[:, :, :])
    sc_s = P.sbuf("sc_s", [128, 8, 2], F32)
    P.op("scalar", lambda e: e.activation(out=sc_s[:], in_=cT_s[:], func=AF.Silu), [cT_s], [sc_s])
    bmod_s = P.sbuf("bmod_s", [128, 48], F32)
    P.dma(bmod_s, bmod_s[:], bmodT, bmodT[:, :])
    g1_s = P.sbuf("g1_s", [128, 8], F32)
    P.dma(g1_s, g1_s[:], g1T, g1T[:, :])
    modT = P.sbuf("modT", [128, 48, 2], F32)
    wm_ring = Ring(P, "wm", 2, [128, 8, 256], F32)
    pmod_full = P.psum("pmod", [128, 512], F32)
    pmod = pmod_full
    for cb in range(24):
        wm = wm_ring.next()
        P.dma(wm, wm[:], w_mod, w_mod.ap.rearrange("(kt p) n -> p kt n", p=128)[:, :, cb * 256:(cb + 1) * 256])
        for ct in range(2):
            for kt in range(8):
                P.op("tensor", lambda e, wm=wm, ct=ct, kt=kt: e.matmul(out=pmod[:, ct * 2:ct * 2 + 2], lhsT=wm[:, kt, ct * 128:(ct + 1) * 128],
                                                                       rhs=sc_s[:, kt, :], start=(kt == 0), stop=(kt == 7)),
                     [wm, sc_s], [pmod])
        P.op("vector", lambda e, cb=cb: e.tensor_tensor(out=modT[:, cb * 2:(cb + 1) * 2, :], in0=pmod[:, 0:4].rearrange("p (c t) -> p c t", c=2),
                                                        in1=bmod_s[:, cb * 2:(cb + 1) * 2].unsqueeze(2).to_broadcast([128, 2, 2]),
                                                        op=ALU.add), [pmod, bmod_s], [modT])
    P.dma(o_mod, o_mod[:, :, :], modT, modT[:])
    if lvl < 1:
        return fin()
    A1 = P.sbuf("A1", [128, 8, 2], F32)
    P.op("vector", lambda e: e.tensor_scalar(out=A1[:], in0=modT[:, 8:16, :], scalar1=1.0, scalar2=None, op0=ALU.add), [modT], [A1])
    P.op("vector", lambda e: e.tensor_tensor(out=A1[:], in0=A1[:], in1=g1_s[:].unsqueeze(2).to_broadcast([128, 8, 2]), op=ALU.mult),
         [A1, g1_s], [A1])

    hT = P.sbuf("hT", [128, 8, HC], BF16)
    hTv = [P.view(f"hTv{i}", None) for i in range(NT + 1)]
    xring = Ring(P, "xt", 2, [128, D], F32)
    sq = P.sbuf("sq", [128, 1024], F32)
    ss = P.sbuf("ss", [128, 8], F32)
    xn = P.sbuf("xn", [128, D], F32)
    ptr = Ring(P, "ptr", 2, [128, 4, 128], F32, psum=True)

    def make_hT(i, nrow, rows, cdst, col):
        xt = xring.next()
        pairs = []
        p0 = 0
        for (r, n) in rows:
            pairs.append((xt[p0:p0 + n, :], xin[r:r + n, :]))
            p0 += n
        P.dma_group(pairs, xt, xin)
        P.op("scalar", lambda e: e.activation(out=sq[0:nrow, :], in_=xt[0:nrow, :], func=AF.Square, accum_out=ss[0:nrow, 0:1]), [xt], [sq, ss])
        P.op("scalar", lambda e: e.activation(out=ss[0:nrow, 0:1], in_=ss[0:nrow, 0:1], func=AF.Ln, scale=1.0 / D, bias=cconst[0:nrow, 0:1]), [ss, cconst], [ss])
        P.op("scalar", lambda e: e.activation(out=ss[0:nrow, 0:1], in_=ss[0:nrow, 0:1], func=AF.Exp, scale=-0.5, bias=cconst[0:nrow, 1:2]), [ss, cconst], [ss])
        P.op("vector", lambda e: e.tensor_scalar(out=xn[0:nrow, :], in0=xt[0:nrow, :], scalar1=ss[0:nrow, 0:1], scalar2=None, op0=ALU.mult), [xt, ss], [xn])
        for half in range(2):
            pt = ptr.next()
            for j in range(4):
                kt = half * 4 + j
                P.op("tensor", lambda e, kt=kt, j=j, pt=pt: e.transpose(out=pt[:, j, 0:nrow], in_=xn[0:nrow, kt * 128:(kt + 1) * 128],
                                                                        identity=identf[0:nrow, 0:nrow]), [xn, identf], [pt])
            for j in range(4):
                kt = half * 4 + j
                for (cc, pp, n) in cdst:
                    P.op("scalar", lambda e, kt=kt, j=j, pt=pt, cc=cc, pp=pp, n=n: e.activation(
                        out=hT[:, kt, cc:cc + n], in_=pt[:, j, pp:pp + n], func=AF.Identity,
                        scale=A1[:, kt, col:col + 1], bias=modT[:, kt, col:col + 1]), [pt, A1, modT], [hTv[i]])

    for i in range(NT):
        make_hT(i, 128, [(tile_col(i), 128)], [(tile_col(i), 0, 128)], 1 if i < 2 else 0)
    make_hT(NT, 4, [(260, 2), (2310, 2)], [(260, 0, 2), (2310, 2, 2)], 0)
    hm_s = P.sbuf("hm_s", [128, 4], F32)
    P.dma(hm_s, hm_s[:], hmask, hmask[:, :])
    for kt in range(8):
        for k, cc in enumerate([260, 2310]):
            P.op("vector", lambda e, kt=kt, k=k, cc=cc: e.tensor_tensor(out=hT[:, kt, cc:cc + 2], in0=hT[:, kt, cc:cc + 2],
                                                                        in1=hm_s[:, 2 * k:2 * k + 2], op=ALU.mult), [hTv[NT], hm_s], [hTv[NT]])
    for cc in (0, 258):
        P.op("vector", lambda e, cc=cc: e.memset(hT[:, :, cc:cc + 2], 0.0), [], [hTv[NT]])
    hT_all = hTv

    if lvl < 2:
        return fin()
    wring = Ring(P, "wb", 2, [128, 8, 512], BF16)
    wkring = Ring(P, "wk", 1, [128, 5, 8, 512], BF16)
    cwring = Ring(P, "cw", 1, [128, 5, 512], F32)
    pacc = Ring(P, "pacc", 3, [128, 512], F32, psum=True)
    ptb = Ring(P, "ptb", 1, [128, 8, 128], BF16, psum=True)
    pup = Ring(P, "pup", 1, [128, 512], F32, psum=True)
    st_f = Ring(P, "stf", 3, [128, 512], F32)
    st_b = Ring(P, "stb", 3, [128, 512], BF16)
    st_T = Ring(P, "stT", 3, [128, 8, 128], BF16)
    nrm = Ring(P, "nrm", 2, [128, 512], F32)
    lat_T = Ring(P, "latT", 2, [128, 3, 128], BF16)
    nrb = Ring(P, "nrb", 2, [128, 512], BF16)
    rtmp = P.sbuf("rtmp", [128, 1024], F32)
    tab_ring = Ring(P, "tab", 2, [128, 192], F32)
    w_in_v = w_in.ap.rearrange("(kt p) n -> p kt n", p=128)

    def load_w(col0, n):
        wb = wring.next()
        P.dma(wb, wb[:, :, 0:n], w_in, w_in_v[:, :, col0:col0 + n], q="gpsimd")
        return wb

    def load_wk(col0, n):
        wb = load_w(col0, n)
        cw = cwring.next()
        P.dma(cw, cw[:, :, 0:n], cwb, cwb[:, :, col0:col0 + n])
        wk = wkring.next()
        for k in range(5):
            P.op("gpsimd", lambda e, k=k, wk=wk, wb=wb, cw=cw: e.tensor_tensor(
                out=wk[:, k, :, 0:n], in0=wb[:, :, 0:n], in1=cw[:, k, 0:n].unsqueeze(1).to_broadcast([128, 8, n]), op=ALU.mult),
                [wb, cw], [wk])
        return wk

    def tm_matmul(i, ps, n, wb=None, wk=None, bias_col0=None):
        c0 = tile_col(i)
        rd = [hT_all[i]]
        if wk is not None:
            rd = [hT_all[t] for t in range(max(i - 1, 0), min(i + 2, NT))] + [hT_all[NT]]
        if wk is None:
            for kt in range(8):
                P.op("tensor", lambda e, kt=kt: e.matmul(out=ps[:, 0:n], lhsT=hT[:, kt, c0:c0 + 128], rhs=wb[:, kt, 0:n],
                                                         start=(kt == 0), stop=(kt == 7)), rd + [wb], [ps])
        else:
            first = True
            for k in range(5):
                for kt in range(8):
                    P.op("tensor", lambda e, k=k, kt=kt, first=first: e.matmul(out=ps[:, 0:n], lhsT=hT[:, kt, c0 + k - 2:c0 + k - 2 + 128],
                                                                              rhs=wk[:, k, kt, 0:n], start=first, stop=False), rd + [wk], [ps])
                    first = False
            P.op("tensor", lambda e: e.matmul(out=ps[:, 0:n], lhsT=ones_row[0:1, :], rhs=cbrow_b[0:1, bias_col0:bias_col0 + n],
                                              start=False, stop=True), [ones_row, cbrow_b], [ps])

    def store(dst, dst_ap, st, st_ap):
        P.dma(dst, dst_ap, st, st_ap)

    def act_store(ps_ap, psT, n, func, dst, r0, c0, bf=False):
        st = (st_b if bf else st_f).next()
        P.op("scalar", lambda e: e.activation(out=st[:, 0:n], in_=ps_ap, func=func), [psT], [st])
        store(dst, dst[r0:r0 + 128, c0:c0 + n], st, st[:, 0:n])

    def transpose_store(src, srcT, G, dst, g0, r0, rows=128):
        pt = ptb.next()
        for g in range(G):
            P.op("tensor", lambda e, g=g: e.transpose(out=pt[:, g, :], in_=src[:, g * 128:(g + 1) * 128], identity=identb[:]), [srcT, identb], [pt])
        st = st_T.next()
        P.op("vector", lambda e: e.tensor_copy(out=st[:, 0:G, :], in_=pt[:, 0:G, :]), [pt], [st])
        if len(dst.ap.shape) == 3:
            store(dst, dst.ap[g0:g0 + G, :, r0:r0 + 128].rearrange("g d t -> d g t"), st, st[:, 0:G, :])
        else:
            store(dst, dst.ap[:, r0:r0 + 128], st, st[:, 0, :])

    def load_tab(i):
        tb = tab_ring.next()
        P.dma(tb, tb[:], ropetab, ropetab[i * 128:(i + 1) * 128, :])
        return tb

    def _blk2():
        for blk in range(2):
            wk = load_wk(blk * 512, 512)
            for i in range(NT):
                ps = pacc.next()
                tm_matmul(i, ps, 512, wk=wk, bias_col0=blk * 512)
                act_store(ps[:, 0:512], ps, 512, AF.Silu, o_xs, i * 128, blk * 512)
    def _blk3():
        wk = load_wk(1024, 512)
        for i in range(NT):
            ps = pacc.next()
            tm_matmul(i, ps, 256, wk=wk, bias_col0=1024)
            act_store(ps[:, 0:256], ps, 256, AF.Silu, o_btm, i * 128, 0, bf=True)
        groups = [(0, 2)] + [(2 + 4 * j, 4) for j in range(4)]
        for ct in range(4):
            dst = o_bT if ct < 2 else o_cT
            for (t0, nt) in groups:
                c0 = tile_col(t0)
                n = nt * 128
                ps = pacc.next()
                rd = [hT_all[t] for t in range(max(t0 - 1, 0), min(t0 + nt + 1, NT))] + [hT_all[NT], wk]
                first = True
                for k in range(5):
                    for kt in range(8):
                        P.op("tensor", lambda e, k=k, kt=kt, first=first, ps=ps, c0=c0, n=n, ct=ct: e.matmul(
                            out=ps[:, 0:n], lhsT=wk[:, k, kt, ct * 128:(ct + 1) * 128], rhs=hT[:, kt, c0 + k - 2:c0 + k - 2 + n],
                            start=first, stop=(k == 4 and kt == 7)), rd, [ps])
                        first = False
                st = st_b.next()
                P.op("scalar", lambda e, ps=ps, n=n, st=st, ct=ct: e.activation(out=st[:, 0:n], in_=ps[:, 0:n], func=AF.Silu, bias=cbT_s[:, ct:ct + 1]),
                     [ps, cbT_s], [st])
                store(dst, dst.ap[ct % 2, :, t0 * 128:t0 * 128 + n], st, st[:, 0:n])
    def _blk4():
        wb = load_w(1536, 32)
        for i in range(NT):
            ps = pacc.next()
            tm_matmul(i, ps, 32, wb=wb)
            st = st_f.next()
            P.op("vector", lambda e, ps=ps, st=st: e.tensor_tensor(out=st[:, 0:32], in0=ps[:, 0:32], in1=dtb_s[:], op=ALU.add), [ps, dtb_s], [st])
            P.op("scalar", lambda e, st=st: e.activation(out=st[:, 0:32], in_=st[:, 0:32], func=AF.Exp), [st], [st])
            P.op("scalar", lambda e, st=st: e.activation(out=st[:, 0:32], in_=st[:, 0:32], func=AF.Ln, bias=cconst[:, 4:5]), [st, cconst], [st])
            store(o_dt, o_dt[i * 128:(i + 1) * 128, :], st, st[:, 0:32])
    def _blk5():
        wb = load_w(1568, 512)
        for i in range(NT):
            ps = pacc.next()
            tm_matmul(i, ps, 512, wb=wb)
            tb = load_tab(i)
            xnf = nrm.next()
            gnorm(P, ps[:, 0:256].rearrange("p (g w) -> p g w", g=2), ps, 2, 128, gains_s[:, G_SK:G_SK + 128],
                  xnf[:, 0:256].rearrange("p (g w) -> p g w", g=2), xnf, sq, ss)
            xb = nrb.next()
            rope(P, xnf[:, 0:256].rearrange("p (g w) -> p g w", g=2), xnf, 2, 128, tb[:, 0:64], tb[:, 64:128], tb,
                 xb[:, 0:256].rearrange("p (g w) -> p g w", g=2), xb, rtmp, rtmp)
            if 'noT' not in DBG:
                transpose_store(xb, xb, 2, o_skT, 0, i * 128)
            if 'noV' not in DBG:
                act_store(ps[:, 256:512], ps, 256, AF.Identity, o_sv, i * 128, 0, bf=True)
    def _blk6():
        wb = load_w(2080, 256)
        for i in range(NT):
            ps = pacc.next()
            tm_matmul(i, ps, 256, wb=wb)
            cb16 = nrb.next()
            gnorm(P, ps[:, 0:256].rearrange("p (g w) -> p g w", g=1), ps, 1, 256, gains_s[:, G_KVL:G_KVL + 256],
                  cb16[:, 0:256].rearrange("p (g w) -> p g w", g=1), cb16, sq, ss)
            pt = ptb.next()
            for g in range(2):
                P.op("tensor", lambda e, g=g, pt=pt, cb16=cb16: e.transpose(out=pt[:, g, :], in_=cb16[:, g * 128:(g + 1) * 128], identity=identb[:]),
                     [cb16, identb], [pt])
            ckT = lat_T.next()
            P.op("vector", lambda e, pt=pt, ckT=ckT: e.tensor_copy(out=ckT[:, 0:2, :], in_=pt[:, 0:2, :]), [pt], [ckT])
            for hb in (range(4) if 'no6a' not in DBG else []):
                pu = pup.next()
                for kt in range(2):
                    P.op("tensor", lambda e, kt=kt, pu=pu, hb=hb, ckT=ckT: e.matmul(out=pu[:], lhsT=ckT[:, kt, :], rhs=wukv_s[:, kt, hb * 512:(hb + 1) * 512],
                                                                                   start=(kt == 0), stop=(kt == 1)), [ckT, wukv_s], [pu])
                kb = nrb.next()
                gnorm(P, pu[:].rearrange("p (g w) -> p g w", g=2)[:, :, 0:128], pu, 2, 128, gains_s[:, G_MK:G_MK + 128],
                      kb[:, 0:256].rearrange("p (g w) -> p g w", g=2), kb, sq, ss)
                transpose_store(kb, kb, 2, o_mkT, hb * 2, i * 128)
                st = st_b.next()
                P.op("scalar", lambda e, st=st, pu=pu: e.activation(out=st[:, 0:256].rearrange("p (g w) -> p g w", g=2),
                                                                     in_=pu[:].rearrange("p (g w) -> p g w", g=2)[:, :, 128:256], func=AF.Identity), [pu], [st])
                store(o_mv, o_mv[i * 128:(i + 1) * 128, hb * 256:(hb + 1) * 256], st, st[:, 0:256])
        wb2 = load_w(2336, 64)
        for i in range(NT):
            ps = pacc.next()
            tm_matmul(i, ps, 64, wb=wb2)
            tb = load_tab(i)
            krf = nrm.next()
            gnorm(P, ps[:, 0:64].rearrange("p (g w) -> p g w", g=1), ps, 1, 64, gains_s[:, G_MK + 128:G_MK + 192],
                  krf[:, 0:64].rearrange("p (g w) -> p g w", g=1), krf, sq, ss)
            krb = nrb.next()
            rope(P, krf[:, 0:64].rearrange("p (g w) -> p g w", g=1), krf, 1, 64, tb[:, 128:160], tb[:, 160:192], tb,
                 krb[:, 0:64].rearrange("p (g w) -> p g w", g=1), krb, rtmp, rtmp)
            P.op("vector", lambda e, krb=krb: e.tensor_copy(out=krb[:, 64:128], in_=krb[:, 0:64]), [krb], [krb])
            transpose_store(krb, krb, 1, o_mkrT, 0, i * 128)
    def _blk7():
        for blk in range(2):
            wb = load_w(2400 + blk * 512, 512)
            for i in range(NT):
                ps = pacc.next()
                tm_matmul(i, ps, 512, wb=wb)
                act_store(ps[:, 0:512], ps, 512, AF.Silu, o_zs, i * 128, blk * 512)
    def _blk8():
        for blk in range(2):
            wb = load_w(3424 + blk * 512, 512)
            for i in range(NT):
                ps = pacc.next()
                tm_matmul(i, ps, 512, wb=wb)
                tb = load_tab(i)
                xnf = nrm.next()
                gnorm(P, ps[:].rearrange("p (g w) -> p g w", g=4), ps, 4, 128, gains_s[:, G_SQ:G_SQ + 128],
                      xnf[:].rearrange("p (g w) -> p g w", g=4), xnf, sq, ss, lnscale=LN_SWA)
                xb = nrb.next()
                rope(P, xnf[:].rearrange("p (g w) -> p g w", g=4), xnf, 4, 128, tb[:, 0:64], tb[:, 64:128], tb,
                     xb[:].rearrange("p (g w) -> p g w", g=4), xb, rtmp, rtmp)
                transpose_store(xb, xb, 4, o_sqT, blk * 4, i * 128)
    def _blk9():
        wb = load_w(4448, 384)
        for i in range(NT):
            ps = pacc.next()
            tm_matmul(i, ps, 384, wb=wb)
            tb = load_tab(i)
            cb16 = nrb.next()
            gnorm(P, ps[:, 0:384].rearrange("p (g w) -> p g w", g=1), ps, 1, 384, gains_s[:, G_QL:G_QL + 384],
                  cb16[:, 0:384].rearrange("p (g w) -> p g w", g=1), cb16, sq, ss)
            pt = ptb.next()
            for g in range(3):
                P.op("tensor", lambda e, g=g, pt=pt, cb16=cb16: e.transpose(out=pt[:, g, :], in_=cb16[:, g * 128:(g + 1) * 128], identity=identb[:]),
                     [cb16, identb], [pt])
            cqT = lat_T.next()
            P.op("vector", lambda e, pt=pt, cqT=cqT: e.tensor_copy(out=cqT[:, 0:3, :], in_=pt[:, 0:3, :]), [pt], [cqT])
            for hb in range(4):
                pu = pup.next()
                for kt in range(3):
                    P.op("tensor", lambda e, kt=kt, pu=pu, hb=hb, cqT=cqT: e.matmul(out=pu[:, 0:384], lhsT=cqT[:, kt, :], rhs=wuq_s[:, kt, hb * 384:(hb + 1) * 384],
                                                                                   start=(kt == 0), stop=(kt == 2)), [cqT, wuq_s], [pu])
                puv = pu[:, 0:384].rearrange("p (g w) -> p g w", g=2)
                qnb = nrb.next()
                gnorm(P, puv[:, :, 0:128], pu, 2, 128, gains_s[:, G_MQ:G_MQ + 128],
                      qnb[:, 0:256].rearrange("p (g w) -> p g w", g=2), qnb, sq, ss, lnscale=LN_MLA)
                transpose_store(qnb, qnb, 2, o_mqT, hb * 2, i * 128)
                qrf = nrm.next()
                gnorm(P, puv[:, :, 128:192], pu, 2, 64, gains_s[:, G_MQ + 128:G_MQ + 192],
                      qrf[:, 0:128].rearrange("p (g w) -> p g w", g=2), qrf, sq, ss, lnscale=LN_MLA)
                qrb = nrb.next()
                rope(P, qrf[:, 0:128].rearrange("p (g w) -> p g w", g=2), qrf, 2, 64, tb[:, 128:160], tb[:, 160:192], tb,
                     qrb[:, 0:128].rearrange("p (g w) -> p g w", g=2), qrb, rtmp, rtmp)
                transpose_store(qrb, qrb, 1, o_mqrT, hb, i * 128)
    def _blk10():
        for blk in range(6):
            wb = load_w(4832 + blk * 512, 512)
            for i in range(NT):
                ps = pacc.next()
                tm_matmul(i, ps, 512, wb=wb)
                act_store(ps[:, 0:512], ps, 512, AF.Sigmoid, o_gates, i * 128, blk * 512)

    for _k, _f in [(2, _blk2), (3, _blk3), (4, _blk4), (5, _blk5), (6, _blk6), (7, _blk7), (8, _blk8), (9, _blk9), (10, _blk10)]:
        if (blocks is None and lvl >= _k) or (blocks is not None and _k in blocks):
            _f()
    P.final_wait(outs)
    return P.finalize()

import math

NEG = -30000.0


def build_LS():
    P = Prog()
    din = lambda n, s, dt=F32: P.dram(n, s, dt, kind="ExternalInput")
    dout = lambda n, s, dt=F32: P.dram(n, s, dt, kind="ExternalOutput")
    xs = din("xs", [NR, 1024])
    btm = din("btm", [NR, 256], BF16)
    bT = din("bT", [2, 128, NR], BF16)
    cT = din("cT", [2, 128, NR], BF16)
    dt = din("dt", [NR, 32])
    alog = din("alog", [128, 32])
    dskip = din("dskip", [128, 16])
    cst = din("cst", [128, 5, 128])
    nmask = din("nmask", [128, 2, 128])

    o_y = dout("o_y", [NR, 1024])
    o_cum = dout("o_cum", [NR, 32])
    o_F = dout("o_F", [4, 128, 1024])
    o_sumA = dout("o_sumA", [128, 4, 16])
    outs = [o_y, o_cum, o_F, o_sumA]

    cst_s = P.sbuf("cst_s", [128, 5, 128], F32)
    P.dma(cst_s, cst_s[:], cst, cst[:, :, :])
    U, L, ONES, IDF = cst_s[:, 0, :], cst_s[:, 1, :], cst_s[:, 2, :], cst_s[:, 3, :]
    nm_s = P.sbuf("nm_s", [128, 2, 128], F32)
    P.dma(nm_s, nm_s[:], nmask, nmask[:, :, :])
    a_s = P.sbuf("a_s", [128, 32], F32)
    P.dma(a_s, a_s[:], alog, alog[:, :])
    P.op("scalar", lambda e: e.activation(out=a_s[:], in_=a_s[:], func=AF.Exp), [a_s], [a_s])
    P.op("vector", lambda e: e.tensor_scalar(out=a_s[:], in0=a_s[:], scalar1=-1.0, scalar2=None, op0=ALU.mult), [a_s], [a_s])
    dsk_s = P.sbuf("dsk_s", [128, 16], F32)
    P.dma(dsk_s, dsk_s[:], dskip, dskip[:, :])
    onesb = P.sbuf("onesb", [128, 128], BF16)
    P.op("vector", lambda e: e.memset(onesb[:], 1.0), [], [onesb])

    yall = P.sbuf("yall", [128, NT, 1024], F32)
    yv = [P.view(f"yv{i}", None) for i in range(NT)]
    cum_all = P.sbuf("cum_all", [128, NT, 32], F32)
    cumv = [P.view(f"cumv{i}", None) for i in range(NT)]

    xring = Ring(P, "xs", 2, [128, 1024], F32)
    bring = Ring(P, "btm", 2, [128, 256], BF16)
    btring = Ring(P, "bT", 2, [128, 2, 128], BF16)
    ctring = Ring(P, "cT", 2, [128, 2, 128], BF16)
    dtring = Ring(P, "dt", 2, [128, 32], F32)

    dA = P.sbuf("dA", [128, 16], F32)
    acs = P.sbuf("acs", [128, 16], F32)
    tot = P.sbuf("tot", [128, 16], F32)
    run = P.sbuf("run", [128, 16], F32)
    dend = P.sbuf("dend", [128, 16], F32)
    eacs = P.sbuf("eacs", [128, 16], F32)
    cd = P.sbuf("cd", [128, 16], F32)
    R1 = P.sbuf("R1", [128, 16, 128], F32)
    R2 = P.sbuf("R2", [128, 16, 128], F32)
    xdt = P.sbuf("xdt", [128, 1024], BF16)
    xdtf = P.sbuf("xdtf", [128, 1024], F32)
    xdtd = P.sbuf("xdtd", [128, 1024], BF16)
    cbs = P.sbuf("cbs", [128, 2, 128], F32)
    Eb = Ring(P, "Eb", 2, [128, 512], F32)
    Mb = Ring(P, "Mb", 2, [128, 512], BF16)
    Hin = P.sbuf("Hin", [128, 1024], F32)
    Hbf = P.sbuf("Hbf", [128, 1024], BF16)
    ytmp = P.sbuf("ytmp", [128, 1024], F32)
    htmp = P.sbuf("htmp", [128, 1024], F32)

    psmall = P.psum("psmall", [128, 512], F32)
    parg = Ring(P, "parg", 2, [128, 512], F32, psum=True)
    py = P.psum("py", [128, 1024], F32)
    poff = P.psum("poff", [128, 1024], F32)

    def chunk(ci, d, first):
        r0 = ci * 128
        xt = xring.next(); P.dma(xt, xt[:], xs, xs[r0:r0 + 128, :])
        bt = bring.next(); P.dma(bt, bt[:], btm, btm[r0:r0 + 128, :])
        bTt = btring.next(); P.dma(bTt, bTt[:], bT, bT.ap[:, :, r0:r0 + 128].rearrange("g n t -> n g t"))
        cTt = ctring.next(); P.dma(cTt, cTt[:], cT, cT.ap[:, :, r0:r0 + 128].rearrange("g n t -> n g t"))
        dtt = dtring.next(); P.dma(dtt, dtt[:], dt, dt[r0:r0 + 128, :])
        CM = U if d == 0 else L
        dts = dtt[:, d * 16:(d + 1) * 16]
        if first:
            P.op("vector", lambda e: e.memset(run[:], 0.0), [], [run])
            P.op("vector", lambda e: e.memset(Hin[:], 0.0), [], [Hin])
            P.op("vector", lambda e: e.memset(Hbf[:], 0.0), [], [Hbf])
        P.op("vector", lambda e: e.tensor_tensor(out=dA[:], in0=dts, in1=a_s[:, d * 16:(d + 1) * 16], op=ALU.mult), [dtt, a_s], [dA])
        P.op("tensor", lambda e: e.matmul(out=psmall[:, 0:16], lhsT=CM, rhs=dA[:], start=True, stop=True), [cst_s, dA], [psmall])
        P.op("tensor", lambda e: e.matmul(out=psmall[:, 16:32], lhsT=ONES, rhs=dA[:], start=True, stop=True), [cst_s, dA], [psmall])
        P.op("vector", lambda e: e.tensor_copy(out=acs[:], in_=psmall[:, 0:16]), [psmall], [acs])
        P.op("vector", lambda e: e.tensor_copy(out=tot[:], in_=psmall[:, 16:32]), [psmall], [tot])
        P.op("vector", lambda e: e.tensor_tensor(out=cum_all[:, ci, d * 16:(d + 1) * 16], in0=acs[:], in1=run[:], op=ALU.add), [acs, run], [cumv[ci]])
        P.op("vector", lambda e: e.tensor_tensor(out=run[:], in0=run[:], in1=tot[:], op=ALU.add), [run, tot], [run])
        P.op("vector", lambda e: e.tensor_tensor(out=dend[:], in0=tot[:], in1=acs[:], op=ALU.subtract), [tot, acs], [dend])
        P.op("scalar", lambda e: e.activation(out=dend[:], in_=dend[:], func=AF.Exp), [dend], [dend])
        P.op("scalar", lambda e: e.activation(out=eacs[:], in_=acs[:], func=AF.Exp), [acs], [eacs])
        P.op("scalar", lambda e: e.activation(out=cd[:], in_=tot[:], func=AF.Exp), [tot], [cd])
        x3 = xt[:].rearrange("p (h w) -> p h w", h=16)
        P.op("vector", lambda e: e.tensor_tensor(out=xdtf[:].rearrange("p (h w) -> p h w", h=16), in0=x3,
                                                 in1=dts.unsqueeze(2).to_broadcast([128, 16, 64]), op=ALU.mult), [xt, dtt], [xdtf])
        P.op("gpsimd", lambda e: e.tensor_copy(out=xdt[:], in_=xdtf[:]), [xdtf], [xdt])
        P.op("vector", lambda e: e.tensor_tensor(out=xdtd[:].rearrange("p (h w) -> p h w", h=16), in0=xdtf[:].rearrange("p (h w) -> p h w", h=16),
                                                 in1=dend[:].unsqueeze(2).to_broadcast([128, 16, 64]), op=ALU.mult), [xdtf, dend], [xdtd])
        for g in range(2):
            P.op("tensor", lambda e, g=g: e.matmul(out=psmall[:, 128 + g * 128:256 + g * 128], lhsT=bTt[:, g, :], rhs=cTt[:, g, :], start=True, stop=True),
                 [bTt, cTt], [psmall])
        P.op("scalar", lambda e: e.activation(out=cbs[:], in_=psmall[:, 128:384].rearrange("p (g w) -> p g w", g=2), func=AF.Identity), [psmall], [cbs])
        P.op("gpsimd", lambda e: e.tensor_tensor(out=R1[:], in0=dA[:].unsqueeze(2).to_broadcast([128, 16, 128]),
                                                 in1=CM.unsqueeze(1).to_broadcast([128, 16, 128]), op=ALU.mult), [dA, cst_s], [R1])
        P.op("vector", lambda e: e.tensor_tensor(out=R2[:], in0=nm_s[:, d, :].unsqueeze(1).to_broadcast([128, 16, 128]),
                                                 in1=acs[:].unsqueeze(2).to_broadcast([128, 16, 128]), op=ALU.subtract), [nm_s, acs], [R2])
        if not first:
            for g in range(2):
                P.op("tensor", lambda e, g=g: e.matmul(out=poff[:, g * 512:(g + 1) * 512], lhsT=cTt[:, g, :], rhs=Hbf[:, g * 512:(g + 1) * 512],
                                                       start=True, stop=True), [cTt, Hbf], [poff])
            P.op("vector", lambda e: e.tensor_tensor(out=ytmp[:].rearrange("p (h w) -> p h w", h=16), in0=poff[:].rearrange("p (h w) -> p h w", h=16),
                                                     in1=eacs[:].unsqueeze(2).to_broadcast([128, 16, 64]), op=ALU.mult), [poff, eacs], [ytmp])
        for bk in range(4):
            pa = parg.next()
            P.op("tensor", lambda e, bk=bk, pa=pa: e.matmul(out=pa[:], lhsT=ONES, rhs=R1[:, bk * 4:(bk + 1) * 4, :], start=True, stop=False), [cst_s, R1], [pa])
            P.op("tensor", lambda e, bk=bk, pa=pa: e.matmul(out=pa[:], lhsT=IDF, rhs=R2[:, bk * 4:(bk + 1) * 4, :], start=False, stop=True), [cst_s, R2], [pa])
            Et = Eb.next()
            P.op("scalar", lambda e, pa=pa, Et=Et: e.activation(out=Et[:], in_=pa[:], func=AF.Exp), [pa], [Et])
            Mt = Mb.next()
            g = bk // 2
            P.op("gpsimd", lambda e, Et=Et, Mt=Mt, g=g: e.tensor_tensor(out=Mt[:].rearrange("p (h w) -> p h w", h=4), in0=Et[:].rearrange("p (h w) -> p h w", h=4),
                                                                         in1=cbs[:, g, :].unsqueeze(1).to_broadcast([128, 4, 128]), op=ALU.mult), [Et, cbs], [Mt])
            for hh in range(4):
                h = bk * 4 + hh
                P.op("tensor", lambda e, hh=hh, h=h, Mt=Mt: e.matmul(out=py[:, h * 64:(h + 1) * 64], lhsT=Mt[:, hh * 128:(hh + 1) * 128], rhs=xdt[:, h * 64:(h + 1) * 64],
                                                                    start=True, stop=True), [Mt, xdt], [py])
        if d == 0:
            if first:
                P.op("vector", lambda e: e.tensor_copy(out=yall[:, ci, :], in_=py[:]), [py], [yv[ci]])
            else:
                P.op("vector", lambda e: e.tensor_tensor(out=yall[:, ci, :], in0=py[:], in1=ytmp[:], op=ALU.add), [py, ytmp], [yv[ci]])
        else:
            if not first:
                P.op("gpsimd", lambda e: e.tensor_tensor(out=yall[:, ci, :], in0=yall[:, ci, :], in1=ytmp[:], op=ALU.add), [yv[ci], ytmp], [yv[ci]])
            P.op("vector", lambda e: e.tensor_tensor(out=yall[:, ci, :], in0=py[:], in1=yall[:, ci, :], op=ALU.add), [py, yv[ci]], [yv[ci]])
            P.op("vector", lambda e: e.tensor_tensor(out=ytmp[:].rearrange("p (h w) -> p h w", h=16), in0=x3,
                                                     in1=dsk_s[:].unsqueeze(2).to_broadcast([128, 16, 64]), op=ALU.mult), [xt, dsk_s], [ytmp])
            P.op("vector", lambda e: e.tensor_tensor(out=yall[:, ci, :], in0=yall[:, ci, :], in1=ytmp[:], op=ALU.add), [yv[ci], ytmp], [yv[ci]])
            P.dma(o_y, o_y[r0:r0 + 128, :], yv[ci], yall[:, ci, :], semT=yv[ci])
            P.dma(o_cum, o_cum[r0:r0 + 128, :], cumv[ci], cum_all[:, ci, :], semT=cumv[ci])
        for g in range(2):
            P.op("tensor", lambda e, g=g: e.matmul(out=poff[:, g * 512:(g + 1) * 512], lhsT=bt[:, g * 128:(g + 1) * 128], rhs=xdtd[:, g * 512:(g + 1) * 512],
                                                   start=True, stop=True), [bt, xdtd], [poff])
        if first:
            P.op("vector", lambda e: e.tensor_copy(out=Hin[:], in_=poff[:]), [poff], [Hin])
        else:
            P.op("vector", lambda e: e.tensor_tensor(out=htmp[:].rearrange("p (h w) -> p h w", h=16), in0=Hin[:].rearrange("p (h w) -> p h w", h=16),
                                                     in1=cd[:].unsqueeze(2).to_broadcast([128, 16, 64]), op=ALU.mult), [Hin, cd], [htmp])
            P.op("vector", lambda e: e.tensor_tensor(out=Hin[:], in0=poff[:], in1=htmp[:], op=ALU.add), [poff, htmp], [Hin])
        P.op("scalar", lambda e: e.activation(out=Hbf[:], in_=Hin[:], func=AF.Identity), [Hin], [Hbf])

    def seq(tiles, d, slot):
        order = tiles if d == 0 else tiles[::-1]
        for k, ci in enumerate(order):
            chunk(ci, d, k == 0)
        P.dma(o_F, o_F.ap[slot, :, :], Hin, Hin[:])
        P.dma(o_sumA, o_sumA.ap[:, slot, :], run, run[:])

    seq([0, 1], 0, 0)
    seq(list(range(2, NT)), 0, 2)
    seq([0, 1], 1, 1)
    seq(list(range(2, NT)), 1, 3)
    P.final_wait(outs)
    return P.finalize()

import math

NKB = 66
NSB = 20


def build_LT(do_swa=True, do_mla=True):
    P = Prog()
    din = lambda n, s, dt=F32: P.dram(n, s, dt, kind="ExternalInput")
    dout = lambda n, s, dt=F32: P.dram(n, s, dt, kind="ExternalOutput")
    sqT = din("sqT", [8, 128, NR], BF16)
    skT = din("skT", [2, 128, NSB * 128], BF16)
    sv = din("sv", [NSB * 128, 256], BF16)
    smask = din("smask", [128, 4, 512], BF16)
    sink = din("sink", [128, 8])
    mqT = din("mqT", [8, 128, NR], BF16)
    mqrT = din("mqrT", [4, 128, NR], BF16)
    mkT = din("mkT", [8, 128, NKB * 128], BF16)
    mkrT = din("mkrT", [128, NKB * 128], BF16)
    mv = din("mv", [8, NKB * 128, 128], BF16)
    identf_d = din("identf", [128, 128])
    o_yswaT = dout("o_yswaT", [8, 128, NR], BF16)
    o_ymlaT = dout("o_ymlaT", [8, 128, NR], BF16)
    outs = [o_yswaT, o_ymlaT]

    identf = P.sbuf("identf", [128, 128], F32)
    P.dma(identf, identf[:], identf_d, identf_d[:, :])
    identb = P.sbuf("identb", [128, 128], BF16)
    P.op("vector", lambda e: e.tensor_copy(out=identb[:], in_=identf[:]), [identf], [identb])
    onesb = P.sbuf("onesb", [128, 128], BF16)
    P.op("vector", lambda e: e.memset(onesb[:], 1.0), [], [onesb])

    ps_s = Ring(P, "ps_s", 4, [128, 512], F32, psum=True)
    ps_o = Ring(P, "ps_o", 2, [128, 512], F32, psum=True)
    ps_d = Ring(P, "ps_d", 2, [128, 512], F32, psum=True)
    pT = Ring(P, "pT", 4, [128, 512], BF16)
    rden = Ring(P, "rden", 2, [128, 512], F32)
    yst = Ring(P, "yst", 2, [128, 512], BF16)

    def attend(nq, steps, sink_ap, sinkT, dst, dst_ap):
        po = ps_o.next()
        pd = ps_d.next()
        n = len(steps)

        def pv(pt, v_ap, vT, si):
            P.op("tensor", lambda e: e.matmul(out=po[:, 0:nq], lhsT=v_ap, rhs=pt[:, 0:nq], start=(si == 0), stop=(si == n - 1)), [pt, vT], [po])
            P.op("tensor", lambda e: e.matmul(out=pd[:, 0:nq], lhsT=onesb[:], rhs=pt[:, 0:nq], start=(si == 0), stop=(si == n - 1)), [pt, onesb], [pd])

        pend = []
        LOOK = 2
        for si, (mms, v_ap, vT) in enumerate(steps):
            ps = ps_s.next()
            for mi, (l_ap, r_ap, rd) in enumerate(mms):
                P.op("tensor", lambda e, l_ap=l_ap, r_ap=r_ap, ps=ps, mi=mi, last=(mi == len(mms) - 1): e.matmul(
                    out=ps[:, 0:nq], lhsT=l_ap, rhs=r_ap, start=(mi == 0), stop=last), rd, [ps])
            pt = pT.next()
            P.op("scalar", lambda e, ps=ps, pt=pt: e.activation(out=pt[:, 0:nq], in_=ps[:, 0:nq], func=AF.Exp), [ps], [pt])
            pend.append((pt, v_ap, vT, si))
            if len(pend) > LOOK:
                pv(*pend.pop(0))
        while pend:
            pv(*pend.pop(0))
        rd_ = rden.next()
        if sink_ap is not None:
            P.op("vector", lambda e: e.tensor_tensor(out=rd_[:, 0:nq].rearrange("p (h q) -> p h q", h=4), in0=pd[:, 0:nq].rearrange("p (h q) -> p h q", h=4),
                                                     in1=sink_ap, op=ALU.add), [pd, sinkT], [rd_])
            P.op("vector", lambda e: e.reciprocal(out=rd_[:, 0:nq], in_=rd_[:, 0:nq]), [rd_], [rd_])
        else:
            P.op("vector", lambda e: e.reciprocal(out=rd_[:, 0:nq], in_=pd[:, 0:nq]), [pd], [rd_])
        ys = yst.next()
        P.op("vector", lambda e: e.tensor_tensor(out=ys[:, 0:nq], in0=po[:, 0:nq], in1=rd_[:, 0:nq], op=ALU.mult), [po, rd_], [ys])
        P.dma(dst, dst_ap, ys, ys[:, 0:nq] if len(dst_ap.shape) == 2 else ys[:, 0:nq].rearrange("p (h q) -> p h q", h=4))

    if do_swa:
        sq_s = P.sbuf("sq_s", [128, 8, NR], BF16)
        for h in range(8):
            pass
        P.dma_group([(sq_s[:, h, :], sqT.ap[h, :, :]) for h in range(8)], sq_s, sqT)
        sk_s = P.sbuf("sk_s", [128, 2, NSB * 128], BF16)
        P.dma_group([(sk_s[:, g, :], skT.ap[g, :, :]) for g in range(2)], sk_s, skT)
        sv_s = P.sbuf("sv_s", [128, NSB, 256], BF16)
        P.dma(sv_s, sv_s[:], sv, sv.ap.rearrange("(b p) d -> p b d", p=128))
        sm_s = P.sbuf("sm_s", [128, 4, 512], BF16)
        P.dma(sm_s, sm_s[:], smask, smask[:, :, :])
        esink = P.sbuf("esink", [128, 8], F32)
        P.dma(esink, esink[:], sink, sink[:, :])
        P.op("scalar", lambda e: e.activation(out=esink[:], in_=esink[:], func=AF.Exp), [esink], [esink])
        for i in range(NT):
            for g in range(2):
                q_ap = sq_s[:, g * 4:(g + 1) * 4, i * 128:(i + 1) * 128]
                if i < 2:
                    blks = [(0, None), (1, None)]
                else:
                    t = i - 2
                    blks = [(2 + t, 2 if t == 0 else 0), (3 + t, None), (4 + t, 3 if t == 15 else 1), (0, None), (1, None)]
                steps = []
                for (kb, mk) in blks:
                    mms = [(sk_s[:, g, kb * 128:(kb + 1) * 128], q_ap, [sk_s, sq_s])]
                    if mk is not None:
                        mms.append((identb[:], sm_s[:, mk, :], [identb, sm_s]))
                    steps.append((mms, sv_s[:, kb, g * 128:(g + 1) * 128], sv_s))
                attend(512, steps, esink[:, g * 4:(g + 1) * 4].unsqueeze(2).to_broadcast([128, 4, 128]), esink,
                       o_yswaT, o_yswaT.ap[g * 4:(g + 1) * 4, :, i * 128:(i + 1) * 128].rearrange("h d t -> d h t"))

    if do_mla:
        mq_s = P.sbuf("mq_s", [128, 8, NR], BF16)
        P.dma_group([(mq_s[:, h, :], mqT.ap[h, :, :]) for h in range(8)], mq_s, mqT)
        mqr_s = P.sbuf("mqr_s", [128, 4, NR], BF16)
        P.dma_group([(mqr_s[:, h, :], mqrT.ap[h, :, :]) for h in range(4)], mqr_s, mqrT)
        kr_s = P.sbuf("kr_s", [128, NKB * 128], BF16)
        P.dma(kr_s, kr_s[:], mkrT, mkrT[:, :])
        kring = Ring(P, "mk", 1, [128, NKB * 128], BF16)
        vring = Ring(P, "mv", 1, [128, NKB, 128], BF16)
        for h in range(8):
            k_s = kring.next()
            P.dma(k_s, k_s[:], mkT, mkT.ap[h, :, :])
            v_s = vring.next()
            vsrc = mv.ap[h, :, :].rearrange("(b p) d -> p b d", p=128)
            P.dma_group([(v_s[:, j * 22:(j + 1) * 22, :], vsrc[:, j * 22:(j + 1) * 22, :]) for j in range(3)], v_s, mv)
            hp = h % 2
            groups = [(0, 256, [64, 65])] + [(256 + 512 * j, 512, list(range(NKB))) for j in range(4)]
            for (q0, nq, kbs) in groups:
                steps = []
                for kb in kbs:
                    mms = [(k_s[:, kb * 128:(kb + 1) * 128], mq_s[:, h, q0:q0 + nq], [k_s, mq_s]),
                           (kr_s[hp * 64:(hp + 1) * 64, kb * 128:(kb + 1) * 128], mqr_s[hp * 64:(hp + 1) * 64, h // 2, q0:q0 + nq], [kr_s, mqr_s])]
                    steps.append((mms, v_s[:, kb, :], v_s))
                attend(nq, steps, None, None, o_ymlaT, o_ymlaT.ap[h, :, q0:q0 + nq])

    P.final_wait(outs)
    return P.finalize()

import math


def row_bcast(P, col_ap, colT, identf, ones8, dst, pst, scr):
    gtT, bd = scr
    P.op("tensor", lambda e: e.transpose(out=pst[0:8, 0:128], in_=col_ap, identity=identf[:]), [colT, identf], [pst])
    P.op("vector", lambda e: e.tensor_copy(out=gtT[0:8, :], in_=pst[0:8, 0:128]), [pst], [gtT])
    P.op("vector", lambda e: e.tensor_tensor(out=bd[0:8, :, :], in0=gtT[0:8, :].unsqueeze(1).to_broadcast([8, 8, 128]),
                                             in1=identf[0:8, 0:8].unsqueeze(2).to_broadcast([8, 8, 128]), op=ALU.mult), [gtT, identf], [bd])
    for hf in range(2):
        P.op("tensor", lambda e, hf=hf: e.matmul(out=pst[:, 0:512], lhsT=ones8[0:8, :], rhs=bd[0:8, hf * 4:(hf + 1) * 4, :], start=True, stop=True),
             [ones8, bd], [pst])
        P.op("vector", lambda e, hf=hf: e.tensor_copy(out=dst[:, hf * 512:(hf + 1) * 512], in_=pst[:, 0:512]), [pst], [dst])


def build_LC():
    P = Prog()
    din = lambda n, s, dt=F32: P.dram(n, s, dt, kind="ExternalInput")
    dout = lambda n, s, dt=F32: P.dram(n, s, dt, kind="ExternalOutput")
    yssm = din("yssm", [NR, 1024])
    cum = din("cum", [NR, 32])
    cT = din("cT", [2, 128, NR], BF16)
    zs = din("zs", [NR, 1024])
    gates = din("gates", [NR, 3072])
    yswaT = din("yswaT", [8, 128, NR], BF16)
    ymlaT = din("ymlaT", [8, 128, NR], BF16)
    xrows = din("xrows", [NR, 1024])
    mod = din("mod", [128, 48, 2])
    Fslot = din("Fslot", [2, 4, 128, 1024])
    sAslot = din("sAslot", [128, 2, 4, 16])
    w_p = [din(f"w_p{b}", [1024, 1024]) for b in range(3)]
    w_o = din("w_o", [1024, 1024])
    sng = din("sng", [128, 1024])
    identf_d = din("identf", [128, 128])
    o_x1 = dout("o_x1", [NR, 1024])

    identf = P.sbuf("identf", [128, 128], F32)
    P.dma(identf, identf[:], identf_d, identf_d[:, :])
    identb = P.sbuf("identb", [128, 128], BF16)
    P.op("vector", lambda e: e.tensor_copy(out=identb[:], in_=identf[:]), [identf], [identb])
    ones8 = P.sbuf("ones8", [8, 128], F32)
    P.op("vector", lambda e: e.memset(ones8[:], 1.0), [], [ones8])
    cconst = P.sbuf("cconst", [128, 8], F32)
    P.c_epsT = cconst
    P.c_eps = cconst
    P.op("vector", lambda e: e.memset(cconst[:, 0:1], EPS), [], [cconst])
    P.op("vector", lambda e: e.memset(cconst[:, 1:2], 0.0), [], [cconst])
    P.cbias = lambda v: cconst[:, 1:2]
    sng_s = P.sbuf("sng_s", [128, 1024], F32)
    P.dma(sng_s, sng_s[:], sng, sng[:, :])
    P.gainsT = sng_s
    mod_s = P.sbuf("mod_s", [128, 48, 2], F32)
    P.dma(mod_s, mod_s[:], mod, mod[:, :, :])
    cT_s = P.sbuf("cT_s", [128, 2, NR], BF16)
    P.dma_group([(cT_s[:, g, :], cT.ap[g, :, :]) for g in range(2)], cT_s, cT)

    pbig = Ring(P, "pbig", 2, [128, 1024], F32, psum=True)
    pacc = Ring(P, "pacc", 2, [128, 512], F32, psum=True)
    ptb = Ring(P, "ptb", 1, [128, 8, 128], BF16, psum=True)
    pst = P.psum("pst", [128, 512], F32)

    gtT = P.sbuf("gtT", [8, 128], F32)
    bd = P.sbuf("bd", [8, 8, 128], F32)
    gt1b = [P.sbuf(f"gt1b{c}", [128, 1024], F32) for c in range(2)]
    for c in range(2):
        row_bcast(P, mod_s[:, 16:24, c], mod_s, identf, ones8, gt1b[c], pst, (gtT, bd))

    Hib = [P.sbuf(f"Hib{d}", [128, 1024], BF16) for d in range(2)]
    Hacc = P.sbuf("Hacc", [128, 1024], F32)
    yring = Ring(P, "yt", 2, [128, 1024], F32)
    Fring = yring
    sA_s = P.sbuf("sA_s", [128, 2, 4, 16], F32)
    P.dma(sA_s, sA_s[:], sAslot, sAslot[:, :, :, :])
    P.op("scalar", lambda e: e.activation(out=sA_s[:], in_=sA_s[:], func=AF.Exp), [sA_s], [sA_s])
    for d in range(2):
        for s in range(4):
            Ft = Fring.next()
            P.dma(Ft, Ft[:], Fslot, Fslot.ap[d, s, :, :])
            if s == 0:
                P.op("vector", lambda e, Ft=Ft: e.tensor_copy(out=Hacc[:], in_=Ft[:]), [Ft], [Hacc])
            else:
                P.op("vector", lambda e, d=d, s=s: e.tensor_tensor(out=Hacc[:].rearrange("p (h w) -> p h w", h=16), in0=Hacc[:].rearrange("p (h w) -> p h w", h=16),
                                                                   in1=sA_s[:, d, s, :].unsqueeze(2).to_broadcast([128, 16, 64]), op=ALU.mult), [Hacc, sA_s], [Hacc])
                P.op("vector", lambda e, Ft=Ft: e.tensor_tensor(out=Hacc[:], in0=Hacc[:], in1=Ft[:], op=ALU.add), [Hacc, Ft], [Hacc])
        P.op("scalar", lambda e, d=d: e.activation(out=Hib[d][:], in_=Hacc[:], func=AF.Identity), [Hacc], [Hib[d]])

    srcT = P.sbuf("srcT", [128, 8, NR], BF16)
    srcv = [P.view(f"srcv{i}", None) for i in range(NT)]
    zring = Ring(P, "zt", 2, [128, 1024], F32)
    cring = Ring(P, "ct", 2, [128, 32], F32)
    sq = P.sbuf("sq", [128, 1024], F32)
    ss = P.sbuf("ss", [128, 8], F32)
    tmpf = P.sbuf("tmpf", [128, 1024], F32)
    ub = Ring(P, "ub", 2, [128, 1024], BF16)

    def to_srcT(i, srcb, srcbT):
        for hf in range(2):
            pt = ptb.next()
            for j in range(4):
                kt = hf * 4 + j
                P.op("tensor", lambda e, kt=kt, j=j, pt=pt: e.transpose(out=pt[:, j, :], in_=srcb[:, kt * 128:(kt + 1) * 128], identity=identb[:]),
                     [srcbT, identb], [pt])
            P.op("vector", lambda e, hf=hf, pt=pt: e.tensor_copy(out=srcT[:, hf * 4:(hf + 1) * 4, i * 128:(i + 1) * 128], in_=pt[:, 0:4, :]), [pt], [srcv[i]])

    def _tile1(i):
        yt = yring.next(); P.dma(yt, yt[:], yssm, yssm[i * 128:(i + 1) * 128, :])
        zt = zring.next(); P.dma(zt, zt[:], zs, zs[i * 128:(i + 1) * 128, :])
        if i >= 2:
            ct = cring.next(); P.dma(ct, ct[:], cum, cum[i * 128:(i + 1) * 128, :])
            P.op("scalar", lambda e, ct=ct: e.activation(out=ct[:], in_=ct[:], func=AF.Exp), [ct], [ct])
            for d in range(2):
                pc = pbig.next()
                for g in range(2):
                    P.op("tensor", lambda e, g=g, d=d, pc=pc: e.matmul(out=pc[:, g * 512:(g + 1) * 512], lhsT=cT_s[:, g, i * 128:(i + 1) * 128],
                                                                       rhs=Hib[d][:, g * 512:(g + 1) * 512], start=True, stop=True), [cT_s, Hib[d]], [pc])
                P.op("vector", lambda e, d=d, pc=pc, ct=ct: e.tensor_tensor(out=tmpf[:].rearrange("p (h w) -> p h w", h=16), in0=pc[:].rearrange("p (h w) -> p h w", h=16),
                                                                            in1=ct[:, d * 16:(d + 1) * 16].unsqueeze(2).to_broadcast([128, 16, 64]), op=ALU.mult),
                     [pc, ct], [tmpf])
                P.op("gpsimd", lambda e, yt=yt: e.tensor_tensor(out=yt[:], in0=yt[:], in1=tmpf[:], op=ALU.add), [yt, tmpf], [yt])
        P.op("gpsimd", lambda e, yt=yt, zt=zt: e.tensor_tensor(out=yt[:], in0=yt[:], in1=zt[:], op=ALU.mult), [yt, zt], [yt])
        u16 = ub.next()
        gnorm(P, yt[:].rearrange("p (g w) -> p g w", g=1), yt, 1, 1024, sng_s[:], u16[:].rearrange("p (g w) -> p g w", g=1), u16, sq, ss)
        to_srcT(i, u16, u16)


    for i in range(NT):
        _tile1(i)
    merged = P.sbuf("merged", [128, NT, 1024], F32)
    mv_ = [P.view(f"mv{i}", None) for i in range(NT)]
    wring = Ring(P, "wp", 1, [128, 8, 1024], BF16)
    gring = Ring(P, "gt", 3, [128, 512], F32)
    tmp2 = Ring(P, "tmp2", 2, [128, 512], F32)

    def load_w(w):
        wb = wring.next()
        P.dma_group([(wb[:, :, hf * 512:(hf + 1) * 512], w.ap.rearrange("(kt p) n -> p kt n", p=128)[:, :, hf * 512:(hf + 1) * 512]) for hf in range(2)],
                    wb, w, q="gpsimd")
        return wb

    for b in range(3):
        wb = load_w(w_p[b])
        if b > 0:
            src_d = yswaT if b == 1 else ymlaT
            for i in range(NT):
                P.dma(srcv[i], srcT[:, :, i * 128:(i + 1) * 128], src_d, src_d.ap[:, :, i * 128:(i + 1) * 128].rearrange("h d t -> d h t"), semT=srcv[i])
        def _tile2(i, b=b, wb=wb):
            for hf in range(2):
                ps = pacc.next()
                for kt in range(8):
                    P.op("tensor", lambda e, kt=kt, ps=ps, hf=hf, wb=wb: e.matmul(out=ps[:], lhsT=srcT[:, kt, i * 128:(i + 1) * 128], rhs=wb[:, kt, hf * 512:(hf + 1) * 512],
                                                                                 start=(kt == 0), stop=(kt == 7)), [srcv[i], wb], [ps])
                gt = gring.next()
                P.dma(gt, gt[:], gates, gates[i * 128:(i + 1) * 128, b * 1024 + hf * 512:b * 1024 + (hf + 1) * 512])
                if b == 0:
                    P.op("vector", lambda e, ps=ps, gt=gt, hf=hf: e.tensor_tensor(out=merged[:, i, hf * 512:(hf + 1) * 512], in0=ps[:], in1=gt[:], op=ALU.mult),
                         [ps, gt], [mv_[i]])
                else:
                    t2 = tmp2.next()
                    P.op("vector", lambda e, ps=ps, gt=gt, t2=t2: e.tensor_tensor(out=t2[:], in0=ps[:], in1=gt[:], op=ALU.mult), [ps, gt], [t2])
                    P.op("gpsimd", lambda e, t2=t2, hf=hf: e.tensor_tensor(out=merged[:, i, hf * 512:(hf + 1) * 512], in0=merged[:, i, hf * 512:(hf + 1) * 512], in1=t2[:], op=ALU.add),
                         [mv_[i], t2], [mv_[i]])
        for i in range(NT):
            _tile2(i)

    wb = load_w(w_o)
    xring = yring
    oring = zring
    def _tile3(i):
        m16 = ub.next()
        P.op("scalar", lambda e, m16=m16: e.activation(out=m16[:], in_=merged[:, i, :], func=AF.Identity), [mv_[i]], [m16])
        to_srcT(i, m16, m16)
        xt = xring.next(); P.dma(xt, xt[:], xrows, xrows[i * 128:(i + 1) * 128, :])
        xo = oring.next()
        col = 1 if i < 2 else 0
        for hf in range(2):
            ps = pacc.next()
            for kt in range(8):
                P.op("tensor", lambda e, kt=kt, ps=ps, hf=hf: e.matmul(out=ps[:], lhsT=srcT[:, kt, i * 128:(i + 1) * 128], rhs=wb[:, kt, hf * 512:(hf + 1) * 512],
                                                                      start=(kt == 0), stop=(kt == 7)), [srcv[i], wb], [ps])
            t2 = tmp2.next()
            P.op("vector", lambda e, ps=ps, t2=t2, hf=hf: e.tensor_tensor(out=t2[:], in0=ps[:], in1=gt1b[col][:, hf * 512:(hf + 1) * 512], op=ALU.mult), [ps, gt1b[col]], [t2])
            P.op("gpsimd", lambda e, t2=t2, hf=hf, xo=xo, xt=xt: e.tensor_tensor(out=xo[:, hf * 512:(hf + 1) * 512], in0=t2[:], in1=xt[:, hf * 512:(hf + 1) * 512], op=ALU.add),
                 [t2, xt], [xo])
        P.dma(o_x1, o_x1[i * 128:(i + 1) * 128, :], xo, xo[:])
    for i in range(NT):
        _tile3(i)
    P.final_wait([o_x1])
    return P.finalize()


def build_LF():
    P = Prog()
    din = lambda n, s, dt=F32: P.dram(n, s, dt, kind="ExternalInput")
    dout = lambda n, s, dt=F32: P.dram(n, s, dt, kind="ExternalOutput")
    x1 = din("x1", [NR, 1024])
    mod = din("mod", [128, 48, 2])
    g2T = din("g2T", [128, 8])
    w_fi = din("w_fi", [1024, 5632])
    w_fo = din("w_fo", [2816, 1024])
    identf_d = din("identf", [128, 128])
    o_x2 = dout("o_x2", [NR, 1024])

    identf = P.sbuf("identf", [128, 128], F32)
    P.dma(identf, identf[:], identf_d, identf_d[:, :])
    ones8 = P.sbuf("ones8", [8, 128], F32)
    P.op("vector", lambda e: e.memset(ones8[:], 1.0), [], [ones8])
    cconst = P.sbuf("cconst", [128, 8], F32)
    P.op("vector", lambda e: e.memset(cconst[:, 0:1], EPS), [], [cconst])
    P.op("vector", lambda e: e.memset(cconst[:, 1:2], 0.0), [], [cconst])
    mod_s = P.sbuf("mod_s", [128, 48, 2], F32)
    P.dma(mod_s, mod_s[:], mod, mod[:, :, :])
    g2_s = P.sbuf("g2_s", [128, 8], F32)
    P.dma(g2_s, g2_s[:], g2T, g2T[:, :])
    A2 = P.sbuf("A2", [128, 8, 2], F32)
    P.op("vector", lambda e: e.tensor_scalar(out=A2[:], in0=mod_s[:, 32:40, :], scalar1=1.0, scalar2=None, op0=ALU.add), [mod_s], [A2])
    P.op("vector", lambda e: e.tensor_tensor(out=A2[:], in0=A2[:], in1=g2_s[:].unsqueeze(2).to_broadcast([128, 8, 2]), op=ALU.mult), [A2, g2_s], [A2])

    ptr = Ring(P, "ptr", 2, [128, 4, 128], F32, psum=True)
    pg = Ring(P, "pg", 2, [128, 512], F32, psum=True)
    pu = Ring(P, "pu", 2, [128, 512], F32, psum=True)
    pout = Ring(P, "pout", 2, [128, 512], F32, psum=True)

    gtT = P.sbuf("gtT", [8, 128], F32)
    bd = P.sbuf("bd", [8, 8, 128], F32)
    gt2b = [P.sbuf(f"gt2b{c}", [128, 1024], F32) for c in range(2)]
    pst = pout.b[0]
    for c in range(2):
        row_bcast(P, mod_s[:, 40:48, c], mod_s, identf, ones8, gt2b[c], pst, (gtT, bd))

    wout = P.sbuf("wout", [128, 22, 1024], BF16)
    wo_v = w_fo.ap.rearrange("(c p) n -> p c n", p=128)
    woutv = [P.view(f"woutv{j}", None) for j in range(11)]
    for j in range(11):
        P.dma(woutv[j], wout[:, 2 * j:2 * j + 2, :], w_fo, wo_v[:, 2 * j:2 * j + 2, :], q="gpsimd", semT=woutv[j])

    HT = 9 * 128
    h2T = P.sbuf("h2T", [128, 8, HT], BF16)
    hidT = P.sbuf("hidT", [128, 22, HT], BF16)
    hidv = [[P.view(f"hid{c}_{g}", None) for g in range(3)] for c in range(22)]
    xring = Ring(P, "x1t", 3, [128, 1024], F32)
    sq = P.sbuf("sq", [128, 1024], F32)
    ss = P.sbuf("ss", [128, 8], F32)
    xn = P.sbuf("xn", [128, 1024], F32)
    wgr = Ring(P, "wg", 2, [128, 8, 256], BF16)
    wur = Ring(P, "wu", 2, [128, 8, 256], BF16)
    sgr = Ring(P, "sg", 2, [128, 512], F32)
    t2r = Ring(P, "t2", 2, [128, 512], F32)
    xor_ = Ring(P, "xo", 2, [128, 1024], F32)
    wfi_v = w_fi.ap.rearrange("(kt p) n -> p kt n", p=128)
    groups = [(0, 512), (512, 512), (1024, 128)]

    for half in range(2):
        h2v = [P.view(f"h2v{half}_{k}", None) for k in range(9)]
        for k in range(9):
            i = half * 9 + k
            col = 1 if i < 2 else 0
            xt = xring.next(); P.dma(xt, xt[:], x1, x1[i * 128:(i + 1) * 128, :])
            P.op("scalar", lambda e, xt=xt: e.activation(out=sq[:], in_=xt[:], func=AF.Square, accum_out=ss[:, 0:1]), [xt], [sq, ss])
            P.op("scalar", lambda e: e.activation(out=ss[:, 0:1], in_=ss[:, 0:1], func=AF.Ln, scale=1.0 / 1024, bias=cconst[:, 0:1]), [ss, cconst], [ss])
            P.op("scalar", lambda e: e.activation(out=ss[:, 0:1], in_=ss[:, 0:1], func=AF.Exp, scale=-0.5, bias=cconst[:, 1:2]), [ss, cconst], [ss])
            P.op("vector", lambda e, xt=xt: e.tensor_scalar(out=xn[:], in0=xt[:], scalar1=ss[:, 0:1], scalar2=None, op0=ALU.mult), [xt, ss], [xn])
            for hf in range(2):
                pt = ptr.next()
                for j in range(4):
                    kt = hf * 4 + j
                    P.op("tensor", lambda e, kt=kt, j=j, pt=pt: e.transpose(out=pt[:, j, :], in_=xn[:, kt * 128:(kt + 1) * 128], identity=identf[:]), [xn, identf], [pt])
                for j in range(4):
                    kt = hf * 4 + j
                    P.op("scalar", lambda e, kt=kt, j=j, pt=pt, k=k, col=col: e.activation(out=h2T[:, kt, k * 128:(k + 1) * 128], in_=pt[:, j, :], func=AF.Identity,
                                                                                           scale=A2[:, kt, col:col + 1], bias=mod_s[:, 24 + kt, col:col + 1]),
                         [pt, A2, mod_s], [h2v[k]])
        for cb in range(11):
            wg = wgr.next(); P.dma(wg, wg[:], w_fi, wfi_v[:, :, cb * 256:(cb + 1) * 256], q="gpsimd")
            wu = wur.next(); P.dma(wu, wu[:], w_fi, wfi_v[:, :, 2816 + cb * 256:2816 + (cb + 1) * 256], q="gpsimd")
            for gi, (t0, n) in enumerate(groups):
                rd = [h2v[k] for k in range(t0 // 128, (t0 + n) // 128)]
                for ct in range(2):
                    c = cb * 2 + ct
                    a = pg.next(); u = pu.next()
                    for kt in range(8):
                        P.op("tensor", lambda e, kt=kt, a=a, ct=ct, wg=wg, t0=t0, n=n: e.matmul(out=a[:, 0:n], lhsT=wg[:, kt, ct * 128:(ct + 1) * 128], rhs=h2T[:, kt, t0:t0 + n],
                                                                                                  start=(kt == 0), stop=(kt == 7)), rd + [wg], [a])
                    for kt in range(8):
                        P.op("tensor", lambda e, kt=kt, u=u, ct=ct, wu=wu, t0=t0, n=n: e.matmul(out=u[:, 0:n], lhsT=wu[:, kt, ct * 128:(ct + 1) * 128], rhs=h2T[:, kt, t0:t0 + n],
                                                                                                  start=(kt == 0), stop=(kt == 7)), rd + [wu], [u])
                    sg = sgr.next()
                    P.op("scalar", lambda e, a=a, sg=sg, n=n: e.activation(out=sg[:, 0:n], in_=a[:, 0:n], func=AF.Silu), [a], [sg])
                    P.op("vector", lambda e, u=u, sg=sg, c=c, t0=t0, n=n: e.tensor_tensor(out=hidT[:, c, t0:t0 + n], in0=u[:, 0:n], in1=sg[:, 0:n], op=ALU.mult),
                         [u, sg], [hidv[c][gi]])
        for k in range(9):
            i = half * 9 + k
            col = 1 if i < 2 else 0
            gi = 0 if k < 4 else (1 if k < 8 else 2)
            xt = xring.next(); P.dma(xt, xt[:], x1, x1[i * 128:(i + 1) * 128, :])
            xo = xor_.next()
            for hf in range(2):
                ps = pout.next()
                for c in range(22):
                    P.op("tensor", lambda e, c=c, ps=ps, hf=hf, k=k: e.matmul(out=ps[:], lhsT=hidT[:, c, k * 128:(k + 1) * 128], rhs=wout[:, c, hf * 512:(hf + 1) * 512],
                                                                             start=(c == 0), stop=(c == 21)), [hidv[c][gi], woutv[c // 2]], [ps])
                t2 = t2r.next()
                P.op("vector", lambda e, ps=ps, t2=t2, hf=hf, col=col: e.tensor_tensor(out=t2[:], in0=ps[:], in1=gt2b[col][:, hf * 512:(hf + 1) * 512], op=ALU.mult),
                     [ps, gt2b[col]], [t2])
                P.op("gpsimd", lambda e, t2=t2, hf=hf, xo=xo, xt=xt: e.tensor_tensor(out=xo[:, hf * 512:(hf + 1) * 512], in0=t2[:], in1=xt[:, hf * 512:(hf + 1) * 512], op=ALU.add),
                     [t2, xt], [xo])
            P.dma(o_x2, o_x2[i * 128:(i + 1) * 128, :], xo, xo[:])
    P.final_wait([o_x2])
    return P.finalize()

import numpy as np

D = 1024
SEQ = 8192
CTX = 256
GRID_W = 64
NT = 18
NR = NT * 128
HC = 2312
TPC = 2048


def rope_np(rows, rot_dim):
    n_freq = rot_dim // 4
    inv = np.power(np.float32(10000.0), -np.arange(n_freq, dtype=np.float32) / np.float32(n_freq)).astype(np.float32)
    r, col = np.meshgrid(np.arange(rows, dtype=np.float32), np.arange(GRID_W, dtype=np.float32), indexing='ij')
    ang = np.stack([r.reshape(-1)[:, None] * inv, col.reshape(-1)[:, None] * inv], axis=1).astype(np.float32)
    L = ang.shape[0]
    return np.cos(ang).reshape(L, -1).astype(np.float32), np.sin(ang).reshape(L, -1).astype(np.float32)


_ROPE = None


def rope_tab_core(q):
    global _ROPE
    if _ROPE is None:
        cs, ss = rope_np(SEQ // GRID_W, 128)
        cm, sm = rope_np(SEQ // GRID_W, 64)
        _ROPE = np.concatenate([cs, ss, cm, sm], axis=1)
    tab = np.zeros((NR, 192), np.float32)
    tab[:256, 0:64] = 1.0
    tab[:256, 128:160] = 1.0
    tab[256:] = _ROPE[q * TPC:(q + 1) * TPC]
    return tab


def colT(v, nt):
    return np.ascontiguousarray(v.reshape(nt, 128).T)


def bcast(v, p=128):
    return np.ascontiguousarray(np.broadcast_to(v.reshape(1, -1), (p, v.size)))


def prep_LA(inp, li, x_l, x_c):
    maps = []
    ident = np.eye(128, dtype=np.float32)
    gains = np.concatenate([inp['swa_q_norm_g'][li], inp['swa_k_norm_g'][li], inp['mla_q_lat_g'][li], inp['mla_kv_lat_g'][li],
                            inp['mla_q_norm_g'][li], inp['mla_k_norm_g'][li], np.zeros(128, np.float32)]).astype(np.float32)
    cw = inp['ssm_conv_w'][li]
    cwb = np.ascontiguousarray(np.broadcast_to(cw[None], (128, 5, 1536)))
    cb = inp['ssm_conv_b'][li]
    shared = dict(w_mod=inp['w_mod'][li], bmodT=colT(inp['b_mod'][li], 48), g1T=colT(inp['norm1_g'][li], 8),
                  w_in=inp['w_in'][li], cwb=cwb, cbrow=cb.reshape(1, 1536), cbT=colT(cb[1024:], 4),
                  dtb=bcast(inp['ssm_dt_bias'][li].reshape(-1)), gains=bcast(gains), w_uq=inp['w_mla_uq'][li], w_ukv=inp['w_mla_ukv'][li],
                  identf=ident)
    for r in range(8):
        b, q = r // 4, r % 4
        xin = np.zeros((HC, D), np.float32)
        xin[2:258] = x_c[b]
        t0 = q * TPC
        xin[262:262 + TPC] = x_l[b, t0:t0 + TPC]
        hm = np.zeros((128, 4), np.float32)
        if q > 0:
            xin[260:262] = x_l[b, t0 - 2:t0]
            hm[:, 0:2] = 1.0
        if q < 3:
            xin[2310:2312] = x_l[b, t0 + TPC:t0 + TPC + 2]
            hm[:, 2:4] = 1.0
        cT = np.stack([inp['c'][b].reshape(8, 128).T, inp['c_ctx'].reshape(8, 128).T], axis=-1).astype(np.float32)
        m = dict(shared)
        m.update(xin=xin, cT=np.ascontiguousarray(cT), hmask=hm, ropetab=rope_tab_core(q))
        maps.append(m)
    return maps


NEG = -30000.0


def ssd_consts():
    t = np.arange(128)
    U = (t[:, None] <= t[None, :]).astype(np.float32)
    L = (t[:, None] >= t[None, :]).astype(np.float32)
    cst = np.stack([U, L, np.ones((128, 128), np.float32), np.eye(128, dtype=np.float32), np.zeros((128, 128), np.float32)], axis=1)
    nm = np.stack([np.where(U > 0, 0.0, NEG), np.where(L > 0, 0.0, NEG)], axis=1).astype(np.float32)
    return np.ascontiguousarray(cst), np.ascontiguousarray(nm)


def prep_LS(inp, li, outA):
    cst, nm = ssd_consts()
    maps = []
    for r in range(8):
        o = outA[r]
        maps.append(dict(xs=o['o_xs'], btm=o['o_btm'], bT=o['o_bT'], cT=o['o_cT'], dt=o['o_dt'],
                         alog=bcast(inp['ssm_a_log'][li].reshape(-1)), dskip=bcast(inp['ssm_d'][li]), cst=cst, nmask=nm))
    return maps


def swa_masks(q):
    import ml_dtypes
    k = np.arange(128)[:, None]; qq = np.arange(128)[None, :]
    mP = np.where(k >= qq, 0.0, NEG).astype(np.float32)
    mN = np.where(k <= qq, 0.0, NEG).astype(np.float32)
    allneg = np.full((128, 128), NEG, np.float32)
    ms = [mP, mN, allneg if q == 0 else mP, allneg if q == 3 else mN]
    m = np.stack([np.tile(x, (1, 4)) for x in ms], axis=1)
    return np.ascontiguousarray(m).astype(ml_dtypes.bfloat16)


def prep_LT(inp, li, outA):
    maps = []
    ident = np.eye(128, dtype=np.float32)
    for r in range(8):
        b, q = r // 4, r % 4
        o = outA[r]
        grp = [outA[b * 4 + j] for j in range(4)]
        skT = np.asarray(o['o_skT'])
        sv = np.asarray(o['o_sv'])
        zk = np.zeros((2, 128, 128), skT.dtype); zv = np.zeros((128, 256), sv.dtype)
        pk = np.asarray(grp[q - 1]['o_skT'])[:, :, NR - 128:] if q > 0 else zk
        nk = np.asarray(grp[q + 1]['o_skT'])[:, :, 256:384] if q < 3 else zk
        pv = np.asarray(grp[q - 1]['o_sv'])[NR - 128:] if q > 0 else zv
        nv = np.asarray(grp[q + 1]['o_sv'])[256:384] if q < 3 else zv
        skT_e = np.concatenate([skT[:, :, :256], pk, skT[:, :, 256:], nk], axis=2)
        sv_e = np.concatenate([sv[:256], pv, sv[256:], nv], axis=0)
        mkT = np.concatenate([np.asarray(g_['o_mkT'])[:, :, 256:] for g_ in grp] + [np.asarray(o['o_mkT'])[:, :, :256]], axis=2)
        mkrT = np.concatenate([np.asarray(g_['o_mkrT'])[:, 256:] for g_ in grp] + [np.asarray(o['o_mkrT'])[:, :256]], axis=1)
        mvv = np.concatenate([np.asarray(g_['o_mv'])[256:] for g_ in grp] + [np.asarray(o['o_mv'])[:256]], axis=0)
        mvh = np.ascontiguousarray(mvv.reshape(-1, 8, 128).transpose(1, 0, 2))
        maps.append(dict(sqT=o['o_sqT'], skT=np.ascontiguousarray(skT_e), sv=np.ascontiguousarray(sv_e), smask=swa_masks(q),
                         sink=bcast(inp['swa_sink'][li]), mqT=o['o_mqT'], mqrT=o['o_mqrT'], mkT=np.ascontiguousarray(mkT),
                         mkrT=np.ascontiguousarray(mkrT), mv=mvh, identf=ident))
    return maps


def fake_outA(ref, r):
    import ml_dtypes
    bf = ml_dtypes.bfloat16
    b, q = r // 4, r % 4
    sl = slice(q * TPC, (q + 1) * TPC)
    cat = lambda c, l: np.concatenate([ref[c][b], ref[l][b, sl]], axis=0)
    conv = cat('c_conv', 'l_conv')
    B = conv[:, 1024:1280]; C = conv[:, 1280:1536]
    T = lambda a: np.ascontiguousarray(a.transpose(1, 2, 0))
    mk = cat('c_mk', 'l_mk'); mq = cat('c_mq', 'l_mq') * np.float32(192 ** -0.5)
    kr = mk[:, 0, 128:]
    return dict(o_xs=conv[:, :1024].copy(), o_btm=B.astype(bf), o_bT=T(B.reshape(-1, 2, 128)).astype(bf), o_cT=T(C.reshape(-1, 2, 128)).astype(bf),
                o_dt=cat('c_dt', 'l_dt'), o_zs=cat('c_zs', 'l_zs'), o_gates=cat('c_gates', 'l_gates'),
                o_skT=T(cat('c_sk', 'l_sk')).astype(bf), o_sv=cat('c_sv', 'l_sv').reshape(-1, 256).astype(bf),
                o_sqT=T(cat('c_sq', 'l_sq') * np.float32(128 ** -0.5)).astype(bf),
                o_mkT=T(mk[:, :, :128]).astype(bf), o_mkrT=np.ascontiguousarray(np.concatenate([kr, kr], 1).T).astype(bf),
                o_mv=cat('c_mv', 'l_mv').reshape(-1, 1024).astype(bf), o_mqT=T(mq[:, :, :128]).astype(bf),
                o_mqrT=np.ascontiguousarray(mq[:, :, 128:].reshape(-1, 4, 128).transpose(1, 2, 0)).astype(bf))


def prep_LC(inp, li, outA, outS, outT, x_l, x_c):
    maps = []
    ident = np.eye(128, dtype=np.float32)
    shared = dict(w_p0=inp['w_p_ssm'][li], w_p1=inp['w_p_swa'][li], w_p2=inp['w_p_mla'][li], w_o=inp['w_out'][li],
                  sng=bcast(inp['ssm_norm_g'][li]), identf=ident)
    for r in range(8):
        b, q = r // 4, r % 4
        F = np.zeros((2, 4, 128, 1024), np.float32)
        sA = np.zeros((128, 2, 4, 16), np.float32)
        F[0, 0] = np.asarray(outS[r]['o_F'])[0]
        F[1, 0] = np.asarray(outS[r]['o_F'])[1]
        for s in range(1, 4):
            j = s - 1
            if j < q:
                F[0, s] = np.asarray(outS[b * 4 + j]['o_F'])[2]
                sA[:, 0, s] = np.asarray(outS[b * 4 + j]['o_sumA'])[:, 2, :]
            j = 4 - s
            if j > q:
                F[1, s] = np.asarray(outS[b * 4 + j]['o_F'])[3]
                sA[:, 1, s] = np.asarray(outS[b * 4 + j]['o_sumA'])[:, 3, :]
        xrows = np.concatenate([x_c[b], x_l[b, q * TPC:(q + 1) * TPC]], axis=0).astype(np.float32)
        m = dict(shared)
        m.update(yssm=outS[r]['o_y'], cum=outS[r]['o_cum'], cT=outA[r]['o_cT'], zs=outA[r]['o_zs'], gates=outA[r]['o_gates'],
                 yswaT=outT[r]['o_yswaT'], ymlaT=outT[r]['o_ymlaT'], xrows=np.ascontiguousarray(xrows), mod=outA[r]['o_mod'],
                 Fslot=F, sAslot=sA)
        maps.append(m)
    return maps


def prep_LF(inp, li, outA, x1s):
    ident = np.eye(128, dtype=np.float32)
    shared = dict(g2T=colT(inp['norm2_g'][li], 8), w_fi=inp['w_ffn_in'][li], w_fo=inp['w_ffn_out'][li], identf=ident)
    maps = []
    for r in range(8):
        m = dict(shared)
        m.update(x1=x1s[r], mod=outA[r]['o_mod'])
        maps.append(m)
    return maps


_PROGS = {}


def _prog(name):
    if name not in _PROGS:
        _PROGS[name] = {"A": build_LA, "S": build_LS, "T": build_LT, "C": build_LC, "F": build_LF}[name]()
    return _PROGS[name]


def _run(name, maps):
    nc = {"A": build_LA, "S": build_LS, "T": build_LT, "C": build_LC, "F": build_LF}[name]()
    res = run_bass_kernel_spmd(nc, maps, core_ids=list(range(8)))
    return [{k: np.asarray(v) for k, v in res.results[r].items()} for r in range(8)]


def kernel(**inputs):
    inp = {k: np.asarray(v) for k, v in inputs.items()}
    x_l = inp['x'].astype(np.float32)
    x_c = inp['ctx'].astype(np.float32)
    for li in range(2):
        outA = _run("A", prep_LA(inp, li, x_l, x_c))
        outS = _run("S", prep_LS(inp, li, outA))
        outT = _run("T", prep_LT(inp, li, outA))
        outC = _run("C", prep_LC(inp, li, outA, outS, outT, x_l, x_c))
        outF = _run("F", prep_LF(inp, li, outA, {r: outC[r]['o_x1'] for r in range(8)}))
        x_l = np.stack([np.concatenate([outF[b * 4 + q]['o_x2'][256:] for q in range(4)], axis=0) for b in range(2)]).astype(np.float32)
        x_c = np.stack([outF[b * 4]['o_x2'][:256] for b in range(2)]).astype(np.float32)
    return x_l
```
